# Optimizing a Trainium2 kernel written in Bass

```python
import jax
import jax.numpy as jnp
from jax import lax
import numpy as np

D_MODEL = 1024
BATCH = 8
SEQ = 8192
DEPTH = 2
DEC_BATCH = 8
DEC_SEQ = 16
PAST_LEN = 2048

CHUNK = 64
N_EVEN = (DEPTH + 1) // 2
N_ODD = DEPTH // 2
MIX_W = D_MODEL
GROUP_W = MIX_W // 2
D_FF = 2816
NORM_EPS = 1e-6

LRU_W = GROUP_W
LRU_BLOCKS = 8
LRU_BS = LRU_W // LRU_BLOCKS
CONV_W = 4
LRU_C = 8.0

FOX_HEADS = 8
FOX_HD = GROUP_W // FOX_HEADS
FOX_BLOCK = 128

HG_HEADS = 4
HG_DK = GROUP_W // HG_HEADS
HG_DV = HG_DK

RW_HEADS = 8
RW_HD = GROUP_W // RW_HEADS
RW_DECAY_LORA = 64
RW_A_LORA = 64
RW_GATE_LORA = 128
RW_LN_EPS = 64e-5
RW_COLS = 3 * GROUP_W + RW_DECAY_LORA + RW_A_LORA + RW_GATE_LORA

E_COLS = 2 * LRU_W + 4 * GROUP_W + FOX_HEADS
O_COLS = 4 * GROUP_W + RW_COLS

kernel_name = 'hybrid_streaming_encoder_step'


def _rmsnorm(x, g, eps=NORM_EPS):
    xf = x.astype(jnp.float32)
    y = xf * lax.rsqrt(jnp.mean(xf * xf, axis=-1, keepdims=True) + eps)
    return (y * g.astype(jnp.float32)).astype(x.dtype)


def _swiglu(x, w_in, w_out):
    gate, up = jnp.split(x @ w_in, 2, axis=-1)
    return (jax.nn.silu(gate) * up) @ w_out


def _causal_conv(x, buf, w, b):
    T = x.shape[1]
    xp = jnp.concatenate([buf.astype(x.dtype), x], axis=1)
    y = b + sum(xp[:, j:j + T] * w[j] for j in range(CONV_W))
    return y, xp[:, xp.shape[1] - (CONV_W - 1):]


def _block_diag(x, w):
    B, T, C = x.shape
    return jnp.einsum('btnc,ncd->btnd', x.reshape(B, T, LRU_BLOCKS, LRU_BS), w).reshape(B, T, C)


def _rg_lru(x, h0, wa, ba, wx, bx, lam):
    r = jax.nn.sigmoid(_block_diag(x, wa) + ba).astype(jnp.float32)
    i = jax.nn.sigmoid(_block_diag(x, wx) + bx)
    log_a = -LRU_C * jax.nn.softplus(-lam.astype(jnp.float32)) * r
    a = jnp.exp(log_a)
    b = jnp.sqrt(-jnp.expm1(2.0 * log_a)) * (i * x).astype(jnp.float32)
    b = b.at[:, 0].add(a[:, 0] * h0.astype(jnp.float32))

    def combine(left, right):
        a1, b1 = left
        a2, b2 = right
        return a1 * a2, a2 * b1 + b2

    _, h = lax.associative_scan(combine, (a, b), axis=1)
    return h.astype(x.dtype), h[:, -1]


def _fox_block(q, pos_q, F_q, k, v, F_k, pos_k):
    s = jnp.einsum('bqhd,bkhd->bhqk', q, k).astype(jnp.float32) * (FOX_HD ** -0.5)
    s = s + jnp.swapaxes(F_q, 1, 2)[..., :, None] - jnp.swapaxes(F_k, 1, 2)[..., None, :]
    s = jnp.where(pos_k[None, :] <= pos_q[:, None], s, -jnp.inf)
    p = jax.nn.softmax(s, axis=-1).astype(v.dtype)
    return jnp.einsum('bhqk,bkhd->bqhd', p, v)


def _fox_attention(q, k, v, log_f, k_past, v_past, lf_past):
    B, T, H, D = q.shape
    P = k_past.shape[1]
    k_all = jnp.concatenate([k_past.astype(k.dtype), k], axis=1)
    v_all = jnp.concatenate([v_past.astype(v.dtype), v], axis=1)
    F = jnp.cumsum(jnp.concatenate([lf_past.astype(jnp.float32), log_f], axis=1), axis=1)
    pos_k = jnp.arange(P + T)
    pos_q = P + jnp.arange(T)
    F_q = F[:, P:]
    if T > FOX_BLOCK and T % FOX_BLOCK == 0:
        nb = T // FOX_BLOCK
        q_b = q.reshape(B, nb, FOX_BLOCK, H, D).swapaxes(0, 1)
        F_b = F_q.reshape(B, nb, FOX_BLOCK, H).swapaxes(0, 1)
        p_b = pos_q.reshape(nb, FOX_BLOCK)
        o = lax.map(lambda blk: _fox_block(blk[0], blk[1], blk[2], k_all, v_all, F, pos_k), (q_b, p_b, F_b))
        return o.swapaxes(0, 1).reshape(B, T, H, D)
    return _fox_block(q, pos_q, F_q, k_all, v_all, F, pos_k)


def _hgrn2(q, k, v, log_f, S0):
    B, T, H, DK = q.shape
    DV = v.shape[-1]
    c = CHUNK if T % CHUNK == 0 else T
    n = T // c
    causal = jnp.tril(jnp.ones((c, c), dtype=bool))

    def to_chunks(t):
        return t.reshape(B, n, c, H, t.shape[-1]).swapaxes(0, 1)

    def chunk_step(S, inp):
        qc, kc, vc, lfc = inp
        G = jnp.cumsum(lfc, axis=1)
        qg = qc * jnp.exp(G)
        kg = kc * jnp.exp(-G)
        A = jnp.where(causal, jnp.einsum('bthk,bshk->bhts', qg, kg), 0.0)
        o = jnp.einsum('bthk,bhkv->bthv', qg, S) + jnp.einsum('bhts,bshv->bthv', A, vc)
        G_last = G[:, -1]
        k_dec = kc * jnp.exp(G_last[:, None] - G)
        S = jnp.exp(G_last)[..., None] * S + jnp.einsum('bshk,bshv->bhkv', k_dec, vc)
        return S, o

    S_last, o = lax.scan(chunk_step, S0, (to_chunks(q), to_chunks(k), to_chunks(v), to_chunks(log_f)))
    return o.swapaxes(0, 1).reshape(B, T, H, DV), S_last


def _rwkv7(z, prev, S0, o_idx, W):
    B, T, _ = z.shape
    G = GROUP_W
    f32 = jnp.float32
    shifted = jnp.concatenate([prev[:, None].astype(z.dtype), z[:, :-1]], axis=1)
    zm = z + (shifted - z) * W['rw_mu'][o_idx]
    r, k, v, wd, ad, gd = jnp.split(zm, [G, 2 * G, 3 * G, 3 * G + RW_DECAY_LORA, 3 * G + RW_DECAY_LORA + RW_A_LORA], axis=-1)
    w = -jax.nn.softplus(-(W['rw_w0'][o_idx] + jnp.tanh(wd) @ W['rw_w2'][o_idx]).astype(f32)) - 0.5
    decay = jnp.exp(-jnp.exp(w))
    a = jax.nn.sigmoid((W['rw_a0'][o_idx] + ad @ W['rw_a2'][o_idx]).astype(f32))
    g = jax.nn.sigmoid(gd) @ W['rw_g2'][o_idx]

    def heads(t):
        return t.astype(f32).reshape(B, T, RW_HEADS, RW_HD)

    r, k, v, decay, a = heads(r), heads(k), heads(v), heads(decay), heads(a)
    kk = k * W['rw_kk'][o_idx].astype(f32).reshape(RW_HEADS, RW_HD)
    kk = kk / jnp.maximum(jnp.sqrt(jnp.sum(kk * kk, axis=-1, keepdims=True)), 1e-12)
    k = k * (1.0 + (a - 1.0) * W['rw_ka'][o_idx].astype(f32).reshape(RW_HEADS, RW_HD))

    def step(S, inp):
        r_t, w_t, k_t, v_t, kk_t, a_t = inp
        sa = jnp.einsum('bhvk,bhk->bhv', S, kk_t)
        S = (S * w_t[:, :, None, :] - sa[..., None] * (kk_t * a_t)[:, :, None, :]
             + v_t[..., None] * k_t[:, :, None, :])
        return S, jnp.einsum('bhvk,bhk->bhv', S, r_t)

    xs = (r.swapaxes(0, 1), decay.swapaxes(0, 1), k.swapaxes(0, 1), v.swapaxes(0, 1), kk.swapaxes(0, 1), a.swapaxes(0, 1))
    S_last, y = lax.scan(step, S0.astype(f32), xs)
    y = y.swapaxes(0, 1)
    mu = jnp.mean(y, axis=-1, keepdims=True)
    var = jnp.mean(jnp.square(y - mu), axis=-1, keepdims=True)
    y = ((y - mu) * lax.rsqrt(var + RW_LN_EPS) * W['rw_ln_g'][o_idx].astype(f32).reshape(RW_HEADS, RW_HD)
         + W['rw_ln_b'][o_idx].astype(f32).reshape(RW_HEADS, RW_HD))
    y = y + jnp.sum(r * k * W['rw_rk'][o_idx].astype(f32), axis=-1, keepdims=True) * v
    out = y.reshape(B, T, G).astype(z.dtype) * g
    return out, z[:, -1], S_last


def _even_mixer(h, st, e, W):
    conv_buf, lru_h, k_past, v_past, lf_past = st
    B, T, _ = h.shape
    G = GROUP_W
    z = h @ W['e_w_in'][e]
    x_rnn, gate, q, k, v, og, fl = jnp.split(
        z, [LRU_W, 2 * LRU_W, 2 * LRU_W + G, 2 * LRU_W + 2 * G, 2 * LRU_W + 3 * G, 2 * LRU_W + 4 * G], axis=-1)
    xc, conv_new = _causal_conv(x_rnn, conv_buf, W['lru_conv_w'][e], W['lru_conv_b'][e])
    hs, h_last = _rg_lru(xc, lru_h, W['lru_wa'][e], W['lru_ba'][e], W['lru_wx'][e], W['lru_bx'][e], W['lru_lambda'][e])
    rnn_out = jax.nn.gelu(gate) * hs
    q = _rmsnorm(q.reshape(B, T, FOX_HEADS, FOX_HD), W['fox_q_gain'][e])
    k = _rmsnorm(k.reshape(B, T, FOX_HEADS, FOX_HD), W['fox_k_gain'][e])
    v = v.reshape(B, T, FOX_HEADS, FOX_HD)
    log_f = jax.nn.log_sigmoid((fl + W['fox_f_bias'][e]).astype(jnp.float32))
    o = _fox_attention(q, k, v, log_f, k_past, v_past, lf_past)
    fox_out = o.reshape(B, T, G) * jax.nn.sigmoid(og)
    out = jnp.concatenate([rnn_out, fox_out], axis=-1) @ W['e_w_out'][e]
    return out.astype(h.dtype), (conv_new, h_last, k, v, log_f)


def _odd_mixer(h, st, o_idx, lb, W):
    S_hg, shift, S_rw = st
    B, T, _ = h.shape
    G = GROUP_W
    f32 = jnp.float32
    z = h @ W['o_w_in'][o_idx]
    hq, hf, hi, hg, zr = jnp.split(z, [G, 2 * G, 3 * G, 4 * G], axis=-1)
    f = lb + (1.0 - lb) * jax.nn.sigmoid(hf.astype(f32))

    def heads(t):
        return t.astype(f32).reshape(B, T, HG_HEADS, t.shape[-1] // HG_HEADS)

    o, S_hg_new = _hgrn2(heads(hq), heads(1.0 - f), heads(hi), heads(jnp.log(f)), S_hg.astype(f32))
    hg_out = _rmsnorm(o, W['hg_norm_g'][o_idx].reshape(HG_HEADS, HG_DV)).reshape(B, T, G).astype(h.dtype) * jax.nn.silu(hg)
    rw_out, shift_new, S_rw_new = _rwkv7(zr, shift, S_rw, o_idx, W)
    out = jnp.concatenate([hg_out, rw_out], axis=-1) @ W['o_w_out'][o_idx]
    return out.astype(h.dtype), (S_hg_new, shift_new, S_rw_new)


def _trunk(x, states, W):
    lru_conv, lru_h, fox_k, fox_v, fox_lf, hg_S, rw_shift, rw_S = states
    sm = jax.nn.softmax(W['hg_lb_logits'].astype(jnp.float32), axis=0)
    lower_bounds = jnp.cumsum(sm, axis=0) - sm[0]
    even_new, odd_new = [], []
    for layer in range(DEPTH):
        g = W['norm_g'][layer]
        x = x + 0.5 * _swiglu(_rmsnorm(x, g[0]), W['ffn_w_in'][layer, 0], W['ffn_w_out'][layer, 0])
        h = _rmsnorm(x, g[1])
        if layer % 2 == 0:
            e = layer // 2
            m, new = _even_mixer(h, (lru_conv[e], lru_h[e], fox_k[e], fox_v[e], fox_lf[e]), e, W)
            even_new.append(new)
        else:
            o = layer // 2
            m, new = _odd_mixer(h, (hg_S[o], rw_shift[o], rw_S[o]), o, lower_bounds[layer], W)
            odd_new.append(new)
        x = x + m
        x = x + 0.5 * _swiglu(_rmsnorm(x, g[2]), W['ffn_w_in'][layer, 1], W['ffn_w_out'][layer, 1])
    ev = [jnp.stack([n[j] for n in even_new]) for j in range(5)]
    od = [jnp.stack([n[j] for n in odd_new]) for j in range(3)]
    return x, (ev[0], ev[1], ev[2], ev[3], ev[4], od[0], od[1], od[2])


def setup_inputs(seed: int = 0) -> dict:
    key = jax.random.key(seed)
    keys = iter(list(jax.random.split(key, 64)))
    f32 = jnp.float32

    def nrm(shape, scale):
        return scale * jax.random.normal(next(keys), shape, f32)

    def gain(shape):
        return 1.0 + 0.02 * jax.random.normal(next(keys), shape, f32)

    def unif(shape, lo, hi):
        return jax.random.uniform(next(keys), shape, f32, lo, hi)

    lam_u = unif((N_EVEN, LRU_W), 0.9, 0.999)
    lam_s = lam_u ** (1.0 / LRU_C)
    return {
        'x_prompt': nrm((BATCH, SEQ, D_MODEL), 1.0),
        'x_sample': nrm((DEC_BATCH, DEC_SEQ, D_MODEL), 1.0),
        'state_lru_conv': nrm((N_EVEN, DEC_BATCH, CONV_W - 1, LRU_W), 1.0),
        'state_lru_h': nrm((N_EVEN, DEC_BATCH, LRU_W), 0.5),
        'cache_fox_k': nrm((N_EVEN, DEC_BATCH, PAST_LEN, FOX_HEADS, FOX_HD), 1.0),
        'cache_fox_v': nrm((N_EVEN, DEC_BATCH, PAST_LEN, FOX_HEADS, FOX_HD), 1.0),
        'cache_fox_logf': jax.nn.log_sigmoid(2.0 + nrm((N_EVEN, DEC_BATCH, PAST_LEN, FOX_HEADS), 1.0)),
        'state_hgrn_S': nrm((N_ODD, DEC_BATCH, HG_HEADS, HG_DK, HG_DV), 0.3),
        'state_rwkv_shift': nrm((N_ODD, DEC_BATCH, RW_COLS), 1.0),
        'state_rwkv_S': nrm((N_ODD, DEC_BATCH, RW_HEADS, RW_HD, RW_HD), 0.3),
        'norm_g': gain((DEPTH, 3, D_MODEL)),
        'ffn_w_in': nrm((DEPTH, 2, D_MODEL, 2 * D_FF), D_MODEL ** -0.5),
        'ffn_w_out': nrm((DEPTH, 2, D_FF, D_MODEL), D_FF ** -0.5),
        'e_w_in': nrm((N_EVEN, D_MODEL, E_COLS), D_MODEL ** -0.5),
        'e_w_out': nrm((N_EVEN, MIX_W, D_MODEL), MIX_W ** -0.5),
        'lru_conv_w': nrm((N_EVEN, CONV_W, LRU_W), CONV_W ** -0.5),
        'lru_conv_b': nrm((N_EVEN, LRU_W), 0.01),
        'lru_wa': nrm((N_EVEN, LRU_BLOCKS, LRU_BS, LRU_BS), LRU_BS ** -0.5),
        'lru_ba': nrm((N_EVEN, LRU_W), 0.01),
        'lru_wx': nrm((N_EVEN, LRU_BLOCKS, LRU_BS, LRU_BS), LRU_BS ** -0.5),
        'lru_bx': nrm((N_EVEN, LRU_W), 0.01),
        'lru_lambda': jnp.log(lam_s) - jnp.log1p(-lam_s),
        'fox_q_gain': gain((N_EVEN, FOX_HD)),
        'fox_k_gain': gain((N_EVEN, FOX_HD)),
        'fox_f_bias': 2.0 + nrm((N_EVEN, FOX_HEADS), 0.1),
        'o_w_in': nrm((N_ODD, D_MODEL, O_COLS), D_MODEL ** -0.5),
        'o_w_out': nrm((N_ODD, MIX_W, D_MODEL), MIX_W ** -0.5),
        'hg_lb_logits': 1.0 + nrm((DEPTH, GROUP_W), 0.1),
        'hg_norm_g': gain((N_ODD, GROUP_W)),
        'rw_mu': unif((N_ODD, RW_COLS), 0.1, 0.9),
        'rw_w0': unif((N_ODD, GROUP_W), -6.0, -1.0),
        'rw_w2': nrm((N_ODD, RW_DECAY_LORA, GROUP_W), 0.1),
        'rw_a0': nrm((N_ODD, GROUP_W), 0.1),
        'rw_a2': nrm((N_ODD, RW_A_LORA, GROUP_W), 0.1),
        'rw_g2': nrm((N_ODD, RW_GATE_LORA, GROUP_W), RW_GATE_LORA ** -0.5),
        'rw_kk': 0.85 + nrm((N_ODD, GROUP_W), 0.02),
        'rw_ka': 1.0 + nrm((N_ODD, GROUP_W), 0.02),
        'rw_rk': nrm((N_ODD, RW_HEADS, RW_HD), 0.1),
        'rw_ln_g': gain((N_ODD, GROUP_W)),
        'rw_ln_b': nrm((N_ODD, GROUP_W), 0.01),
    }


def reference(x_prompt, x_sample, state_lru_conv, state_lru_h, cache_fox_k, cache_fox_v, cache_fox_logf,
              state_hgrn_S, state_rwkv_shift, state_rwkv_S, norm_g, ffn_w_in, ffn_w_out, e_w_in, e_w_out,
              lru_conv_w, lru_conv_b, lru_wa, lru_ba, lru_wx, lru_bx, lru_lambda, fox_q_gain, fox_k_gain,
              fox_f_bias, o_w_in, o_w_out, hg_lb_logits, hg_norm_g, rw_mu, rw_w0, rw_w2, rw_a0, rw_a2, rw_g2,
              rw_kk, rw_ka, rw_rk, rw_ln_g, rw_ln_b):
    W = dict(norm_g=norm_g, ffn_w_in=ffn_w_in, ffn_w_out=ffn_w_out, e_w_in=e_w_in, e_w_out=e_w_out,
             lru_conv_w=lru_conv_w, lru_conv_b=lru_conv_b, lru_wa=lru_wa, lru_ba=lru_ba, lru_wx=lru_wx,
             lru_bx=lru_bx, lru_lambda=lru_lambda, fox_q_gain=fox_q_gain, fox_k_gain=fox_k_gain,
             fox_f_bias=fox_f_bias, o_w_in=o_w_in, o_w_out=o_w_out, hg_lb_logits=hg_lb_logits,
             hg_norm_g=hg_norm_g, rw_mu=rw_mu, rw_w0=rw_w0, rw_w2=rw_w2, rw_a0=rw_a0, rw_a2=rw_a2,
             rw_g2=rw_g2, rw_kk=rw_kk, rw_ka=rw_ka, rw_rk=rw_rk, rw_ln_g=rw_ln_g, rw_ln_b=rw_ln_b)
    nb = x_prompt.shape[0]
    dt = x_prompt.dtype
    prompt_states = (jnp.zeros((N_EVEN, nb, CONV_W - 1, LRU_W), dt),
                     jnp.zeros((N_EVEN, nb, LRU_W), dt),
                     jnp.zeros((N_EVEN, nb, 0, FOX_HEADS, FOX_HD), dt),
                     jnp.zeros((N_EVEN, nb, 0, FOX_HEADS, FOX_HD), dt),
                     jnp.zeros((N_EVEN, nb, 0, FOX_HEADS), dt),
                     jnp.zeros((N_ODD, nb, HG_HEADS, HG_DK, HG_DV), dt),
                     jnp.zeros((N_ODD, nb, RW_COLS), dt),
                     jnp.zeros((N_ODD, nb, RW_HEADS, RW_HD, RW_HD), dt))
    sample_states = (state_lru_conv, state_lru_h, cache_fox_k, cache_fox_v, cache_fox_logf,
                     state_hgrn_S, state_rwkv_shift, state_rwkv_S)
    y_prompt, p_new = _trunk(x_prompt, prompt_states, W)
    y_sample, s_new = _trunk(x_sample, sample_states, W)
    lru_conv_p, lru_h_p, fox_k_p, fox_v_p, fox_logf_p, hgrn_S_p, rwkv_shift_p, rwkv_S_p = p_new
    lru_conv_s, lru_h_s, fox_k_s, fox_v_s, fox_logf_s, hgrn_S_s, rwkv_shift_s, rwkv_S_s = s_new
    return (y_prompt, y_sample, lru_conv_p, lru_conv_s, lru_h_p, lru_h_s, fox_k_p, fox_k_s, fox_v_p, fox_v_s,
            fox_logf_p, fox_logf_s, hgrn_S_p, hgrn_S_s, rwkv_shift_p, rwkv_shift_s, rwkv_S_p, rwkv_S_s)
```

```python
import contextlib
import numpy as np
import concourse.bass as bass
import concourse.mybir as mybir
from concourse.bass_utils import run_bass_kernel_spmd

F32 = mybir.dt.float32
BF16 = mybir.dt.bfloat16
AF = mybir.ActivationFunctionType
ALU = mybir.AluOpType
AX = mybir.AxisListType

D = 1024
DFF = 2816
NJ = DFF // 128
G = 512
ECOLS = 3080
RWC = 1792
OCOLS = 4 * G + RWC
EPS = 1e-6


class Buf:
    def __init__(self, t, name):
        self.t = t
        self.name = name
        self.lw = None
        self.pw = []
        self.rd = {}
        self.rdd = []
        self.psum = False

    def __getitem__(self, k):
        return self.t[k]

    def ap(self):
        return self.t


class Rec:
    __slots__ = ("eng", "fn", "dma", "deps", "ref", "val", "sem")

    def __init__(self, eng, fn, dma):
        self.eng = eng
        self.fn = fn
        self.dma = dma
        self.deps = set()
        self.ref = False
        self.val = None
        self.sem = None


ENGS = ["pe", "act", "dve", "pool", "sp"]
EPOCH = 16000
NDSEM = 8


class Prog:
    def __init__(self, nc, stack):
        self.nc = nc
        self.stack = stack
        self.ins = {e: [] for e in ENGS}
        self.nbuf = 0

    def sb(self, name, shape, dt):
        t = self.stack.enter_context(self.nc.sbuf_tensor(name, list(shape), dt))
        return Buf(t, name)

    def ps(self, name, shape, dt=F32):
        t = self.stack.enter_context(self.nc.psum_tensor(name, list(shape), dt))
        b = Buf(t, name)
        b.psum = True
        return b

    def dram(self, name, shape, dt, kind):
        t = self.nc.dram_tensor(name, list(shape), dt, kind=kind)
        return Buf(t.ap(), name)

    def view(self, arena, name, off, shape):
        n = 1
        for d in shape[1:]:
            n *= d
        ap = arena.t[0:shape[0], off:off + n]
        if len(shape) == 3:
            ap = ap.rearrange("p (a b) -> p a b", b=shape[2])
        elif len(shape) == 4:
            ap = ap.rearrange("p (a b c) -> p a b c", b=shape[2], c=shape[3])
        return Buf(ap, name)

    def barrier(self, bufs, dummy):
        self.op("dve", lambda e: e.memset(dummy[0:1, 0:1], 0.0), [], list(bufs) + [dummy])

    def op(self, eng, fn, reads=(), writes=(), dma=False, pwrites=()):
        rec = Rec(eng, fn, dma)
        for b in reads:
            if b.lw is not None:
                rec.deps.add(b.lw)
            for tk in b.pw:
                rec.deps.add(tk)
            if b.psum:
                for e2, i2 in b.rd.items():
                    if e2 != eng:
                        rec.deps.add((e2, i2))
        for b in writes:
            if b.lw is not None:
                rec.deps.add(b.lw)
            for tk in b.pw:
                rec.deps.add(tk)
            for e2, i2 in b.rd.items():
                rec.deps.add((e2, i2))
            for tk in b.rdd:
                rec.deps.add(tk)
        for b in pwrites:
            if b.lw is not None:
                rec.deps.add(b.lw)
            for e2, i2 in b.rd.items():
                rec.deps.add((e2, i2))
            for tk in b.rdd:
                rec.deps.add(tk)
        idx = len(self.ins[eng])
        self.ins[eng].append(rec)
        tok = (eng, idx)
        for b in writes:
            b.lw = tok
            b.pw = []
            b.rd = {}
            b.rdd = []
        for b in pwrites:
            b.pw.append(tok)
        for b in reads:
            if dma:
                b.rdd.append(tok)
            else:
                if b.rd.get(eng, -1) < idx:
                    b.rd[eng] = idx
        return tok

    def finish(self):
        nc = self.nc
        ins = self.ins
        for e in ENGS:
            for i, r in enumerate(ins[e]):
                nd = set()
                for (e2, i2) in r.deps:
                    if e2 == e and i2 == i:
                        continue
                    r2 = ins[e2][i2]
                    if e2 == e and e == "pe":
                        continue
                    nd.add((e2, i2))
                    r2.ref = True
                r.deps = nd
        esems = {e: [] for e in ENGS}
        dsems = {}
        for e in ENGS:
            cnt = 0
            nd = 0
            for r in ins[e]:
                if r.dma:
                    k = nd % NDSEM
                    r.sem = ("d", e, k)
                    r.val = 16 * (nd // NDSEM + 1)
                    nd += 1
                elif r.ref:
                    ep = cnt // EPOCH
                    r.sem = ("e", e, ep)
                    r.val = cnt % EPOCH + 1
                    cnt += 1
            nep = (cnt + EPOCH - 1) // EPOCH
            for ep in range(max(nep, 1)):
                esems[e].append(self.stack.enter_context(nc.semaphore(f"s_{e}_{ep}")))
            if nd:
                dsems[e] = [self.stack.enter_context(nc.semaphore(f"d_{e}_{k}")) for k in range(NDSEM)]

        def semh(key):
            if key[0] == "e":
                return esems[key[1]][key[2]]
            return dsems[key[1]][key[2]]

        final_waits = []
        for e in ("sp", "pool"):
            if e in dsems:
                last = {}
                for r in ins[e]:
                    if r.dma:
                        last[r.sem] = r.val
                final_waits += list(last.items())

        block = self.stack.enter_context(nc.Block())

        def run(e, eng):
            waited = {}
            for r in ins[e]:
                need = {}
                for (e2, i2) in r.deps:
                    r2 = ins[e2][i2]
                    if need.get(r2.sem, 0) < r2.val:
                        need[r2.sem] = r2.val
                if r.dma and r.val > 16:
                    if need.get(r.sem, 0) < r.val - 16:
                        need[r.sem] = r.val - 16
                for sk, v in need.items():
                    if waited.get(sk, 0) < v:
                        eng.wait_ge(semh(sk), v)
                        waited[sk] = v
                i = r.fn(eng)
                if r.dma:
                    i.then_inc(semh(r.sem), 16)
                elif r.ref:
                    i.then_inc(semh(r.sem), 1)
            if e == "sp":
                for sk, v in final_waits:
                    if waited.get(sk, 0) < v:
                        eng.wait_ge(semh(sk), v)

        @block.tensor
        def _(eng):
            run("pe", eng)

        @block.scalar
        def _(eng):
            run("act", eng)

        @block.vector
        def _(eng):
            run("dve", eng)

        @block.gpsimd
        def _(eng):
            run("pool", eng)

        @block.sync
        def _(eng):
            run("sp", eng)


def cdiv(a, b):
    return (a + b - 1) // b


def build(SEQ, DSEQ=16, PAST=2048, NT=512, do_sample=True, nlayers=2, stage=99, sub=99, vmode=0):
    nc = bass.Bass("TRN2", target_bir_lowering=False)
    stack = contextlib.ExitStack()
    P = Prog(nc, stack)
    TK = max(SEQ, PAST + 128)
    KTMAX = TK // 128

    def din(name, shape):
        return P.dram(name, shape, F32, "ExternalInput")

    def dout(name, shape):
        return P.dram(name, shape, F32, "ExternalOutput")

    I = {}
    for nm, sh in [("x_prompt", [SEQ, D]), ("x_sample", [DSEQ, D]), ("state_lru_conv", [3, G]), ("state_lru_h", [G]),
                   ("cache_fox_k", [PAST, G]), ("cache_fox_v", [PAST, G]), ("cache_fox_logf", [PAST, 8]),
                   ("state_hgrn_S", [4, 128, 128]), ("state_rwkv_shift", [RWC]), ("state_rwkv_S", [8, 64, 64]),
                   ("norm_g", [2, 3, D]), ("ffn_w_in", [2, 2, D, 2 * DFF]), ("ffn_w_out", [2, 2, DFF, D]),
                   ("e_w_in", [D, ECOLS]), ("e_w_out", [D, D]), ("lru_conv_w", [4, G]), ("lru_conv_b", [G]),
                   ("lru_wa", [8, 64, 64]), ("lru_ba", [G]), ("lru_wx", [8, 64, 64]), ("lru_bx", [G]),
                   ("lru_lambda", [G]), ("fox_q_gain", [64]), ("fox_k_gain", [64]), ("fox_f_bias", [8]),
                   ("o_w_in", [D, OCOLS]), ("o_w_out", [D, D]), ("hg_lb_logits", [2, G]), ("hg_norm_g", [G]),
                   ("rw_mu", [RWC]), ("rw_w0", [G]), ("rw_w2", [64, G]), ("rw_a0", [G]), ("rw_a2", [64, G]),
                   ("rw_g2", [128, G]), ("rw_kk", [G]), ("rw_ka", [G]), ("rw_rk", [G]), ("rw_ln_g", [G]),
                   ("rw_ln_b", [G]),
                   ("c_ones", [128, 128]), ("c_ident", [128, 128]), ("c_utri", [128, 128]), ("c_mask", [128, 4, 512]), ("c_m64", [128, 4, 128]), ("c_m16", [32, 4, 32]),
                   ("c_blk", [128, 128])]:
        I[nm] = din(nm, sh)
    O = {}
    for nm, sh in [("y_prompt", [SEQ, D]), ("y_sample", [DSEQ, D]), ("lru_conv_p", [3, G]), ("lru_conv_s", [3, G]),
                   ("lru_h_p", [G]), ("lru_h_s", [G]), ("fox_k_p", [SEQ, G]), ("fox_k_s", [DSEQ, G]),
                   ("fox_v_p", [SEQ, G]), ("fox_v_s", [DSEQ, G]), ("fox_logf_p", [SEQ, 8]), ("fox_logf_s", [DSEQ, 8]),
                   ("hgrn_S_p", [4, 128, 128]), ("hgrn_S_s", [4, 128, 128]), ("rwkv_shift_p", [RWC]),
                   ("rwkv_shift_s", [RWC]), ("rwkv_S_p", [8, 64, 64]), ("rwkv_S_s", [8, 64, 64])]:
        O[nm] = dout(nm, sh)
    KT_scr = P.dram("kt_scr", [8, 128, TK], BF16, "Internal")
    VW = 72
    V_scr = P.dram("v_scr", [8, 128, KTMAX, VW], BF16, "Internal")

    def MM(o, l, r, st, sp, R, W):
        P.op("pe", lambda e: e.matmul(o, lhsT=l, rhs=r, start=st, stop=sp), R, W)

    def TR(o, i, idn, R, W):
        P.op("pe", lambda e: e.transpose(o, i, idn), R, W)

    def ACT(o, i, f, R, W, bias=None, scale=None):
        kw = {}
        if bias is not None:
            kw["bias"] = bias
        if scale is not None:
            kw["scale"] = scale
        P.op("act", lambda e: e.activation(out=o, in_=i, func=f, **kw), R, W)

    def TT(eng, o, a, b, op, R, W):
        P.op(eng, lambda e: e.tensor_tensor(out=o, in0=a, in1=b, op=op), R, W)

    def TS(eng, o, a, s1, s2, op0, op1, R, W):
        if s2 is None:
            P.op(eng, lambda e: e.tensor_scalar(out=o, in0=a, scalar1=s1, scalar2=None, op0=op0), R, W)
        else:
            P.op(eng, lambda e: e.tensor_scalar(out=o, in0=a, scalar1=s1, scalar2=s2, op0=op0, op1=op1), R, W)

    def STT(o, a, sc, b, op0, op1, R, W):
        P.op("dve", lambda e: e.scalar_tensor_tensor(out=o, in0=a, scalar=sc, in1=b, op0=op0, op1=op1), R, W)

    def CP(eng, o, i, R, W):
        if eng == "act":
            P.op("act", lambda e: e.copy(out=o, in_=i), R, W)
        else:
            P.op(eng, lambda e: e.tensor_copy(out=o, in_=i), R, W)

    def DMA(q, o, i, R, W, slow=False, PW=()):
        if slow:
            P.op(q, lambda e: e.dma_start(out=o, in_=i, allow_slow_non_contiguous=True), R, W, dma=True, pwrites=PW)
        else:
            P.op(q, lambda e: e.dma_start(out=o, in_=i), R, W, dma=True, pwrites=PW)

    def RECIP(o, i, R, W):
        P.op("dve", lambda e: e.reciprocal(out=o, in_=i), R, W)

    def MEMSET(eng, o, v, W):
        P.op(eng, lambda e: e.memset(o, v), [], W)

    cnt = {}

    def rot(key, n):
        v = cnt.get(key, 0)
        cnt[key] = v + 1
        return v % n

    ones_f = P.sb("ones_f", [128, 128], F32)
    ones_fb = P.sb("ones_fb", [128, 128], BF16)
    ones1 = P.sb("ones1", [128, 128], F32)
    ident = P.sb("ident", [128, 128], F32)
    identb = P.sb("identb", [128, 128], BF16)
    utri = P.sb("utri", [128, 128], F32)
    maskb = P.sb("maskb", [128, 4, 512], BF16)
    ones3 = P.sb("ones3", [3, 128], BF16)
    normg = P.sb("normg", [128, 6, 8], F32)
    dummy = P.sb("dummy_bar", [128, 8], F32)
    DMA("sp", ones_f[:], I["c_ones"][:], [], [ones_f])
    DMA("pool", ones_fb[:], I["c_ones"][:], [], [ones_fb])
    DMA("sp", ident[:], I["c_ident"][:], [], [ident])
    DMA("sp", utri[:], I["c_utri"][:], [], [utri])
    DMA("pool", identb[:], I["c_ident"][:], [], [identb])
    DMA("pool", maskb[:], I["c_mask"][:], [], [maskb])
    MEMSET("dve", ones3[:], 1.0, [ones3])
    MEMSET("dve", ones1[:], 1.0, [ones1])
    DMA("sp", normg[:], I["norm_g"][:].rearrange("l w (c p) -> p (l w) c", p=128), [], [normg], slow=True)

    def colvec(name, src, n):
        t = P.sb(name, [128, n], F32)
        DMA("sp", t[:], src.rearrange("(c p) -> p c", p=128), [], [t], slow=True)
        return t

    convb = colvec("convb", I["lru_conv_b"][:], 4)
    lba = colvec("lba", I["lru_ba"][:], 4)
    lbx = colvec("lbx", I["lru_bx"][:], 4)
    lam = colvec("lam", I["lru_lambda"][:], 4)
    convw = P.sb("convw", [128, 4, 4], F32)
    for j in range(4):
        DMA("sp", convw[:, :, j], I["lru_conv_w"][j, :].rearrange("(c p) -> p c", p=128), [], [convw], slow=True)
    c1 = P.sb("c1", [128, 4], F32)
    c2 = P.sb("c2", [128, 4], F32)
    ACT(c1[:], lam[:], AF.Exp, [lam], [c1], scale=-1.0)
    ACT(c1[:], c1[:], AF.Ln, [c1], [c1], bias=1.0)
    TS("dve", c2[:], c1[:], -16.0, None, ALU.mult, None, [c1], [c2])
    TS("dve", c1[:], c1[:], -8.0, None, ALU.mult, None, [c1], [c1])
    bda = P.sb("bda", [128, 4, 128], BF16)
    bdx = P.sb("bdx", [128, 4, 128], BF16)
    for bd, src in ((bda, I["lru_wa"]), (bdx, I["lru_wx"])):
        MEMSET("pool", bd[:], 0.0, [bd])
        for c in range(4):
            DMA("pool", bd[0:64, c, 0:64], src[2 * c], [], [bd])
            DMA("pool", bd[64:128, c, 64:128], src[2 * c + 1], [], [bd])
    gq = P.sb("gq", [128, 64], F32)
    gk = P.sb("gk", [128, 64], F32)
    fbias = P.sb("fbias", [128, 8], F32)
    DMA("sp", gq[:], I["fox_q_gain"][:].partition_broadcast(128), [], [gq])
    DMA("sp", gk[:], I["fox_k_gain"][:].partition_broadcast(128), [], [gk])
    DMA("sp", fbias[:], I["fox_f_bias"][:].partition_broadcast(128), [], [fbias])
    TS("dve", gq[:], gq[:], 0.125, None, ALU.mult, None, [gq], [gq])
    wfl = P.sb("wfl", [128, 8, 8], BF16)
    DMA("pool", wfl[:], I["e_w_in"][:, 3072:3080].rearrange("(k p) n -> p k n", p=128), [], [wfl], slow=True)


    m64 = P.sb("m64", [128, 4, 128], F32)
    m16 = P.sb("m16", [32, 4, 32], F32)
    blk = P.sb("blk", [128, 128], F32)
    DMA("sp", m64[:], I["c_m64"][:], [], [m64])
    DMA("sp", m16[:], I["c_m16"][:], [], [m16])
    DMA("sp", blk[:], I["c_blk"][:], [], [blk])
    ones_row = P.sb("ones_row", [128, 64], F32)
    MEMSET("dve", ones_row[:], 1.0, [ones_row])
    lb0 = colvec("lb0", I["hg_lb_logits"][0, :], 4)
    lb = colvec("lb", I["hg_lb_logits"][1, :], 4)
    oml = P.sb("oml", [128, 4], F32)
    TT("dve", lb[:], lb[:], lb0[:], ALU.subtract, [lb, lb0], [lb])
    ACT(lb[:], lb[:], AF.Sigmoid, [lb], [lb])
    TS("dve", oml[:], lb[:], -1.0, 1.0, ALU.mult, ALU.add, [lb], [oml])
    hgn = colvec("hgn", I["hg_norm_g"][:], 4)
    mu = colvec("mu", I["rw_mu"][:], 14)
    omu = P.sb("omu", [128, 14], F32)
    TS("dve", omu[:], mu[:], -1.0, 1.0, ALU.mult, ALU.add, [mu], [omu])
    nw0 = colvec("nw0", I["rw_w0"][:], 4)
    TS("dve", nw0[:], nw0[:], -1.0, None, ALU.mult, None, [nw0], [nw0])
    a0 = colvec("a0", I["rw_a0"][:], 4)
    kkw = colvec("kkw", I["rw_kk"][:], 4)
    kaw = colvec("kaw", I["rw_ka"][:], 4)
    omka = P.sb("omka", [128, 4], F32)
    TS("dve", omka[:], kaw[:], -1.0, 1.0, ALU.mult, ALU.add, [kaw], [omka])
    rkw = colvec("rkw", I["rw_rk"][:], 4)
    lng = colvec("lng", I["rw_ln_g"][:], 4)
    lnb = colvec("lnb", I["rw_ln_b"][:], 4)
    w2t = P.sb("w2t", [64, G], BF16)
    a2t = P.sb("a2t", [128, G], BF16)
    g2t = P.sb("g2t", [128, G], BF16)
    DMA("pool", w2t[:, :], I["rw_w2"][:, :], [], [w2t])
    DMA("pool", a2t[64:128, :], I["rw_a2"][:, :], [], [a2t])
    DMA("pool", g2t[:, :], I["rw_g2"][:, :], [], [g2t])
    S_hg = P.sb("S_hg", [128, 4, 128], F32)
    S_hgb = P.sb("S_hgb", [128, 4, 128], BF16)
    ST = P.sb("ST_rw", [128, 4, 128], F32)
    STb = P.sb("ST_rwb", [128, 4, 128], BF16)
    zprev = P.sb("zprev", [128, 14], F32)


    WQ = "sp"
    ffi_t = nc.dram_tensor("ffn_in_b", [2, 2, NJ, 128, 2048], BF16, kind="Internal").ap()
    ffo_t = nc.dram_tensor("ffn_out_b", [2, 2, 2, NJ // 2, 128, 1024], BF16, kind="Internal").ap()
    ffi_b, ffo_b = {}, {}
    mixw = {}
    conv_done = set()
    mix_t = {}
    for nm, ncol in (("e_w_in", 3072), ("o_w_in", OCOLS)):
        mix_t[nm] = nc.dram_tensor(nm + "_b", [ncol // 128, 128, 1024], BF16, kind="Internal").ap()
    for nm, c0s in (("e_w_in", (1024, 1536, 2048, 2560)), ("e_w_out", (0, 512)), ("o_w_out", (0, 512))):
        mix_t[nm + "_t"] = nc.dram_tensor(nm + "_tb", [len(c0s), 128, 4096], BF16, kind="Internal").ap()

    def convert(key):
        if key in conv_done:
            return
        conv_done.add(key)
        if key[0] == "ffn":
            _, l, w = key
            for j in range(NJ):
                bb = Buf(ffi_t[l, w, j], "ffi")
                ffi_b[(l, w, j)] = bb
                for gu in range(2):
                    c0 = gu * DFF + j * 128
                    DMA("pool", bb[:, :].rearrange("p (c g n) -> p c g n", g=2, n=128)[:, :, gu, :],
                        I["ffn_w_in"][l, w, :, c0:c0 + 128].rearrange("(c p) n -> p c n", p=128), [], [], PW=[bb])
            for half in range(2):
                for j2 in range(NJ // 2):
                    bb = Buf(ffo_t[l, w, half, j2], "ffo")
                    ffo_b[(l, w, half, j2)] = bb
                    DMA("pool", bb[:, :].rearrange("p (a n) -> p a n", a=2),
                        I["ffn_w_out"][l, w, j2 * 256:(j2 + 1) * 256, half * 512:(half + 1) * 512].rearrange("(a p) n -> p a n", p=128),
                        [], [], PW=[bb])
        else:
            for nm, ncol in ((("e_w_in", 3072),) if key[0] == "even" else (("o_w_in", OCOLS),)):
                for c in range(ncol // 128):
                    bb = Buf(mix_t[nm][c], nm + "_b")
                    mixw[(nm, c)] = bb
                    DMA("pool", bb[:, :].rearrange("p (k n) -> p k n", n=128),
                        I[nm][:, c * 128:(c + 1) * 128].rearrange("(k p) n -> p k n", p=128), [], [], PW=[bb])
            groups = (("e_w_in", (1024, 1536, 2048, 2560)), ("e_w_out", (0, 512))) if key[0] == "even" else (("o_w_out", (0, 512)),)
            for nm, c0s in groups:
                for gi_, c0 in enumerate(c0s):
                    bb = Buf(mix_t[nm + "_t"][gi_], nm + "_tb")
                    mixw[(nm + "_t", c0)] = bb
                    DMA("pool", bb[:, :].rearrange("p (k n) -> p k n", n=512),
                        I[nm][:, c0:c0 + 512].rearrange("(k p) n -> p k n", p=128), [], [], PW=[bb])

    x = P.sb("x", [128, 8, NT], F32)
    h = P.sb("h", [128, 8, NT], BF16)
    sqt = [P.sb(f"sqt{i}", [128, NT], BF16) for i in range(4)]
    rstd = P.sb("rstd", [128, NT], F32)
    sg = [P.sb(f"sg{i}", [128, NT], F32) for i in range(2)]
    NWB = 3
    wib = [P.sb(f"wib{i}", [128, 8, 2, 128], BF16) for i in range(NWB)]
    wob = [P.sb(f"wob{i}", [128, 2, 512], BF16) for i in range(NWB)]
    wtm = [P.sb(f"wtm{i}", [128, 8, 512], BF16) for i in range(2)]
    xtms = [P.sb(f"xtm{i}", [128, D], F32) for i in range(2)]
    banks = [P.ps(f"bank{i}", [128, 512], F32) for i in range(8)]
    hcar = P.sb("hcar", [128, 4], F32)
    xhist = P.sb("xhist", [128, 4, 3], F32)
    Rsum = P.sb("Rsum", [128, 8], F32)
    negF = P.sb("negF", [128, KTMAX, 8], F32)
    ktl = [P.sb(f"ktl{i}", [128, 512], BF16) for i in range(4)]
    vl = [P.sb(f"vl{i}", [128, 4, VW], BF16) for i in range(4)]
    PT = [P.sb(f"PT{i}", [128, 512], BF16) for i in range(3)]
    ABF = P.sb("arena_bf", [128, 20480], BF16)
    AF32 = P.sb("arena_f32", [128, 13056], F32)
    hid = P.view(ABF, "hid", 0, [128, NJ, NT])
    o = 0
    ev_bf = {}
    for nm, sh in [("xcb", [128, NT]), ("rnn", [128, 4, NT]), ("QT", [128, 8, NT]), ("KTt", [128, 8, NT]),
                   ("Vt", [128, 4, 8, VW]), ("FQ", [3, 8, NT]), ("foT", [128, 4, NT]), ("F3", [128, 4, 8, 3])]:
        n = int(np.prod(sh[1:]))
        ev_bf[nm] = P.view(ABF, "ev_" + nm, o, sh)
        o += n
    assert o <= 20480, o
    o = 0
    ev_f = {}
    for nm, sh in [("xp", [128, 4, NT + 3]), ("xc", [128, NT]), ("hs", [128, 4, NT]), ("t0", [128, NT]), ("t1", [128, NT]),
                   ("t2", [128, NT]), ("t3", [128, NT]), ("t4", [128, NT]), ("qn", [128, NT]), ("kn", [128, NT]),
                   ("vf", [128, NT]), ("ogs", [128, 4, NT]), ("fo", [128, 4, NT]), ("lf", [128, 4, 8]),
                   ("s8a", [128, 8]), ("s8b", [128, 8]), ("s8c", [128, 8]), ("rec", [128, 8])]:
        n = int(np.prod(sh[1:]))
        ev_f[nm] = P.view(AF32, "evf_" + nm, o, sh)
        o += n
    assert o <= 13056, o

    o = 0
    od_bf = {}
    for nm, sh in [("bdA", [128, 1024]), ("bdB", [128, 1024]), ("bdK", [128, 1024]), ("bdR", [128, 1024]),
                   ("hgT", [128, 4, NT]), ("rwT", [128, 4, NT]), ("b0", [128, NT]), ("b1", [128, NT]), ("b2", [128, NT]),
                   ("b3", [128, NT])] + [("q%d" % i, [128, 128]) for i in range(72)]:
        n = int(np.prod(sh[1:]))
        od_bf[nm] = P.view(ABF, "od_" + nm, o, sh)
        o += n
    assert o <= 20480, o
    o = 0
    od_f = {}
    for nm, sh in ([("T%d" % i, [128, NT + 8]) for i in range(16)] +
                   [("fV", [128, 1024]), ("fB", [128, 1024]), ("fK", [128, 1024]), ("fA", [128, 1024])]):
        n = int(np.prod(sh[1:]))
        od_f[nm] = P.view(AF32, "odf_" + nm, o, sh)
        o += n
    assert o <= 13056, o
    grp_odd = list(od_bf.values()) + list(od_f.values())
    grp_ffn = [hid]
    grp_even = list(ev_bf.values()) + list(ev_f.values())
    grp_all = grp_even + grp_ffn + grp_odd

    def rmsnorm(N, gi):
        b = banks[7]
        for c in range(8):
            s = sqt[rot("sqt", 4)]
            if c % 2 == 0:
                ACT(s[:, :N], x[:, c, :N], AF.Square, [x], [s])
            else:
                TT("dve", s[:, :N], x[:, c, :N], x[:, c, :N], ALU.mult, [x], [s])
            MM(b[:, :N], ones_fb[:], s[:, :N], c == 0, c == 7, [ones_fb, s], [b])
        ACT(rstd[:, :N], b[:, :N], AF.Ln, [b], [rstd], bias=EPS)
        ACT(rstd[:, :N], rstd[:, :N], AF.Exp, [rstd], [rstd], scale=-0.5)
        for c in range(8):
            STT(h[:, c, :N], x[:, c, :N], normg[:, gi, c:c + 1], rstd[:, :N], ALU.mult, ALU.mult, [x, normg, rstd], [h])

    def ffn(N, layer, which):
        convert(("ffn", layer, which))
        rmsnorm(N, layer * 3 + (0 if which == 0 else 2))
        win = I["ffn_w_in"]
        wout = I["ffn_w_out"]
        for j in range(NJ):
            wb = wib[rot("wib", NWB)]
            src = ffi_b[(layer, which, j)]
            DMA(WQ, wb[:, :, :, :], src[:, :].rearrange("p (c g n) -> p c g n", g=2, n=128), [src], [wb])
            bg, bu = banks[(2 * j) % 4], banks[(2 * j + 1) % 4]
            for gu, bk in ((0, bg), (1, bu)):
                for c in range(8):
                    MM(bk[:, :N], wb[:, c, gu, :], h[:, c, :N], c == 0, c == 7, [wb, h], [bk])
            s = sg[rot("sg", 2)]
            ACT(s[:, :N], bg[:, :N], AF.Silu, [bg], [s])
            TT("dve", hid[:, j, :N], s[:, :N], bu[:, :N], ALU.mult, [s, bu], [hid])
        for half in range(2):
            acc = [banks[4 + m] for m in range(4)]
            for j in range(NJ):
                if j % 2 == 0:
                    wb = wob[rot("wob", NWB)]
                    src = ffo_b[(layer, which, half, j // 2)]
                    DMA(WQ, wb[:, :, :], src[:, :].rearrange("p (a n) -> p a n", a=2), [src], [wb])
                for m in range(4):
                    MM(acc[m][:, :N], wb[:, j % 2, m * 128:(m + 1) * 128], hid[:, j, :N], j == 0, j == NJ - 1, [wb, hid], [acc[m]])
            for m in range(4):
                c = half * 4 + m
                STT(x[:, c, :N], acc[m][:, :N], 0.5, x[:, c, :N], ALU.mult, ALU.add, [acc[m], x], [x])

    def load_x(src, t0, N):
        for s in range(cdiv(N, 128)):
            r = min(128, N - s * 128)
            xtm = xtms[rot("xtm", 2)]
            DMA("sp", xtm[:r, :], src[t0 + s * 128:t0 + s * 128 + r, :], [], [xtm])
            for c in range(8):
                bk = banks[rot("bk4", 4)]
                TR(bk[:, :r], xtm[:r, c * 128:(c + 1) * 128], ident[:r, :r], [xtm, ident], [bk])
                CP("dve" if c % 2 == 0 else "act", x[:, c, s * 128:s * 128 + r], bk[:, :r], [bk], [x])

    def store_x(dst, t0, N):
        for s in range(cdiv(N, 128)):
            r = min(128, N - s * 128)
            xtm = xtms[rot("xtm", 2)]
            for c in range(8):
                bk = banks[rot("bk4", 4)]
                TR(bk[:r, :128], x[:, c, s * 128:s * 128 + r], ident[:, :], [x, ident], [bk])
                CP("dve" if c % 2 == 0 else "act", xtm[:r, c * 128:(c + 1) * 128], bk[:r, :128], [bk], [xtm])
            DMA("sp", dst[t0 + s * 128:t0 + s * 128 + r, :], xtm[:r, :], [xtm], [])

    E = ev_f
    B = ev_bf

    def norm_heads(dst, src_ps, gain, r):
        ACT(E["t0"][:r, :], src_ps[:r, :], AF.Square, [src_ps], [E["t0"]])
        P.op("dve", lambda e: e.tensor_reduce(out=E["s8a"][:r, :], in_=E["t0"][:r, :].rearrange("p (h d) -> p h d", d=64),
                                              axis=AX.X, op=ALU.add), [E["t0"]], [E["s8a"]])
        TS("dve", E["s8a"][:r, :], E["s8a"][:r, :], 1.0 / 64, EPS, ALU.mult, ALU.add, [E["s8a"]], [E["s8a"]])
        ACT(E["s8a"][:r, :], E["s8a"][:r, :], AF.Sqrt, [E["s8a"]], [E["s8a"]])
        P.op("dve", lambda e: e.reciprocal(out=E["s8a"][:r, :], in_=E["s8a"][:r, :]), [E["s8a"]], [E["s8a"]])
        d3 = dst[:r, :].rearrange("p (h d) -> p h d", d=64)
        TT("dve", d3, src_ps[:r, :].rearrange("p (h d) -> p h d", d=64),
           E["s8a"][:r, :].unsqueeze(2).to_broadcast([r, 8, 64]), ALU.mult, [src_ps, E["s8a"]], [dst])
        TT("dve", d3, d3, gain[:r, :].unsqueeze(1).to_broadcast([r, 8, 64]), ALU.mult, [dst, gain], [dst])

    def to_featmajor(dstT, src, s, r, eng_alt=0):
        for c in range(4):
            bk = banks[rot("bk4", 4)]
            TR(bk[:, :r], src[:r, c * 128:(c + 1) * 128], ident[:r, :r], [src, ident], [bk])
            CP("act" if (c + eng_alt) % 2 == 0 else "dve", dstT[:, c, s * 128:s * 128 + r], bk[:, :r], [bk], [dstT])

    def to_heads(dstT, src, s, r, eng_alt=0):
        for c in range(4):
            bk = banks[rot("bk4", 4)]
            TR(bk[:, :r], src[:r, c * 128:(c + 1) * 128], ident[:r, :r], [src, ident], [bk])
            e0, e1 = ("act", "dve") if (c + eng_alt) % 2 == 0 else ("dve", "act")
            P.op(e0, (lambda e, o_=dstT[0:64, 2 * c, s * 128:s * 128 + r], i_=bk[0:64, :r], en=e0:
                      (e.copy(out=o_, in_=i_) if en == "act" else e.tensor_copy(out=o_, in_=i_))), [bk], [], pwrites=[dstT])
            P.op(e1, (lambda e, o_=dstT[64:128, 2 * c + 1, s * 128:s * 128 + r], i_=bk[64:128, :r], en=e1:
                      (e.copy(out=o_, in_=i_) if en == "act" else e.tensor_copy(out=o_, in_=i_))), [bk], [], pwrites=[dstT])

    def init_heads():
        MEMSET("pool", B["QT"][:, :, :], 0.0, [B["QT"]])
        MEMSET("pool", B["KTt"][:, :, :], 0.0, [B["KTt"]])
        kv = B["KTt"][:, :, :].rearrange("p (h two) n -> p h two n", two=2)
        P.op("pool", lambda e: e.memset(kv[64:67, :, 0, :], 1.0), [], [], pwrites=[B["KTt"]])
        P.op("pool", lambda e: e.memset(kv[0:3, :, 1, :], 1.0), [], [], pwrites=[B["KTt"]])

    def cumF(lf_ap, lfbuf, r, kt, s, want_fq):
        bk = banks[rot("bk4", 4)]
        MM(bk[:r, 0:8], utri[:r, :r], lf_ap, True, False, [utri, lfbuf], [bk])
        MM(bk[:r, 0:8], ones1[:, :r], Rsum[:, :], False, True, [ones1, Rsum], [bk])
        TS("dve", negF[:r, kt, :], bk[:r, 0:8], -1.0, None, ALU.mult, None, [bk], [negF])
        TT("pool", Rsum[:r, :], Rsum[:r, :], lf_ap, ALU.add, [Rsum, lfbuf], [Rsum])
        if want_fq:
            F3 = B["F3"]
            CP("dve", F3[:r, s, :, 0], bk[:r, 0:8], [bk], [F3])
            CP("dve", E["s8b"][:r, :], F3[:r, s, :, 0], [F3], [E["s8b"]])
            TT("dve", E["s8c"][:r, :], bk[:r, 0:8], E["s8b"][:r, :], ALU.subtract, [bk, E["s8b"]], [E["s8c"]])
            CP("dve", F3[:r, s, :, 1], E["s8c"][:r, :], [E["s8c"]], [F3])
            CP("dve", E["s8b"][:r, :], F3[:r, s, :, 1], [F3], [E["s8b"]])
            TT("dve", E["s8c"][:r, :], E["s8c"][:r, :], E["s8b"][:r, :], ALU.subtract, [E["s8c"], E["s8b"]], [E["s8c"]])
            CP("dve", F3[:r, s, :, 2], E["s8c"][:r, :], [E["s8c"]], [F3])

    def store_kv(key_base, N):
        nsub = cdiv(N, 128)
        r = min(128, N)
        ktb = key_base // 128
        DMA("sp", KT_scr[:, :, key_base:key_base + N].rearrange("q p t -> p q t"), B["KTt"][:, :, :N], [B["KTt"]], [KT_scr], slow=True)
        for s in range(nsub):
            rr = min(128, N - s * 128)
            DMA("sp", V_scr[:, :rr, ktb + s, :].rearrange("h p d -> p h d"), B["Vt"][:rr, s, :, :], [B["Vt"]], [V_scr], slow=True)

    def ingest_past(PASTN):
        for c in range(PASTN // 512):
            DMA("sp", E["lf"][:, :, :], I["cache_fox_logf"][c * 512:(c + 1) * 512, :].rearrange("(s p) h -> p s h", p=128),
                [], [E["lf"]], slow=True)
            for s in range(4):
                t0 = c * 512 + s * 128
                DMA("sp", E["kn"][:, :], I["cache_fox_k"][t0:t0 + 128, :], [], [E["kn"]])
                to_heads(B["KTt"], E["kn"], s, 128)
                DMA("sp", E["vf"][:, :], I["cache_fox_v"][t0:t0 + 128, :], [], [E["vf"]])
                CP("pool", B["Vt"][:, s, :, 0:64], E["vf"][:, :].rearrange("p (h d) -> p h d", d=64), [E["vf"]], [B["Vt"]])
                cumF(E["lf"][:, s, :], E["lf"], 128, c * 4 + s, s, False)
            store_kv(c * 512, 512)

    def even_mixer(N, S):
        key_base, t0 = S["key_base"], S["t0"]
        nsub = cdiv(N, 128)
        W = I["e_w_in"]
        convert(("even",))
        rmsnorm(N, 1)
        xp, xc, hs = E["xp"], E["xc"], E["hs"]
        CP("pool", xp[:, :, 0:3], xhist[:, :, :], [xhist], [xp])
        for c in range(4):
            wb = wib[rot("wib", NWB)]
            src = mixw[("e_w_in", c)]
            DMA(WQ, wb[:, :, 0, :], src[:, :].rearrange("p (k n) -> p k n", n=128), [src], [wb])
            bk = banks[rot("bk4", 4)]
            for kc in range(8):
                MM(bk[:, :N], wb[:, kc, 0, :], h[:, kc, :N], kc == 0, kc == 7, [wb, h], [bk])
            CP("act", xp[:, c, 3:3 + N], bk[:, :N], [bk], [xp])
        CP("pool", xhist[:, :, :], xp[:, :, N:N + 3], [xp], [xhist])
        sets = [(xc, E["t0"], E["t1"], E["t2"], E["t3"], B["xcb"]),
                (E["qn"], E["kn"], E["vf"], E["fo"][:, 0, :], E["fo"][:, 1, :], B["foT"][:, 0, :])]
        setbufs = [(xc, E["t0"], E["t1"], E["t2"], E["t3"], B["xcb"]),
                   (E["qn"], E["kn"], E["vf"], E["fo"], E["fo"], B["foT"])]
        for c in range(4):
            xc_, r_, i_, a_, q_, xb_ = sets[c % 2]
            bxc, br_, bi_, ba_, bq_, bxb = setbufs[c % 2]
            TS("dve", xc_[:, :N], xp[:, c, 0:N], convw[:, c, 0:1], convb[:, c:c + 1], ALU.mult, ALU.add, [xp, convw, convb], [bxc])
            for j in range(1, 4):
                STT(xc_[:, :N], xp[:, c, j:j + N], convw[:, c, j:j + 1], xc_[:, :N], ALU.mult, ALU.add, [xp, convw, bxc], [bxc])
            CP("act", xb_[:, :N], xc_[:, :N], [bxc], [bxb])
            b1, b2 = banks[rot("bk4", 4)], banks[rot("bk4", 4)]
            MM(b1[:, :N], bda[:, c, :], xb_[:, :N], True, True, [bda, bxb], [b1])
            MM(b2[:, :N], bdx[:, c, :], xb_[:, :N], True, True, [bdx, bxb], [b2])
            ACT(r_[:, :N], b1[:, :N], AF.Sigmoid, [b1, lba], [br_], bias=lba[:, c:c + 1])
            ACT(i_[:, :N], b2[:, :N], AF.Sigmoid, [b2, lbx], [bi_], bias=lbx[:, c:c + 1])
            ACT(a_[:, :N], r_[:, :N], AF.Exp, [br_, c1], [ba_], scale=c1[:, c:c + 1])
            if bq_ is E["fo"]:
                ACT(q_[:, :N], r_[:, :N], AF.Exp, [br_, c2], [], scale=c2[:, c:c + 1]) if False else \
                    P.op("act", (lambda e, o_=q_[:, :N], in__=r_[:, :N], sc=c2[:, c:c + 1]: e.activation(out=o_, in_=in__, func=AF.Exp, scale=sc)),
                         [br_, c2], [bq_])
            else:
                ACT(q_[:, :N], r_[:, :N], AF.Exp, [br_, c2], [bq_], scale=c2[:, c:c + 1])
            ACT(q_[:, :N], q_[:, :N], AF.Sqrt, [bq_], [bq_], bias=1.0, scale=-1.0)
            TT("dve", i_[:, :N], i_[:, :N], xc_[:, :N], ALU.mult, [bi_, bxc], [bi_])
            TT("dve", i_[:, :N], i_[:, :N], q_[:, :N], ALU.mult, [bi_, bq_], [bi_])
            P.op("dve", (lambda e, c=c, a__=a_[:, :N], i__=i_[:, :N]: e.tensor_tensor_scan(
                out=hs[:, c, :N], data0=a__, data1=i__, initial=hcar[:, c:c + 1], op0=ALU.mult, op1=ALU.add)),
                 [ba_, bi_, hcar], [hs])
            CP("dve", hcar[:, c:c + 1], hs[:, c, N - 1:N], [hs], [hcar])
        if stage < 1:
            return
        for c in range(4):
            wb = wib[rot("wib", NWB)]
            src = mixw[("e_w_in", 4 + c)]
            DMA(WQ, wb[:, :, 0, :], src[:, :].rearrange("p (k n) -> p k n", n=128), [src], [wb])
            bk = banks[rot("bk4", 4)]
            for kc in range(8):
                MM(bk[:, :N], wb[:, kc, 0, :], h[:, kc, :N], kc == 0, kc == 7, [wb, h], [bk])
            ga, gb = (E["t0"], E["t4"]) if c % 2 == 0 else (E["t1"], E["t2"])
            ACT(ga[:, :N], bk[:, :N], AF.Square, [bk], [ga])
            TS("dve", ga[:, :N], ga[:, :N], 0.044715, 1.0, ALU.mult, ALU.add, [ga], [ga])
            TT("dve", ga[:, :N], ga[:, :N], bk[:, :N], ALU.mult, [ga, bk], [ga])
            ACT(ga[:, :N], ga[:, :N], AF.Sigmoid, [ga], [ga], scale=1.5957691216057308)
            TT("dve", gb[:, :N], bk[:, :N], hs[:, c, :N], ALU.mult, [bk, hs], [gb])
            TT("dve", B["rnn"][:, c, :N], gb[:, :N], ga[:, :N], ALU.mult, [gb, ga], [B["rnn"]])
        if stage < 2:
            return
        for gi, c0 in enumerate((1024, 1536, 2048, 2560)):
            wt = wtm[rot("wtm", 2)]
            src = mixw[("e_w_in_t", c0)]
            DMA(WQ, wt[:, :, :], src[:, :].rearrange("p (k n) -> p k n", n=512), [src], [wt])
            for s in range(nsub):
                r = min(128, N - s * 128)
                bk = banks[4 + rot("bk4b", 4)]
                for kc in range(8):
                    MM(bk[:r, :], h[:, kc, s * 128:s * 128 + r], wt[:, kc, :], kc == 0, kc == 7, [h, wt], [bk])
                if gi > sub:
                    continue
                if gi == 0:
                    norm_heads(E["qn"], bk, gq, r)
                    to_heads(B["QT"], E["qn"], s, r)
                elif gi == 1:
                    norm_heads(E["kn"], bk, gk, r)
                    DMA("sp", S["fox_k"][t0 + s * 128:t0 + s * 128 + r, :], E["kn"][:r, :], [E["kn"]], [])
                    to_heads(B["KTt"], E["kn"], s, r, 1)
                elif gi == 2:
                    if vmode in (0, 1):
                        CP("act", E["vf"][:r, :], bk[:r, :], [bk], [E["vf"]])
                        DMA("sp", S["fox_v"][t0 + s * 128:t0 + s * 128 + r, :], E["vf"][:r, :], [E["vf"]], [])
                    if vmode in (0, 2):
                        CP("dve", B["Vt"][:r, s, :, 0:64], bk[:r, :].rearrange("p (h d) -> p h d", d=64), [bk], [B["Vt"]])
                else:
                    ACT(E["ogs"][:r, s, :], bk[:r, :], AF.Sigmoid, [bk], [E["ogs"]])
        if stage < 3:
            return
        for s in range(nsub):
            r = min(128, N - s * 128)
            bk = banks[rot("bk4", 4)]
            for kc in range(8):
                MM(bk[:r, 0:8], h[:, kc, s * 128:s * 128 + r], wfl[:, kc, :], kc == 0, kc == 7, [h, wfl], [bk])
            TT("dve", E["s8b"][:r, :], bk[:r, 0:8], fbias[:r, :], ALU.add, [bk, fbias], [E["s8b"]])
            ACT(E["s8b"][:r, :], E["s8b"][:r, :], AF.Exp, [E["s8b"]], [E["s8b"]], scale=-1.0)
            ACT(E["s8b"][:r, :], E["s8b"][:r, :], AF.Ln, [E["s8b"]], [E["s8b"]], bias=1.0)
            TS("dve", E["lf"][:r, s, :], E["s8b"][:r, :], -1.0, None, ALU.mult, None, [E["s8b"]], [E["lf"]])
            DMA("sp", S["fox_lf"][t0 + s * 128:t0 + s * 128 + r, :], E["lf"][:r, s, :], [E["lf"]], [])
            cumF(E["lf"][:r, s, :], E["lf"], r, key_base // 128 + s, s, True)
        for hh in range(8):
            bk = banks[rot("bk4", 4)]
            for s in range(nsub):
                r = min(128, N - s * 128)
                MM(bk[0:3, s * 128:s * 128 + r], B["F3"][:r, s, hh, :], identb[:r, :r], True, True, [B["F3"], identb], [bk])
            CP("act" if hh % 2 == 0 else "dve", B["FQ"][0:3, hh, :N], bk[0:3, :N], [bk], [], ) if False else \
                P.op("act" if hh % 2 == 0 else "dve",
                     (lambda e, o_=B["FQ"][0:3, hh, :N], i_=bk[0:3, :N], en=("act" if hh % 2 == 0 else "dve"):
                      (e.copy(out=o_, in_=i_) if en == "act" else e.tensor_copy(out=o_, in_=i_))), [bk], [], pwrites=[B["FQ"]])
        fqv = B["FQ"][0:3, :, :N].rearrange("p (h two) n -> p h two n", two=2)
        qv = B["QT"][:, :, :N].rearrange("p (h two) n -> p h two n", two=2)
        DMA("sp", qv[64:67, :, 0, :], fqv[:, :, 0, :], [B["FQ"]], [], PW=[B["QT"]])
        DMA("sp", qv[0:3, :, 1, :], fqv[:, :, 1, :], [B["FQ"]], [], PW=[B["QT"]])
        if stage < 4:
            return
        store_kv(key_base, N)
        n_keys = key_base + N
        nch = cdiv(n_keys, 512)
        for p in range(4):
            Ob = [banks[4 + rot("bkO", 4)], banks[4 + rot("bkO", 4)]]
            tiles = []
            loads = {}
            for c in range(nch):
                vk = min(512, n_keys - c * 512)
                diag = (c == nch - 1)
                for hh in range(2):
                    hd = 2 * p + hh
                    kb = ktl[(c % 2) * 2 + hh]
                    vb = vl[(c % 2) * 2 + hh]
                    loads[(c, hh)] = (kb, vb, vk, hd)
                    for kk in range(cdiv(vk, 128)):
                        rk = min(128, vk - kk * 128)
                        tiles.append((c, hh, hd, kk, rk, diag, kb, vb))

            def issue_loads(c):
                for hh in range(2):
                    kb, vb, vk, hd = loads[(c, hh)]
                    nkt = cdiv(vk, 128)
                    rr = min(128, vk)
                    DMA("sp", kb[:, :vk], KT_scr[hd, :, c * 512:c * 512 + vk], [KT_scr], [kb])
                    DMA("sp", vb[:rr, :nkt, :], V_scr[hd, :rr, c * 4:c * 4 + nkt, :], [V_scr], [vb])

            nt_ = len(tiles)
            first_seen = [True, True]
            last_ti = [max(i for i, t_ in enumerate(tiles) if t_[1] == hh_) for hh_ in range(2)]
            loaded = set()

            def need(c):
                if c < nch and c not in loaded:
                    loaded.add(c)
                    issue_loads(c)

            def emit_S(ti):
                c, hh, hd, kk, rk, diag, kb, vb = tiles[ti]
                need(c)
                sb_ = banks[rot("bkS", 4)]
                MM(sb_[:rk, :N], kb[:, kk * 128:kk * 128 + rk], B["QT"][:, hd, :N], True, not diag, [kb, B["QT"]], [sb_])
                if diag:
                    MM(sb_[:rk, :N], identb[:rk, :rk], maskb[:rk, kk, :N], False, True, [identb, maskb], [sb_])
                return sb_

            need(0)
            pend = emit_S(0) if nt_ else None
            for ti in range(nt_):
                c, hh, hd, kk, rk, diag, kb, vb = tiles[ti]
                if hh == 0 and kk == 0:
                    need(c + 1)
                sb_ = pend
                if ti + 1 < nt_:
                    pend = emit_S(ti + 1)
                pt = PT[rot("PT", 3)]
                ACT(pt[:rk, :N], sb_[:rk, :N], AF.Exp, [sb_, negF], [pt], bias=negF[:rk, c * 4 + kk, hd:hd + 1])
                qlist = [qs for qs in range(nsub) if not (diag and qs < kk)]
                for qs in qlist:
                    rq = min(128, N - qs * 128)
                    last = (ti == last_ti[hh]) and (qs == qlist[-1])
                    MM(Ob[hh][:rq, qs * 65:(qs + 1) * 65], pt[:rk, qs * 128:qs * 128 + rq], vb[:rk, kk, 0:65],
                       first_seen[hh], last, [pt, vb], [Ob[hh]])
                    first_seen[hh] = False
            for hh in range(2):
                hd = 2 * p + hh
                for qs in range(nsub):
                    rq = min(128, N - qs * 128)
                    RECIP(E["rec"][:rq, qs:qs + 1], Ob[hh][:rq, qs * 65 + 64:qs * 65 + 65], [Ob[hh]], [E["rec"]])
                    STT(E["fo"][:rq, qs, hd * 64:(hd + 1) * 64], Ob[hh][:rq, qs * 65:qs * 65 + 64], E["rec"][:rq, qs:qs + 1],
                        E["ogs"][:rq, qs, hd * 64:(hd + 1) * 64], ALU.mult, ALU.mult, [Ob[hh], E["rec"], E["ogs"]], [E["fo"]])
        if stage < 5:
            return
        for s in range(nsub):
            r = min(128, N - s * 128)
            for c in range(4):
                bk = banks[rot("bk4", 4)]
                TR(bk[:, :r], E["fo"][:r, s, c * 128:(c + 1) * 128], ident[:r, :r], [E["fo"], ident], [bk])
                CP("act" if c % 2 == 0 else "dve", B["foT"][:, c, s * 128:s * 128 + r], bk[:, :r], [bk], [B["foT"]])
        if stage < 6:
            return
        Wo = I["e_w_out"]
        for half in range(2):
            wt = wtm[rot("wtm", 2)]
            src = mixw[("e_w_out_t", half * 512)]
            DMA(WQ, wt[:, :, :], src[:, :].rearrange("p (k n) -> p k n", n=512), [src], [wt])
            for m in range(4):
                bk = banks[rot("bk4", 4)]
                for kc in range(8):
                    rhs = B["rnn"][:, kc, :N] if kc < 4 else B["foT"][:, kc - 4, :N]
                    MM(bk[:, :N], wt[:, kc, m * 128:(m + 1) * 128], rhs, kc == 0, kc == 7, [wt, B["rnn"], B["foT"]], [bk])
                cc = half * 4 + m
                TT("dve", x[:, cc, :N], x[:, cc, :N], bk[:, :N], ALU.add, [x, bk], [x])

    def even_finish(S):
        for j in range(3):
            DMA("sp", S["lru_conv"][j, :].rearrange("(c p) -> p c", p=128), xhist[:, :, j], [xhist], [], slow=True)
        DMA("sp", S["lru_h"][:].rearrange("(c p) -> p c", p=128), hcar[:, :], [hcar], [], slow=True)

    def even_init(S, sample):
        if sample:
            for j in range(3):
                DMA("sp", xhist[:, :, j], I["state_lru_conv"][j, :].rearrange("(c p) -> p c", p=128), [], [xhist], slow=True)
            DMA("sp", hcar[:, :], I["state_lru_h"][:].rearrange("(c p) -> p c", p=128), [], [hcar], slow=True)
        else:
            MEMSET("dve", xhist[:, :, :], 0.0, [xhist])
            MEMSET("dve", hcar[:, :], 0.0, [hcar])
        MEMSET("dve", Rsum[:, :], 0.0, [Rsum])
        MEMSET("pool", B["Vt"][:, :, :, :], 1.0, [B["Vt"]])

    Fo = od_f
    Bo = od_bf
    Wd = I["o_w_in"]
    TMP = [Fo["T%d" % i] for i in range(16)]

    def SCAN(o_, d0, d1, R, W):
        P.op("dve", lambda e: e.tensor_tensor_scan(out=o_, data0=d0, data1=d1, initial=0.0, op0=ALU.mult, op1=ALU.add), R, W)

    def odd_mixer(N, S):
        C = min(64, N)
        nch = N // C
        W2 = 2 * C
        MK = m64 if C == 64 else m16
        su, sl, ui, nsu = MK[:W2, 0, :], MK[:W2, 1, :], MK[:W2, 2, :], MK[:W2, 3, :]
        T = TMP
        convert(("odd",))
        rmsnorm(N, 4)
        for nm in ("bdA", "bdB", "bdK", "bdR"):
            MEMSET("pool", Bo[nm][:, :], 0.0, [Bo[nm]])
        for nm in ("fV", "fB", "fK", "fA"):
            MEMSET("pool", Fo[nm][:, :], 0.0, [Fo[nm]])

        def proj_fm(col0, bank):
            wb = wib[rot("wib", NWB)]
            DMA("pool", wb[:, :, 0, :], Wd[:, col0:col0 + 128].rearrange("(k p) n -> p k n", p=128), [], [wb])
            for kc in range(8):
                MM(bank[:, :N], wb[:, kc, 0, :], h[:, kc, :N], kc == 0, kc == 7, [wb, h], [bank])

        def v3(buf, pr=slice(0, 128)):
            return buf[pr, 0:N].rearrange("p (j t) -> p j t", t=C)

        def lastcol(buf, pr=slice(0, 128)):
            return v3(buf, pr)[:, :, C - 1:C].to_broadcast([pr.stop - pr.start, nch, C])

        def nb():
            return banks[rot("bkR", 8)]

        for c in range(4):
            bq, bf_, bi = nb(), nb(), nb()
            proj_fm(c * 128, bq)
            proj_fm(512 + c * 128, bf_)
            proj_fm(1024 + c * 128, bi)
            bg = nb()
            proj_fm(1536 + c * 128, bg)
            ACT(T[0][:, :N], bf_[:, :N], AF.Sigmoid, [bf_], [T[0]])
            TS("dve", T[0][:, :N], T[0][:, :N], oml[:, c:c + 1], lb[:, c:c + 1], ALU.mult, ALU.add, [T[0], oml, lb], [T[0]])
            TS("dve", T[1][:, :N], T[0][:, :N], -1.0, 1.0, ALU.mult, ALU.add, [T[0]], [T[1]])
            ACT(T[2][:, :N], T[0][:, :N], AF.Ln, [T[0]], [T[2]])
            for j in range(nch):
                SCAN(T[3][:, j * C:(j + 1) * C], ones_row[:, :C], T[2][:, j * C:(j + 1) * C], [ones_row, T[2]], [T[3]])
            ACT(T[4][:, :N], T[3][:, :N], AF.Exp, [T[3]], [T[4]])
            ACT(T[5][:, :N], T[3][:, :N], AF.Exp, [T[3]], [T[5]], scale=-1.0)
            TT("dve", Bo["b0"][:, :N], bq[:, :N], T[4][:, :N], ALU.mult, [bq, T[4]], [Bo["b0"]])
            TT("dve", T[1][:, :N], T[1][:, :N], T[5][:, :N], ALU.mult, [T[1], T[5]], [T[1]])
            CP("act", Bo["b1"][:, :N], T[1][:, :N], [T[1]], [Bo["b1"]])
            TT("dve", v3(T[6]), v3(T[1]), lastcol(T[4]), ALU.mult, [T[1], T[4]], [T[6]])
            CP("act", T[7][:, :N], bi[:, :N], [bi], [T[7]])
            ACT(T[10][:, :N], bg[:, :N], AF.Silu, [bg], [T[10]])
            HB = [Bo["q%d" % i] for i in range(32)]
            for j in range(nch):
                cs = slice(j * C, (j + 1) * C)
                vtm, kdtm = HB[j], HB[8 + j]
                t0_, t1_ = nb(), nb()
                TR(t0_[:C, :128], T[7][:, cs], ident[:, :], [T[7], ident], [t0_])
                CP("act", vtm[:C, :], t0_[:C, :128], [t0_], [vtm])
                TR(t1_[:C, :128], T[6][:, cs], ident[:, :], [T[6], ident], [t1_])
                CP("dve", kdtm[:C, :], t1_[:C, :128], [t1_], [kdtm])
            for j in range(nch):
                cs = slice(j * C, (j + 1) * C)
                Am = HB[16 + j]
                t2_ = nb()
                MM(t2_[:C, :C], Bo["b1"][:, cs], Bo["b0"][:, cs], True, True, [Bo["b1"], Bo["b0"]], [t2_])
                TT("dve", Am[:C, :C], t2_[:C, :C], ui[:C, :C], ALU.mult, [t2_, MK], [Am])
            Sb = [None] * (nch + 1)
            for j in range(nch):
                vtm, kdtm = HB[j], HB[8 + j]
                t4_ = nb()
                MM(t4_[:, :128], kdtm[:C, :], vtm[:C, :], True, True, [kdtm, vtm], [t4_])
                if j == 0:
                    CP("pool", HB[24][:, :], S_hgb[:, c, :], [S_hgb], [HB[24]])
                STT(S_hg[:, c, :], S_hg[:, c, :], T[4][:, (j + 1) * C - 1:(j + 1) * C], t4_[:, :128], ALU.mult, ALU.add,
                    [S_hg, T[4], t4_], [S_hg])
                if j < nch - 1:
                    CP("act", HB[24 + j + 1][:, :], S_hg[:, c, :], [S_hg], [HB[24 + j + 1]])
                else:
                    CP("act", S_hgb[:, c, :], S_hg[:, c, :], [S_hg], [S_hgb])
            for j in range(nch):
                cs = slice(j * C, (j + 1) * C)
                vtm, Am, Sbj = HB[j], HB[16 + j], HB[24 + j]
                t3_ = nb()
                MM(t3_[:, :C], vtm[:C, :], Am[:C, :C], True, False, [vtm, Am], [t3_])
                MM(t3_[:, :C], Sbj[:, :], Bo["b0"][:, cs], False, True, [Sbj, Bo["b0"]], [t3_])
                CP("act" if j % 2 == 0 else "dve", T[8][:, cs], t3_[:, :C], [t3_], [T[8]])
            ACT(T[9][:, :N], T[8][:, :N], AF.Square, [T[8]], [T[9]])
            t5_ = nb()
            MM(t5_[:, :N], ones1[:, :], T[9][:, :N], True, True, [ones1, T[9]], [t5_])
            ACT(T[9][:, :N], t5_[:, :N], AF.Ln, [t5_], [T[9]], scale=1.0 / 128, bias=EPS)
            ACT(T[9][:, :N], T[9][:, :N], AF.Exp, [T[9]], [T[9]], scale=-0.5)
            STT(T[8][:, :N], T[8][:, :N], hgn[:, c:c + 1], T[9][:, :N], ALU.mult, ALU.mult, [T[8], hgn, T[9]], [T[8]])
            TT("dve", Bo["hgT"][:, c, :N], T[8][:, :N], T[10][:, :N], ALU.mult, [T[8], T[10]], [Bo["hgT"]])

        def zmix(ci, dst):
            bk = nb()
            proj_fm(2048 + ci * 128, bk)
            zx = T[15]
            CP("act", zx[:, 1:N + 1], bk[:, :N], [bk], [zx])
            CP("pool", zx[:, 0:1], zprev[:, ci:ci + 1], [zprev], [zx])
            CP("pool", zprev[:, ci:ci + 1], zx[:, N:N + 1], [zx], [zprev])
            TS("dve", dst[:, :N], zx[:, 0:N], mu[:, ci:ci + 1], None, ALU.mult, None, [zx, mu], [dst])
            STT(dst[:, :N], zx[:, 1:N + 1], omu[:, ci:ci + 1], dst[:, :N], ALU.mult, ALU.add, [zx, omu, dst], [dst])

        zmix(12, T[0])
        ACT(Bo["b2"][0:64, :N], T[0][0:64, :N], AF.Tanh, [T[0]], [Bo["b2"]])
        CP("pool", Bo["b2"][64:128, :N], T[0][64:128, :N], [T[0]], [Bo["b2"]])
        zmix(13, T[0])
        ACT(Bo["b3"][:, :N], T[0][:, :N], AF.Sigmoid, [T[0]], [Bo["b3"]])
        nsteps = {64: 5, 16: 3}[C]
        def bdv(buf, hh):
            return buf[hh * 64:(hh + 1) * 64, 0:nch * W2].rearrange("p (j b t) -> p j b t", b=2, t=C)[:, :, hh, :]

        def prep_early(c):
            cc = slice(c * 128, (c + 1) * 128)
            specs = ((c, T[0], T[15]), (4 + c, T[1], T[13]), (8 + c, T[2], T[12]))
            pbanks = []
            for ci, dst, zx in specs:
                bk_ = nb()
                proj_fm(2048 + ci * 128, bk_)
                pbanks.append(bk_)
            tw, ta = nb(), nb()
            MM(tw[:, :N], w2t[0:64, cc], Bo["b2"][0:64, :N], True, True, [w2t, Bo["b2"]], [tw])
            MM(ta[:, :N], a2t[64:128, cc], Bo["b2"][64:128, :N], True, True, [a2t, Bo["b2"]], [ta])
            for (ci, dst, zx), bk_ in zip(specs, pbanks):
                CP("act", zx[:, 1:N + 1], bk_[:, :N], [bk_], [zx])
                CP("pool", zx[:, 0:1], zprev[:, ci:ci + 1], [zprev], [], ) if False else \
                    P.op("dve", (lambda e, o_=zx[:, 0:1], i_=zprev[:, ci:ci + 1]: e.tensor_copy(out=o_, in_=i_)), [zprev], [], pwrites=[zx])
                CP("dve", zprev[:, ci:ci + 1], zx[:, N:N + 1], [zx], [zprev])
            ACT(T[3][:, :N], tw[:, :N], AF.Exp, [tw, nw0], [T[3]], scale=-1.0, bias=nw0[:, c:c + 1])
            ACT(T[3][:, :N], T[3][:, :N], AF.Ln, [T[3]], [T[3]], bias=1.0)
            ACT(T[3][:, :N], T[3][:, :N], AF.Exp, [T[3]], [T[3]], scale=-1.0, bias=-0.5)
            ACT(T[4][:, :N], ta[:, :N], AF.Sigmoid, [ta, a0], [T[4]], bias=a0[:, c:c + 1])
            for ci, dst, zx in specs:
                TS("dve", dst[:, :N], zx[:, 0:N], mu[:, ci:ci + 1], None, ALU.mult, None, [zx, mu], [dst])
                STT(dst[:, :N], zx[:, 1:N + 1], omu[:, ci:ci + 1], dst[:, :N], ALU.mult, ALU.add, [zx, omu, dst], [dst])
            yield
            TS("dve", T[6][:, :N], T[1][:, :N], kkw[:, c:c + 1], None, ALU.mult, None, [T[1], kkw], [T[6]])
            TT("dve", T[10][:, :N], T[6][:, :N], T[6][:, :N], ALU.mult, [T[6]], [T[10]])
            t_ = nb()
            MM(t_[:, :N], blk[:, :], T[10][:, :N], True, True, [blk, T[10]], [t_])
            TS("dve", T[10][:, :N], t_[:, :N], 1e-24, None, ALU.max, None, [t_], [T[10]])
            yield
            ACT(T[10][:, :N], T[10][:, :N], AF.Ln, [T[10]], [T[10]])
            ACT(T[10][:, :N], T[10][:, :N], AF.Exp, [T[10]], [T[10]], scale=-0.5)
            TT("dve", T[6][:, :N], T[6][:, :N], T[10][:, :N], ALU.mult, [T[6], T[10]], [T[6]])
            yield
            TS("dve", T[10][:, :N], T[4][:, :N], kaw[:, c:c + 1], omka[:, c:c + 1], ALU.mult, ALU.add, [T[4], kaw, omka], [T[10]])
            TT("dve", T[1][:, :N], T[1][:, :N], T[10][:, :N], ALU.mult, [T[1], T[10]], [T[1]])
            TT("dve", T[8][:, :N], T[6][:, :N], T[4][:, :N], ALU.mult, [T[6], T[4]], [T[8]])
            yield
            for j in range(nch):
                SCAN(T[10][:, j * C:(j + 1) * C], ones_row[:, :C], T[3][:, j * C:(j + 1) * C], [ones_row, T[3]], [T[10]])
            yield
            ACT(T[12][:, :N], T[10][:, :N], AF.Exp, [T[10]], [T[12]])
            TT("dve", T[13][:, :N], T[3][:, :N], T[10][:, :N], ALU.subtract, [T[3], T[10]], [T[13]])
            ACT(T[13][:, :N], T[13][:, :N], AF.Exp, [T[13]], [T[13]])
            yield

        def prep_late(c):
            cc = slice(c * 128, (c + 1) * 128)
            t_ = nb()
            MM(t_[:, :N], g2t[:, cc], Bo["b3"][:, :N], True, True, [g2t, Bo["b3"]], [t_])
            CP("act", T[5][:, :N], t_[:, :N], [t_], [T[5]])
            STT(T[9][:, :N], T[0][:, :N], rkw[:, c:c + 1], T[1][:, :N], ALU.mult, ALU.mult, [T[0], rkw, T[1]], [T[9]])
            t_ = nb()
            MM(t_[:, :N], blk[:, :], T[9][:, :N], True, True, [blk, T[9]], [t_])
            TT("dve", T[9][:, :N], t_[:, :N], T[2][:, :N], ALU.mult, [t_, T[2]], [T[9]])
            ACT(T[11][:, :N], T[10][:, :N], AF.Exp, [T[10]], [T[11]], scale=-1.0)
            TT("dve", T[14][:, :N], T[1][:, :N], T[12][:, :N], ALU.mult, [T[1], T[12]], [T[14]])
            TT("dve", T[4][:, :N], T[8][:, :N], T[12][:, :N], ALU.mult, [T[8], T[12]], [T[4]])
            for hh in range(2):
                pr = slice(hh * 64, (hh + 1) * 64)
                eng = "dve" if hh == 0 else "pool"
                TT(eng, bdv(Bo["bdA"], hh), v3(T[13], pr), v3(T[6], pr), ALU.mult, [T[13], T[6]], [Bo["bdA"]])
                ceng = "dve" if hh == 0 else "act"
                CP(ceng, bdv(Bo["bdB"], hh), v3(T[4], pr), [T[4]], [Bo["bdB"]])
                CP(ceng, bdv(Bo["bdK"], hh), v3(T[14], pr), [T[14]], [Bo["bdK"]])
                TT(eng, bdv(Bo["bdR"], hh), v3(T[0], pr), v3(T[11], pr), ALU.mult, [T[0], T[11]], [Bo["bdR"]])
                CP(ceng, bdv(Fo["fV"], hh), v3(T[2], pr), [T[2]], [Fo["fV"]])
                TT(eng, bdv(Fo["fA"], hh), v3(T[13], pr), v3(T[6], pr), ALU.mult, [T[13], T[6]], [Fo["fA"]])
                TT(eng, bdv(Fo["fK"], hh), v3(T[14], pr), lastcol(T[11], pr), ALU.mult, [T[14], T[11]], [Fo["fK"]])
                if hh == 0:
                    STT(bdv(Fo["fB"], hh), v3(T[4], pr), -1.0, lastcol(T[11], pr), ALU.mult, ALU.mult, [T[4], T[11]], [Fo["fB"]])
                else:
                    TT("pool", bdv(Fo["fB"], hh), v3(T[4], pr), lastcol(T[11], pr), ALU.mult, [T[4], T[11]], [Fo["fB"]])
                    TS("pool", bdv(Fo["fB"], hh), bdv(Fo["fB"], hh), -1.0, None, ALU.mult, None, [Fo["fB"]], [Fo["fB"]])

        for c in range(4):
            for _ in prep_early(c):
                pass
            prep_late(c)
            nxt = None
            yT = T[7]

            def phaseA(j, M):
                ws = slice(j * W2, (j + 1) * W2)
                A_, B_, K_, R_ = Bo["bdA"][:, ws], Bo["bdB"][:, ws], Bo["bdK"][:, ws], Bo["bdR"][:, ws]
                (N0, Nt0, Na, Nta, P0, Pa, LakT, nMrbT, MrkT, Vbd, nBd, Kd, Atm, Wtm, LV, U0, Rhat, PhiT) = M
                p = nb()
                MM(p[:W2, :W2], B_, A_, True, True, [Bo["bdB"], Bo["bdA"]], [p])
                TT("dve", N0[:W2, :W2], p[:W2, :W2], nsu, ALU.mult, [p, MK], [N0])
                yield
                p = nb()
                MM(p[:W2, :W2], A_, B_, True, True, [Bo["bdA"], Bo["bdB"]], [p])
                STT(Nt0[:W2, :W2], p[:W2, :W2], -1.0, sl, ALU.mult, ALU.mult, [p, MK], [Nt0])
                TT("dve", P0[:W2, :W2], N0[:W2, :W2], ident[:W2, :W2], ALU.add, [N0, ident], [P0])
                yield
                p = nb()
                MM(p[:W2, :W2], K_, A_, True, True, [Bo["bdK"], Bo["bdA"]], [p])
                TT("dve", LakT[:W2, :W2], p[:W2, :W2], su, ALU.mult, [p, MK], [LakT])
                yield
                p = nb()
                MM(p[:W2, :W2], B_, R_, True, True, [Bo["bdB"], Bo["bdR"]], [p])
                STT(nMrbT[:W2, :W2], p[:W2, :W2], -1.0, ui, ALU.mult, ALU.mult, [p, MK], [nMrbT])
                yield
                p = nb()
                MM(p[:W2, :W2], K_, R_, True, True, [Bo["bdK"], Bo["bdR"]], [p])
                TT("dve", MrkT[:W2, :W2], p[:W2, :W2], ui, ALU.mult, [p, MK], [MrkT])
                yield
                for (src, dstm, eng) in ((Fo["fV"], Vbd, "act"), (Fo["fB"], nBd, "dve"), (Fo["fK"], Kd, "act"), (Fo["fA"], Atm, "dve")):
                    p = nb()
                    TR(p[:W2, :128], src[:, ws], ident[:, :], [src, ident], [p])
                    CP(eng, dstm[:W2, :], p[:W2, :128], [p], [dstm])
                    yield
                p = nb()
                MM(p[:W2, :128], LakT[:W2, :W2], Vbd[:W2, :], True, True, [LakT, Vbd], [p])
                CP("act", LV[:W2, :], p[:W2, :128], [p], [LV])
                yield
                Nc, Ntc, Pc = N0, Nt0, P0
                oth = {id(N0): Na, id(Na): N0, id(Nt0): Nta, id(Nta): Nt0, id(P0): Pa, id(Pa): P0}
                for i in range(nsteps):
                    nN, nNt, nP = oth[id(Nc)], oth[id(Ntc)], oth[id(Pc)]
                    q1 = nb()
                    MM(q1[:W2, :W2], Nc[:W2, :W2], Ntc[:W2, :W2], True, True, [Nc, Ntc], [q1])
                    CP("act" if i % 2 == 1 else "dve", nNt[:W2, :W2], q1[:W2, :W2], [q1], [nNt])
                    if i < nsteps - 1:
                        q0 = nb()
                        MM(q0[:W2, :W2], Ntc[:W2, :W2], Nc[:W2, :W2], True, True, [Ntc, Nc], [q0])
                        CP("dve", nN[:W2, :W2], q0[:W2, :W2], [q0], [nN])
                    yield
                    q2 = nb()
                    MM(q2[:W2, :W2], nNt[:W2, :W2], Pc[:W2, :W2], True, False, [nNt, Pc], [q2])
                    MM(q2[:W2, :W2], identb[:W2, :W2], Pc[:W2, :W2], False, True, [identb, Pc], [q2])
                    CP("act" if i % 2 == 0 else "dve", nP[:W2, :W2], q2[:W2, :W2], [q2], [nP])
                    yield
                    Nc, Ntc, Pc = nN, nNt, nP
                Tt = Pc
                p = nb()
                MM(p[:W2, :128], Tt[:W2, :W2], Atm[:W2, :], True, True, [Tt, Atm], [p])
                CP("act", Wtm[:W2, :], p[:W2, :128], [p], [Wtm])
                p = nb()
                MM(p[:W2, :128], Tt[:W2, :W2], LV[:W2, :], True, True, [Tt, LV], [p])
                CP("dve", U0[:W2, :], p[:W2, :128], [p], [U0])
                yield
                p = nb()
                MM(p[:, :W2], Wtm[:W2, :], nMrbT[:W2, :W2], True, False, [Wtm, nMrbT], [p])
                MM(p[:, :W2], identb[:, :], R_, False, True, [identb, Bo["bdR"]], [p])
                CP("dve", Rhat[:, :W2], p[:, :W2], [p], [Rhat])
                p = nb()
                MM(p[:, :128], Wtm[:W2, :], nBd[:W2, :], True, True, [Wtm, nBd], [p])
                CP("act", PhiT[:, :], p[:, :128], [p], [PhiT])
                yield

            def phaseB(j, M):
                (N0, Nt0, Na, Nta, P0, Pa, LakT, nMrbT, MrkT, Vbd, nBd, Kd, Atm, Wtm, LV, U0, Rhat, PhiT) = M
                y0 = nb()
                MM(y0[:, :W2], U0[:W2, :], nMrbT[:W2, :W2], True, False, [U0, nMrbT], [y0])
                MM(y0[:, :W2], Vbd[:W2, :], MrkT[:W2, :W2], False, False, [Vbd, MrkT], [y0])
                MM(y0[:, :W2], STb[:, c, :], Rhat[:, :W2], False, True, [STb, Rhat], [y0])
                s0 = nb()
                MM(s0[:, :128], Kd[:W2, :], Vbd[:W2, :], True, False, [Kd, Vbd], [s0])
                MM(s0[:, :128], nBd[:W2, :], U0[:W2, :], False, False, [nBd, U0], [s0])
                MM(s0[:, :128], PhiT[:, :], STb[:, c, :], False, True, [PhiT, STb], [s0])
                STT(ST[:, c, :], ST[:, c, :], T[11][:, (j + 1) * C - 1:(j + 1) * C], s0[:, :128], ALU.mult, ALU.add,
                    [ST, T[11], s0], [ST])
                CP("act", STb[:, c, :], ST[:, c, :], [ST], [STb])
                CP("act", yT[0:64, j * C:(j + 1) * C], y0[0:64, 0:C], [y0], [yT])
                CP("act", yT[64:128, j * C:(j + 1) * C], y0[64:128, C:W2], [y0], [yT])

            GS = 4
            for g0 in range(0, nch, GS):
                js = list(range(g0, min(nch, g0 + GS)))
                Ms = {j: [Bo["q%d" % (18 * (j - g0) + i)] for i in range(18)] for j in js}
                gens = [phaseA(j, Ms[j]) for j in js]
                while gens:
                    for g_ in list(gens):
                        try:
                            next(g_)
                        except StopIteration:
                            gens.remove(g_)
                    if nxt is not None:
                        try:
                            next(nxt)
                        except StopIteration:
                            nxt = None
                for j in js:
                    phaseB(j, Ms[j])
            if nxt is not None:
                for _ in nxt:
                    pass
            t_ = nb()
            MM(t_[:, :N], blk[:, :], yT[:, :N], True, True, [blk, yT], [t_])
            STT(yT[:, :N], t_[:, :N], -1.0 / 64, yT[:, :N], ALU.mult, ALU.add, [t_, yT], [yT])
            TT("dve", T[14][:, :N], yT[:, :N], yT[:, :N], ALU.mult, [yT], [T[14]])
            t_ = nb()
            MM(t_[:, :N], blk[:, :], T[14][:, :N], True, True, [blk, T[14]], [t_])
            ACT(T[14][:, :N], t_[:, :N], AF.Ln, [t_], [T[14]], scale=1.0 / 64, bias=64e-5)
            ACT(T[14][:, :N], T[14][:, :N], AF.Exp, [T[14]], [T[14]], scale=-0.5)
            TT("dve", yT[:, :N], yT[:, :N], T[14][:, :N], ALU.mult, [yT, T[14]], [yT])
            TS("dve", yT[:, :N], yT[:, :N], lng[:, c:c + 1], lnb[:, c:c + 1], ALU.mult, ALU.add, [yT, lng, lnb], [yT])
            TT("dve", yT[:, :N], yT[:, :N], T[9][:, :N], ALU.add, [yT, T[9]], [yT])
            TT("dve", Bo["rwT"][:, c, :N], yT[:, :N], T[5][:, :N], ALU.mult, [yT, T[5]], [Bo["rwT"]])
        Wo = I["o_w_out"]
        for half in range(2):
            wt = wtm[rot("wtm", 2)]
            DMA("pool", wt[:, :, :], Wo[:, half * 512:(half + 1) * 512].rearrange("(k p) n -> p k n", p=128), [], [wt])
            for m in range(4):
                bk = banks[rot("bk4", 4)]
                for kc in range(8):
                    rhs = Bo["hgT"][:, kc, :N] if kc < 4 else Bo["rwT"][:, kc - 4, :N]
                    MM(bk[:, :N], wt[:, kc, m * 128:(m + 1) * 128], rhs, kc == 0, kc == 7, [wt, Bo["hgT"], Bo["rwT"]], [bk])
                cc_ = half * 4 + m
                TT("dve", x[:, cc_, :N], x[:, cc_, :N], bk[:, :N], ALU.add, [x, bk], [x])

    def odd_init(S, sample):
        if sample:
            DMA("sp", S_hg[:, :, :], I["state_hgrn_S"][:, :, :].rearrange("h k v -> k h v"), [], [S_hg])
            MEMSET("dve", ST[:, :, :], 0.0, [ST])
            for hd in range(8):
                pp, hh = hd // 2, hd % 2
                DMA("sp", ST[hh * 64:(hh + 1) * 64, pp, hh * 64:(hh + 1) * 64], I["state_rwkv_S"][hd].rearrange("v k -> k v"),
                    [], [ST], slow=True)
            DMA("sp", zprev[:, :], I["state_rwkv_shift"][:].rearrange("(c p) -> p c", p=128), [], [zprev], slow=True)
        else:
            MEMSET("dve", S_hg[:, :, :], 0.0, [S_hg])
            MEMSET("dve", ST[:, :, :], 0.0, [ST])
            MEMSET("dve", zprev[:, :], 0.0, [zprev])
        CP("pool", S_hgb[:, :, :], S_hg[:, :, :], [S_hg], [S_hgb])
        CP("pool", STb[:, :, :], ST[:, :, :], [ST], [STb])

    def odd_finish(S):
        DMA("sp", S["hgrn_S"][:, :, :].rearrange("h k v -> k h v"), S_hg[:, :, :], [S_hg], [])
        for hd in range(8):
            pp, hh = hd // 2, hd % 2
            DMA("sp", S["rwkv_S"][hd].rearrange("v k -> k v"), ST[hh * 64:(hh + 1) * 64, pp, hh * 64:(hh + 1) * 64],
                [ST], [], slow=True)
        DMA("sp", S["rwkv_shift"][:].rearrange("(c p) -> p c", p=128), zprev[:, :], [zprev], [], slow=True)

    def tile(S, src, dst, t0, N):
        S["t0"] = t0
        load_x(src, t0, N)
        for layer in range(nlayers):
            P.barrier(grp_all, dummy)
            ffn(N, layer, 0)
            P.barrier(grp_all, dummy)
            if layer == 0:
                MEMSET("pool", B["Vt"][:, :, :, 64:66], 1.0, [B["Vt"]])
                init_heads()
                even_mixer(N, S)
            else:
                odd_mixer(N, S)
            P.barrier(grp_all, dummy)
            ffn(N, layer, 1)
        store_x(dst, t0, N)

    S = {"fox_k": O["fox_k_p"], "fox_v": O["fox_v_p"], "fox_lf": O["fox_logf_p"], "lru_conv": O["lru_conv_p"],
         "lru_h": O["lru_h_p"], "key_base": 0, "hgrn_S": O["hgrn_S_p"], "rwkv_S": O["rwkv_S_p"],
         "rwkv_shift": O["rwkv_shift_p"]}
    even_init(S, False)
    odd_init(S, False)
    for t in range(SEQ // NT):
        S["key_base"] = t * NT
        tile(S, I["x_prompt"], O["y_prompt"], t * NT, NT)
    even_finish(S)
    odd_finish(S)
    if do_sample:
        S = {"fox_k": O["fox_k_s"], "fox_v": O["fox_v_s"], "fox_lf": O["fox_logf_s"], "lru_conv": O["lru_conv_s"],
             "lru_h": O["lru_h_s"], "key_base": PAST, "hgrn_S": O["hgrn_S_s"], "rwkv_S": O["rwkv_S_s"],
             "rwkv_shift": O["rwkv_shift_s"]}
        P.barrier(grp_all, dummy)
        even_init(S, True)
        odd_init(S, True)
        init_heads()
        ingest_past(PAST)
        tile(S, I["x_sample"], O["y_sample"], 0, DSEQ)
        even_finish(S)
        odd_finish(S)

    P.finish()
    stack.close()
    return nc


def consts():
    k = np.arange(128)
    q = np.arange(512)
    mask = np.zeros((128, 4, 512), np.float32)
    for kk in range(4):
        mask[:, kk, :] = np.where((kk * 128 + k)[:, None] <= q[None, :], 0.0, -30000.0)
    def bdmask(C, fn):
        m = np.zeros((2 * C, 2 * C), np.float32)
        i = np.arange(C)
        blkm = fn(i[:, None], i[None, :]).astype(np.float32)
        m[:C, :C] = blkm
        m[C:, C:] = blkm
        return m
    def m3(C):
        return np.stack([bdmask(C, lambda j, t: t > j), bdmask(C, lambda t, j: t > j), bdmask(C, lambda j, t: t >= j),
                         -bdmask(C, lambda j, t: t > j)], axis=1)
    blk = np.zeros((128, 128), np.float32)
    blk[:64, :64] = 1.0
    blk[64:, 64:] = 1.0
    return {"c_m64": m3(64), "c_m16": m3(16), "c_blk": blk,
            "c_ones": np.full((128, 128), 1.0 / 1024, np.float32),
            "c_ident": np.eye(128, dtype=np.float32),
            "c_utri": np.triu(np.ones((128, 128), np.float32)),
            "c_mask": mask}


OUT_NAMES = ["y_prompt", "y_sample", "lru_conv_p", "lru_conv_s", "lru_h_p", "lru_h_s", "fox_k_p", "fox_k_s", "fox_v_p",
             "fox_v_s", "fox_logf_p", "fox_logf_s", "hgrn_S_p", "hgrn_S_s", "rwkv_shift_p", "rwkv_shift_s", "rwkv_S_p",
             "rwkv_S_s"]


def percore_inputs(inp, b):
    f = lambda a: np.ascontiguousarray(np.asarray(a), dtype=np.float32)
    m = {"x_prompt": f(inp["x_prompt"][b]), "x_sample": f(inp["x_sample"][b]),
         "state_lru_conv": f(inp["state_lru_conv"][0, b]), "state_lru_h": f(inp["state_lru_h"][0, b]),
         "cache_fox_k": f(inp["cache_fox_k"][0, b]).reshape(-1, G), "cache_fox_v": f(inp["cache_fox_v"][0, b]).reshape(-1, G),
         "cache_fox_logf": f(inp["cache_fox_logf"][0, b]), "state_hgrn_S": f(inp["state_hgrn_S"][0, b]),
         "state_rwkv_shift": f(inp["state_rwkv_shift"][0, b]), "state_rwkv_S": f(inp["state_rwkv_S"][0, b]),
         "norm_g": f(inp["norm_g"]), "ffn_w_in": f(inp["ffn_w_in"]), "ffn_w_out": f(inp["ffn_w_out"]),
         "hg_lb_logits": f(inp["hg_lb_logits"]), "rw_rk": f(inp["rw_rk"][0]).reshape(-1)}
    for k in ("e_w_in", "e_w_out", "lru_conv_w", "lru_conv_b", "lru_wa", "lru_ba", "lru_wx", "lru_bx", "lru_lambda",
              "fox_q_gain", "fox_k_gain", "fox_f_bias", "o_w_in", "o_w_out", "hg_norm_g", "rw_mu", "rw_w0", "rw_w2",
              "rw_a0", "rw_a2", "rw_g2", "rw_kk", "rw_ka", "rw_ln_g", "rw_ln_b"):
        m[k] = f(inp[k][0])
    m.update(consts())
    return m


_NC_CACHE = {}


def kernel(**inputs):
    SEQ = inputs["x_prompt"].shape[1]
    NB = inputs["x_prompt"].shape[0]
    if SEQ not in _NC_CACHE:
        _NC_CACHE[SEQ] = build(SEQ)
    nc = _NC_CACHE[SEQ]
    in_maps = [percore_inputs(inputs, b) for b in range(NB)]
    res = run_bass_kernel_spmd(nc, in_maps, core_ids=list(range(NB))).results
    outs = []
    for nm in OUT_NAMES:
        a = np.stack([np.asarray(r[nm], dtype=np.float32) for r in res], axis=0)
        if nm.startswith("y_"):
            outs.append(a)
        elif nm.startswith("fox_k") or nm.startswith("fox_v"):
            outs.append(a.reshape(1, NB, a.shape[1], 8, 64))
        else:
            outs.append(a.reshape((1, NB) + a.shape[1:]))
    return tuple(outs)
```

```python
import contextlib
import numpy as np
import concourse.bass as bass
import concourse.mybir as mybir
from concourse.bass_utils import run_bass_kernel_spmd

F32 = mybir.dt.float32
BF16 = mybir.dt.bfloat16
AF = mybir.ActivationFunctionType
ALU = mybir.AluOpType
AX = mybir.AxisListType

D = 1024
DFF = 2816
NJ = DFF // 128
G = 512
ECOLS = 3080
RWC = 1792
OCOLS = 4 * G + RWC
EPS = 1e-6


class Buf:
    def __init__(self, t, name):
        self.t = t
        self.name = name
        self.lw = None
        self.pw = []
        self.rd = {}
        self.rdd = []
        self.psum = False

    def __getitem__(self, k):
        return self.t[k]

    def ap(self):
        return self.t


class Rec:
    __slots__ = ("eng", "fn", "dma", "deps", "ref", "val", "sem")

    def __init__(self, eng, fn, dma):
        self.eng = eng
        self.fn = fn
        self.dma = dma
        self.deps = set()
        self.ref = False
        self.val = None
        self.sem = None


ENGS = ["pe", "act", "dve", "pool", "sp"]
EPOCH = 16000
NDSEM = 8


class Prog:
    def __init__(self, nc, stack):
        self.nc = nc
        self.stack = stack
        self.ins = {e: [] for e in ENGS}
        self.nbuf = 0

    def sb(self, name, shape, dt):
        t = self.stack.enter_context(self.nc.sbuf_tensor(name, list(shape), dt))
        return Buf(t, name)

    def ps(self, name, shape, dt=F32):
        t = self.stack.enter_context(self.nc.psum_tensor(name, list(shape), dt))
        b = Buf(t, name)
        b.psum = True
        return b

    def dram(self, name, shape, dt, kind):
        t = self.nc.dram_tensor(name, list(shape), dt, kind=kind)
        return Buf(t.ap(), name)

    def view(self, arena, name, off, shape):
        n = 1
        for d in shape[1:]:
            n *= d
        ap = arena.t[0:shape[0], off:off + n]
        if len(shape) == 3:
            ap = ap.rearrange("p (a b) -> p a b", b=shape[2])
        elif len(shape) == 4:
            ap = ap.rearrange("p (a b c) -> p a b c", b=shape[2], c=shape[3])
        return Buf(ap, name)

    def barrier(self, bufs, dummy):
        self.op("dve", lambda e: e.memset(dummy[0:1, 0:1], 0.0), [], list(bufs) + [dummy])

    def op(self, eng, fn, reads=(), writes=(), dma=False, pwrites=()):
        rec = Rec(eng, fn, dma)
        for b in reads:
            if b.lw is not None:
                rec.deps.add(b.lw)
            for tk in b.pw:
                rec.deps.add(tk)
            if b.psum:
                for e2, i2 in b.rd.items():
                    if e2 != eng:
                        rec.deps.add((e2, i2))
        for b in writes:
            if b.lw is not None:
                rec.deps.add(b.lw)
            for tk in b.pw:
                rec.deps.add(tk)
            for e2, i2 in b.rd.items():
                rec.deps.add((e2, i2))
            for tk in b.rdd:
                rec.deps.add(tk)
        for b in pwrites:
            if b.lw is not None:
                rec.deps.add(b.lw)
            for e2, i2 in b.rd.items():
                rec.deps.add((e2, i2))
            for tk in b.rdd:
                rec.deps.add(tk)
        idx = len(self.ins[eng])
        self.ins[eng].append(rec)
        tok = (eng, idx)
        for b in writes:
            b.lw = tok
            b.pw = []
            b.rd = {}
            b.rdd = []
        for b in pwrites:
            b.pw.append(tok)
        for b in reads:
            if dma:
                b.rdd.append(tok)
            else:
                if b.rd.get(eng, -1) < idx:
                    b.rd[eng] = idx
        return tok

    def finish(self):
        nc = self.nc
        ins = self.ins
        for e in ENGS:
            for i, r in enumerate(ins[e]):
                nd = set()
                for (e2, i2) in r.deps:
                    if e2 == e and i2 == i:
                        continue
                    r2 = ins[e2][i2]
                    if e2 == e and e == "pe":
                        continue
                    nd.add((e2, i2))
                    r2.ref = True
                r.deps = nd
        esems = {e: [] for e in ENGS}
        dsems = {}
        for e in ENGS:
            cnt = 0
            nd = 0
            for r in ins[e]:
                if r.dma:
                    k = nd % NDSEM
                    r.sem = ("d", e, k)
                    r.val = 16 * (nd // NDSEM + 1)
                    nd += 1
                elif r.ref:
                    ep = cnt // EPOCH
                    r.sem = ("e", e, ep)
                    r.val = cnt % EPOCH + 1
                    cnt += 1
            nep = (cnt + EPOCH - 1) // EPOCH
            for ep in range(max(nep, 1)):
                esems[e].append(self.stack.enter_context(nc.semaphore(f"s_{e}_{ep}")))
            if nd:
                dsems[e] = [self.stack.enter_context(nc.semaphore(f"d_{e}_{k}")) for k in range(NDSEM)]

        def semh(key):
            if key[0] == "e":
                return esems[key[1]][key[2]]
            return dsems[key[1]][key[2]]

        final_waits = []
        for e in ("sp", "pool"):
            if e in dsems:
                last = {}
                for r in ins[e]:
                    if r.dma:
                        last[r.sem] = r.val
                final_waits += list(last.items())

        block = self.stack.enter_context(nc.Block())

        def run(e, eng):
            waited = {}
            for r in ins[e]:
                need = {}
                for (e2, i2) in r.deps:
                    r2 = ins[e2][i2]
                    if need.get(r2.sem, 0) < r2.val:
                        need[r2.sem] = r2.val
                if r.dma and r.val > 16:
                    if need.get(r.sem, 0) < r.val - 16:
                        need[r.sem] = r.val - 16
                for sk, v in need.items():
                    if waited.get(sk, 0) < v:
                        eng.wait_ge(semh(sk), v)
                        waited[sk] = v
                i = r.fn(eng)
                if r.dma:
                    i.then_inc(semh(r.sem), 16)
                elif r.ref:
                    i.then_inc(semh(r.sem), 1)
            if e == "sp":
                for sk, v in final_waits:
                    if waited.get(sk, 0) < v:
                        eng.wait_ge(semh(sk), v)

        @block.tensor
        def _(eng):
            run("pe", eng)

        @block.scalar
        def _(eng):
            run("act", eng)

        @block.vector
        def _(eng):
            run("dve", eng)

        @block.gpsimd
        def _(eng):
            run("pool", eng)

        @block.sync
        def _(eng):
            run("sp", eng)


def cdiv(a, b):
    return (a + b - 1) // b


def build(SEQ, DSEQ=16, PAST=2048, NT=512, do_sample=True, nlayers=2, stage=99, sub=99, vmode=0):
    nc = bass.Bass("TRN2", target_bir_lowering=False)
    stack = contextlib.ExitStack()
    P = Prog(nc, stack)
    TK = max(SEQ, PAST + 128)
    KTMAX = TK // 128

    def din(name, shape):
        return P.dram(name, shape, F32, "ExternalInput")

    def dout(name, shape):
        return P.dram(name, shape, F32, "ExternalOutput")

    I = {}
    for nm, sh in [("x_prompt", [SEQ, D]), ("x_sample", [DSEQ, D]), ("state_lru_conv", [3, G]), ("state_lru_h", [G]),
                   ("cache_fox_k", [PAST, G]), ("cache_fox_v", [PAST, G]), ("cache_fox_logf", [PAST, 8]),
                   ("state_hgrn_S", [4, 128, 128]), ("state_rwkv_shift", [RWC]), ("state_rwkv_S", [8, 64, 64]),
                   ("norm_g", [2, 3, D]), ("ffn_w_in", [2, 2, D, 2 * DFF]), ("ffn_w_out", [2, 2, DFF, D]),
                   ("e_w_in", [D, ECOLS]), ("e_w_out", [D, D]), ("lru_conv_w", [4, G]), ("lru_conv_b", [G]),
                   ("lru_wa", [8, 64, 64]), ("lru_ba", [G]), ("lru_wx", [8, 64, 64]), ("lru_bx", [G]),
                   ("lru_lambda", [G]), ("fox_q_gain", [64]), ("fox_k_gain", [64]), ("fox_f_bias", [8]),
                   ("o_w_in", [D, OCOLS]), ("o_w_out", [D, D]), ("hg_lb_logits", [2, G]), ("hg_norm_g", [G]),
                   ("rw_mu", [RWC]), ("rw_w0", [G]), ("rw_w2", [64, G]), ("rw_a0", [G]), ("rw_a2", [64, G]),
                   ("rw_g2", [128, G]), ("rw_kk", [G]), ("rw_ka", [G]), ("rw_rk", [G]), ("rw_ln_g", [G]),
                   ("rw_ln_b", [G]),
                   ("c_ones", [128, 128]), ("c_ident", [128, 128]), ("c_utri", [128, 128]), ("c_mask", [128, 4, 512]), ("c_m64", [128, 4, 128]), ("c_m16", [32, 4, 32]),
                   ("c_blk", [128, 128])]:
        I[nm] = din(nm, sh)
    O = {}
    for nm, sh in [("y_prompt", [SEQ, D]), ("y_sample", [DSEQ, D]), ("lru_conv_p", [3, G]), ("lru_conv_s", [3, G]),
                   ("lru_h_p", [G]), ("lru_h_s", [G]), ("fox_k_p", [SEQ, G]), ("fox_k_s", [DSEQ, G]),
                   ("fox_v_p", [SEQ, G]), ("fox_v_s", [DSEQ, G]), ("fox_logf_p", [SEQ, 8]), ("fox_logf_s", [DSEQ, 8]),
                   ("hgrn_S_p", [4, 128, 128]), ("hgrn_S_s", [4, 128, 128]), ("rwkv_shift_p", [RWC]),
                   ("rwkv_shift_s", [RWC]), ("rwkv_S_p", [8, 64, 64]), ("rwkv_S_s", [8, 64, 64])]:
        O[nm] = dout(nm, sh)
    KT_scr = P.dram("kt_scr", [8, 128, TK], BF16, "Internal")
    VW = 72
    V_scr = P.dram("v_scr", [8, 128, KTMAX, VW], BF16, "Internal")

    def MM(o, l, r, st, sp, R, W):
        P.op("pe", lambda e: e.matmul(o, lhsT=l, rhs=r, start=st, stop=sp), R, W)

    def TR(o, i, idn, R, W):
        P.op("pe", lambda e: e.transpose(o, i, idn), R, W)

    def ACT(o, i, f, R, W, bias=None, scale=None):
        kw = {}
        if bias is not None:
            kw["bias"] = bias
        if scale is not None:
            kw["scale"] = scale
        P.op("act", lambda e: e.activation(out=o, in_=i, func=f, **kw), R, W)

    def TT(eng, o, a, b, op, R, W):
        P.op(eng, lambda e: e.tensor_tensor(out=o, in0=a, in1=b, op=op), R, W)

    def TS(eng, o, a, s1, s2, op0, op1, R, W):
        if s2 is None:
            P.op(eng, lambda e: e.tensor_scalar(out=o, in0=a, scalar1=s1, scalar2=None, op0=op0), R, W)
        else:
            P.op(eng, lambda e: e.tensor_scalar(out=o, in0=a, scalar1=s1, scalar2=s2, op0=op0, op1=op1), R, W)

    def STT(o, a, sc, b, op0, op1, R, W):
        P.op("dve", lambda e: e.scalar_tensor_tensor(out=o, in0=a, scalar=sc, in1=b, op0=op0, op1=op1), R, W)

    def CP(eng, o, i, R, W):
        if eng == "act":
            P.op("act", lambda e: e.copy(out=o, in_=i), R, W)
        else:
            P.op(eng, lambda e: e.tensor_copy(out=o, in_=i), R, W)

    def DMA(q, o, i, R, W, slow=False, PW=()):
        if slow:
            P.op(q, lambda e: e.dma_start(out=o, in_=i, allow_slow_non_contiguous=True), R, W, dma=True, pwrites=PW)
        else:
            P.op(q, lambda e: e.dma_start(out=o, in_=i), R, W, dma=True, pwrites=PW)

    def RECIP(o, i, R, W):
        P.op("dve", lambda e: e.reciprocal(out=o, in_=i), R, W)

    def MEMSET(eng, o, v, W):
        P.op(eng, lambda e: e.memset(o, v), [], W)

    cnt = {}

    def rot(key, n):
        v = cnt.get(key, 0)
        cnt[key] = v + 1
        return v % n

    ones_f = P.sb("ones_f", [128, 128], F32)
    ones_fb = P.sb("ones_fb", [128, 128], BF16)
    ones1 = P.sb("ones1", [128, 128], F32)
    ident = P.sb("ident", [128, 128], F32)
    identb = P.sb("identb", [128, 128], BF16)
    utri = P.sb("utri", [128, 128], F32)
    maskb = P.sb("maskb", [128, 4, 512], BF16)
    ones3 = P.sb("ones3", [3, 128], BF16)
    normg = P.sb("normg", [128, 6, 8], F32)
    dummy = P.sb("dummy_bar", [128, 8], F32)
    DMA("sp", ones_f[:], I["c_ones"][:], [], [ones_f])
    DMA("pool", ones_fb[:], I["c_ones"][:], [], [ones_fb])
    DMA("sp", ident[:], I["c_ident"][:], [], [ident])
    DMA("sp", utri[:], I["c_utri"][:], [], [utri])
    DMA("pool", identb[:], I["c_ident"][:], [], [identb])
    DMA("pool", maskb[:], I["c_mask"][:], [], [maskb])
    MEMSET("dve", ones3[:], 1.0, [ones3])
    MEMSET("dve", ones1[:], 1.0, [ones1])
    DMA("sp", normg[:], I["norm_g"][:].rearrange("l w (c p) -> p (l w) c", p=128), [], [normg], slow=True)

    def colvec(name, src, n):
        t = P.sb(name, [128, n], F32)
        DMA("sp", t[:], src.rearrange("(c p) -> p c", p=128), [], [t], slow=True)
        return t

    convb = colvec("convb", I["lru_conv_b"][:], 4)
    lba = colvec("lba", I["lru_ba"][:], 4)
    lbx = colvec("lbx", I["lru_bx"][:], 4)
    lam = colvec("lam", I["lru_lambda"][:], 4)
    convw = P.sb("convw", [128, 4, 4], F32)
    for j in range(4):
        DMA("sp", convw[:, :, j], I["lru_conv_w"][j, :].rearrange("(c p) -> p c", p=128), [], [convw], slow=True)
    c1 = P.sb("c1", [128, 4], F32)
    c2 = P.sb("c2", [128, 4], F32)
    ACT(c1[:], lam[:], AF.Exp, [lam], [c1], scale=-1.0)
    ACT(c1[:], c1[:], AF.Ln, [c1], [c1], bias=1.0)
    TS("dve", c2[:], c1[:], -16.0, None, ALU.mult, None, [c1], [c2])
    TS("dve", c1[:], c1[:], -8.0, None, ALU.mult, None, [c1], [c1])
    bda = P.sb("bda", [128, 4, 128], BF16)
    bdx = P.sb("bdx", [128, 4, 128], BF16)
    for bd, src in ((bda, I["lru_wa"]), (bdx, I["lru_wx"])):
        MEMSET("pool", bd[:], 0.0, [bd])
        for c in range(4):
            DMA("pool", bd[0:64, c, 0:64], src[2 * c], [], [bd])
            DMA("pool", bd[64:128, c, 64:128], src[2 * c + 1], [], [bd])
    gq = P.sb("gq", [128, 64], F32)
    gk = P.sb("gk", [128, 64], F32)
    fbias = P.sb("fbias", [128, 8], F32)
    DMA("sp", gq[:], I["fox_q_gain"][:].partition_broadcast(128), [], [gq])
    DMA("sp", gk[:], I["fox_k_gain"][:].partition_broadcast(128), [], [gk])
    DMA("sp", fbias[:], I["fox_f_bias"][:].partition_broadcast(128), [], [fbias])
    TS("dve", gq[:], gq[:], 0.125, None, ALU.mult, None, [gq], [gq])
    wfl = P.sb("wfl", [128, 8, 8], BF16)
    DMA("pool", wfl[:], I["e_w_in"][:, 3072:3080].rearrange("(k p) n -> p k n", p=128), [], [wfl], slow=True)


    m64 = P.sb("m64", [128, 4, 128], F32)
    m16 = P.sb("m16", [32, 4, 32], F32)
    blk = P.sb("blk", [128, 128], F32)
    DMA("sp", m64[:], I["c_m64"][:], [], [m64])
    DMA("sp", m16[:], I["c_m16"][:], [], [m16])
    DMA("sp", blk[:], I["c_blk"][:], [], [blk])
    ones_row = P.sb("ones_row", [128, 64], F32)
    MEMSET("dve", ones_row[:], 1.0, [ones_row])
    lb0 = colvec("lb0", I["hg_lb_logits"][0, :], 4)
    lb = colvec("lb", I["hg_lb_logits"][1, :], 4)
    oml = P.sb("oml", [128, 4], F32)
    TT("dve", lb[:], lb[:], lb0[:], ALU.subtract, [lb, lb0], [lb])
    ACT(lb[:], lb[:], AF.Sigmoid, [lb], [lb])
    TS("dve", oml[:], lb[:], -1.0, 1.0, ALU.mult, ALU.add, [lb], [oml])
    hgn = colvec("hgn", I["hg_norm_g"][:], 4)
    mu = colvec("mu", I["rw_mu"][:], 14)
    omu = P.sb("omu", [128, 14], F32)
    TS("dve", omu[:], mu[:], -1.0, 1.0, ALU.mult, ALU.add, [mu], [omu])
    nw0 = colvec("nw0", I["rw_w0"][:], 4)
    TS("dve", nw0[:], nw0[:], -1.0, None, ALU.mult, None, [nw0], [nw0])
    a0 = colvec("a0", I["rw_a0"][:], 4)
    kkw = colvec("kkw", I["rw_kk"][:], 4)
    kaw = colvec("kaw", I["rw_ka"][:], 4)
    omka = P.sb("omka", [128, 4], F32)
    TS("dve", omka[:], kaw[:], -1.0, 1.0, ALU.mult, ALU.add, [kaw], [omka])
    rkw = colvec("rkw", I["rw_rk"][:], 4)
    lng = colvec("lng", I["rw_ln_g"][:], 4)
    lnb = colvec("lnb", I["rw_ln_b"][:], 4)
    w2t = P.sb("w2t", [64, G], BF16)
    a2t = P.sb("a2t", [128, G], BF16)
    g2t = P.sb("g2t", [128, G], BF16)
    DMA("pool", w2t[:, :], I["rw_w2"][:, :], [], [w2t])
    DMA("pool", a2t[64:128, :], I["rw_a2"][:, :], [], [a2t])
    DMA("pool", g2t[:, :], I["rw_g2"][:, :], [], [g2t])
    S_hg = P.sb("S_hg", [128, 4, 128], F32)
    S_hgb = P.sb("S_hgb", [128, 4, 128], BF16)
    ST = P.sb("ST_rw", [128, 4, 128], F32)
    STb = P.sb("ST_rwb", [128, 4, 128], BF16)
    zprev = P.sb("zprev", [128, 14], F32)


    WQ = "sp"
    ffi_t = nc.dram_tensor("ffn_in_b", [2, 2, NJ, 128, 2048], BF16, kind="Internal").ap()
    ffo_t = nc.dram_tensor("ffn_out_b", [2, 2, 2, NJ // 2, 128, 1024], BF16, kind="Internal").ap()
    ffi_b, ffo_b = {}, {}
    mixw = {}
    conv_done = set()
    mix_t = {}
    for nm, ncol in (("e_w_in", 3072), ("o_w_in", OCOLS)):
        mix_t[nm] = nc.dram_tensor(nm + "_b", [ncol // 128, 128, 1024], BF16, kind="Internal").ap()
    for nm, c0s in (("e_w_in", (1024, 1536, 2048, 2560)), ("e_w_out", (0, 512)), ("o_w_out", (0, 512))):
        mix_t[nm + "_t"] = nc.dram_tensor(nm + "_tb", [len(c0s), 128, 4096], BF16, kind="Internal").ap()

    def convert(key):
        if key in conv_done:
            return
        conv_done.add(key)
        if key[0] == "ffn":
            _, l, w = key
            for j in range(NJ):
                bb = Buf(ffi_t[l, w, j], "ffi")
                ffi_b[(l, w, j)] = bb
                for gu in range(2):
                    c0 = gu * DFF + j * 128
                    DMA("pool", bb[:, :].rearrange("p (c g n) -> p c g n", g=2, n=128)[:, :, gu, :],
                        I["ffn_w_in"][l, w, :, c0:c0 + 128].rearrange("(c p) n -> p c n", p=128), [], [], PW=[bb])
            for half in range(2):
                for j2 in range(NJ // 2):
                    bb = Buf(ffo_t[l, w, half, j2], "ffo")
                    ffo_b[(l, w, half, j2)] = bb
                    DMA("pool", bb[:, :].rearrange("p (a n) -> p a n", a=2),
                        I["ffn_w_out"][l, w, j2 * 256:(j2 + 1) * 256, half * 512:(half + 1) * 512].rearrange("(a p) n -> p a n", p=128),
                        [], [], PW=[bb])
        else:
            for nm, ncol in ((("e_w_in", 3072),) if key[0] == "even" else (("o_w_in", OCOLS),)):
                for c in range(ncol // 128):
                    bb = Buf(mix_t[nm][c], nm + "_b")
                    mixw[(nm, c)] = bb
                    DMA("pool", bb[:, :].rearrange("p (k n) -> p k n", n=128),
                        I[nm][:, c * 128:(c + 1) * 128].rearrange("(k p) n -> p k n", p=128), [], [], PW=[bb])
            groups = (("e_w_in", (1024, 1536, 2048, 2560)), ("e_w_out", (0, 512))) if key[0] == "even" else (("o_w_out", (0, 512)),)
            for nm, c0s in groups:
                for gi_, c0 in enumerate(c0s):
                    bb = Buf(mix_t[nm + "_t"][gi_], nm + "_tb")
                    mixw[(nm + "_t", c0)] = bb
                    DMA("pool", bb[:, :].rearrange("p (k n) -> p k n", n=512),
                        I[nm][:, c0:c0 + 512].rearrange("(k p) n -> p k n", p=128), [], [], PW=[bb])

    x = P.sb("x", [128, 8, NT], F32)
    h = P.sb("h", [128, 8, NT], BF16)
    sqt = [P.sb(f"sqt{i}", [128, NT], BF16) for i in range(4)]
    rstd = P.sb("rstd", [128, NT], F32)
    sg = [P.sb(f"sg{i}", [128, NT], F32) for i in range(2)]
    NWB = 3
    NWI = 4
    wib = [P.sb(f"wib{i}", [128, 8, 2, 128], BF16) for i in range(NWI)]
    wob = [P.sb(f"wob{i}", [128, 2, 512], BF16) for i in range(NWB)]
    wtm = [P.sb(f"wtm{i}", [128, 8, 512], BF16) for i in range(2)]
    xtms = [P.sb(f"xtm{i}", [128, D], F32) for i in range(2)]
    banks = [P.ps(f"bank{i}", [128, 512], F32) for i in range(8)]
    hcar = P.sb("hcar", [128, 4], F32)
    xhist = P.sb("xhist", [128, 4, 3], F32)
    Rsum = P.sb("Rsum", [128, 8], F32)
    negF = P.sb("negF", [128, KTMAX, 8], F32)
    ktl = [P.sb(f"ktl{i}", [128, 512], BF16) for i in range(4)]
    vl = [P.sb(f"vl{i}", [128, 4, VW], BF16) for i in range(4)]
    PT = [P.sb(f"PT{i}", [128, 512], BF16) for i in range(3)]
    ABF = P.sb("arena_bf", [128, 20480], BF16)
    AF32 = P.sb("arena_f32", [128, 13056], F32)
    hid = P.view(ABF, "hid", 0, [128, NJ, NT])
    o = 0
    ev_bf = {}
    for nm, sh in [("xcb", [128, NT]), ("rnn", [128, 4, NT]), ("QT", [128, 8, NT]), ("KTt", [128, 8, NT]),
                   ("Vt", [128, 4, 8, VW]), ("FQ", [3, 8, NT]), ("foT", [128, 4, NT]), ("F3", [128, 4, 8, 3])]:
        n = int(np.prod(sh[1:]))
        ev_bf[nm] = P.view(ABF, "ev_" + nm, o, sh)
        o += n
    assert o <= 20480, o
    o = 0
    ev_f = {}
    for nm, sh in [("xp", [128, 4, NT + 3]), ("xc", [128, NT]), ("hs", [128, 4, NT]), ("t0", [128, NT]), ("t1", [128, NT]),
                   ("t2", [128, NT]), ("t3", [128, NT]), ("t4", [128, NT]), ("qn", [128, NT]), ("kn", [128, NT]),
                   ("vf", [128, NT]), ("ogs", [128, 4, NT]), ("fo", [128, 4, NT]), ("lf", [128, 4, 8]),
                   ("s8a", [128, 8]), ("s8b", [128, 8]), ("s8c", [128, 8]), ("rec", [128, 8])]:
        n = int(np.prod(sh[1:]))
        ev_f[nm] = P.view(AF32, "evf_" + nm, o, sh)
        o += n
    assert o <= 13056, o

    o = 0
    od_bf = {}
    for nm, sh in [("bdA", [128, 1024]), ("bdB", [128, 1024]), ("bdK", [128, 1024]), ("bdR", [128, 1024]),
                   ("hgT", [128, 4, NT]), ("rwT", [128, 4, NT]), ("b0", [128, NT]), ("b1", [128, NT]), ("b2", [128, NT]),
                   ("b3", [128, NT])] + [("q%d" % i, [128, 128]) for i in range(72)]:
        n = int(np.prod(sh[1:]))
        od_bf[nm] = P.view(ABF, "od_" + nm, o, sh)
        o += n
    assert o <= 20480, o
    o = 0
    od_f = {}
    for nm, sh in ([("T%d" % i, [128, NT + 8]) for i in range(16)] +
                   [("fV", [128, 1024]), ("fB", [128, 1024]), ("fK", [128, 1024]), ("fA", [128, 1024])]):
        n = int(np.prod(sh[1:]))
        od_f[nm] = P.view(AF32, "odf_" + nm, o, sh)
        o += n
    assert o <= 13056, o
    grp_odd = list(od_bf.values()) + list(od_f.values())
    grp_ffn = [hid]
    grp_even = list(ev_bf.values()) + list(ev_f.values())
    grp_all = grp_even + grp_ffn + grp_odd

    def rmsnorm(N, gi):
        b = banks[7]
        for c in range(8):
            s = sqt[rot("sqt", 4)]
            if c % 2 == 0:
                ACT(s[:, :N], x[:, c, :N], AF.Square, [x], [s])
            else:
                TT("dve", s[:, :N], x[:, c, :N], x[:, c, :N], ALU.mult, [x], [s])
            MM(b[:, :N], ones_fb[:], s[:, :N], c == 0, c == 7, [ones_fb, s], [b])
        ACT(rstd[:, :N], b[:, :N], AF.Ln, [b], [rstd], bias=EPS)
        ACT(rstd[:, :N], rstd[:, :N], AF.Exp, [rstd], [rstd], scale=-0.5)
        for c in range(8):
            STT(h[:, c, :N], x[:, c, :N], normg[:, gi, c:c + 1], rstd[:, :N], ALU.mult, ALU.mult, [x, normg, rstd], [h])

    def ffn(N, layer, which):
        convert(("ffn", layer, which))
        rmsnorm(N, layer * 3 + (0 if which == 0 else 2))
        win = I["ffn_w_in"]
        wout = I["ffn_w_out"]
        for j in range(NJ):
            wb = wib[rot("wib", NWI)]
            src = ffi_b[(layer, which, j)]
            DMA(WQ, wb[:, :, :, :], src[:, :].rearrange("p (c g n) -> p c g n", g=2, n=128), [src], [wb])
            bg, bu = banks[(2 * j) % 4], banks[(2 * j + 1) % 4]
            for gu, bk in ((0, bg), (1, bu)):
                for c in range(8):
                    MM(bk[:, :N], wb[:, c, gu, :], h[:, c, :N], c == 0, c == 7, [wb, h], [bk])
            s = sg[rot("sg", 2)]
            ACT(s[:, :N], bg[:, :N], AF.Silu, [bg], [s])
            TT("dve", hid[:, j, :N], s[:, :N], bu[:, :N], ALU.mult, [s, bu], [hid])
        for half in range(2):
            acc = [banks[4 + m] for m in range(4)]
            for j in range(NJ):
                if j % 2 == 0:
                    wb = wob[rot("wob", NWB)]
                    src = ffo_b[(layer, which, half, j // 2)]
                    DMA(WQ, wb[:, :, :], src[:, :].rearrange("p (a n) -> p a n", a=2), [src], [wb])
                for m in range(4):
                    MM(acc[m][:, :N], wb[:, j % 2, m * 128:(m + 1) * 128], hid[:, j, :N], j == 0, j == NJ - 1, [wb, hid], [acc[m]])
            for m in range(4):
                c = half * 4 + m
                STT(x[:, c, :N], acc[m][:, :N], 0.5, x[:, c, :N], ALU.mult, ALU.add, [acc[m], x], [x])

    def load_x(src, t0, N):
        for s in range(cdiv(N, 128)):
            r = min(128, N - s * 128)
            xtm = xtms[rot("xtm", 2)]
            DMA("sp", xtm[:r, :], src[t0 + s * 128:t0 + s * 128 + r, :], [], [xtm])
            for c in range(8):
                bk = banks[rot("bk4", 4)]
                TR(bk[:, :r], xtm[:r, c * 128:(c + 1) * 128], ident[:r, :r], [xtm, ident], [bk])
                CP("dve" if c % 2 == 0 else "act", x[:, c, s * 128:s * 128 + r], bk[:, :r], [bk], [x])

    def store_x(dst, t0, N):
        for s in range(cdiv(N, 128)):
            r = min(128, N - s * 128)
            xtm = xtms[rot("xtm", 2)]
            for c in range(8):
                bk = banks[rot("bk4", 4)]
                TR(bk[:r, :128], x[:, c, s * 128:s * 128 + r], ident[:, :], [x, ident], [bk])
                CP("dve" if c % 2 == 0 else "act", xtm[:r, c * 128:(c + 1) * 128], bk[:r, :128], [bk], [xtm])
            DMA("sp", dst[t0 + s * 128:t0 + s * 128 + r, :], xtm[:r, :], [xtm], [])

    E = ev_f
    B = ev_bf

    def norm_heads(dst, src_ps, gain, r):
        ACT(E["t0"][:r, :], src_ps[:r, :], AF.Square, [src_ps], [E["t0"]])
        P.op("dve", lambda e: e.tensor_reduce(out=E["s8a"][:r, :], in_=E["t0"][:r, :].rearrange("p (h d) -> p h d", d=64),
                                              axis=AX.X, op=ALU.add), [E["t0"]], [E["s8a"]])
        TS("dve", E["s8a"][:r, :], E["s8a"][:r, :], 1.0 / 64, EPS, ALU.mult, ALU.add, [E["s8a"]], [E["s8a"]])
        ACT(E["s8a"][:r, :], E["s8a"][:r, :], AF.Sqrt, [E["s8a"]], [E["s8a"]])
        P.op("dve", lambda e: e.reciprocal(out=E["s8a"][:r, :], in_=E["s8a"][:r, :]), [E["s8a"]], [E["s8a"]])
        d3 = dst[:r, :].rearrange("p (h d) -> p h d", d=64)
        TT("dve", d3, src_ps[:r, :].rearrange("p (h d) -> p h d", d=64),
           E["s8a"][:r, :].unsqueeze(2).to_broadcast([r, 8, 64]), ALU.mult, [src_ps, E["s8a"]], [dst])
        TT("dve", d3, d3, gain[:r, :].unsqueeze(1).to_broadcast([r, 8, 64]), ALU.mult, [dst, gain], [dst])

    def to_featmajor(dstT, src, s, r, eng_alt=0):
        for c in range(4):
            bk = banks[rot("bk4", 4)]
            TR(bk[:, :r], src[:r, c * 128:(c + 1) * 128], ident[:r, :r], [src, ident], [bk])
            CP("act" if (c + eng_alt) % 2 == 0 else "dve", dstT[:, c, s * 128:s * 128 + r], bk[:, :r], [bk], [dstT])

    def to_heads(dstT, src, s, r, eng_alt=0):
        for c in range(4):
            bk = banks[rot("bk4", 4)]
            TR(bk[:, :r], src[:r, c * 128:(c + 1) * 128], ident[:r, :r], [src, ident], [bk])
            e0, e1 = ("act", "dve") if (c + eng_alt) % 2 == 0 else ("dve", "act")
            P.op(e0, (lambda e, o_=dstT[0:64, 2 * c, s * 128:s * 128 + r], i_=bk[0:64, :r], en=e0:
                      (e.copy(out=o_, in_=i_) if en == "act" else e.tensor_copy(out=o_, in_=i_))), [bk], [], pwrites=[dstT])
            P.op(e1, (lambda e, o_=dstT[64:128, 2 * c + 1, s * 128:s * 128 + r], i_=bk[64:128, :r], en=e1:
                      (e.copy(out=o_, in_=i_) if en == "act" else e.tensor_copy(out=o_, in_=i_))), [bk], [], pwrites=[dstT])

    def init_heads():
        MEMSET("pool", B["QT"][:, :, :], 0.0, [B["QT"]])
        MEMSET("pool", B["KTt"][:, :, :], 0.0, [B["KTt"]])
        kv = B["KTt"][:, :, :].rearrange("p (h two) n -> p h two n", two=2)
        P.op("pool", lambda e: e.memset(kv[64:67, :, 0, :], 1.0), [], [], pwrites=[B["KTt"]])
        P.op("pool", lambda e: e.memset(kv[0:3, :, 1, :], 1.0), [], [], pwrites=[B["KTt"]])

    def cumF(lf_ap, lfbuf, r, kt, s, want_fq):
        bk = banks[rot("bk4", 4)]
        MM(bk[:r, 0:8], utri[:r, :r], lf_ap, True, False, [utri, lfbuf], [bk])
        MM(bk[:r, 0:8], ones1[:, :r], Rsum[:, :], False, True, [ones1, Rsum], [bk])
        TS("dve", negF[:r, kt, :], bk[:r, 0:8], -1.0, None, ALU.mult, None, [bk], [negF])
        TT("pool", Rsum[:r, :], Rsum[:r, :], lf_ap, ALU.add, [Rsum, lfbuf], [Rsum])
        if want_fq:
            F3 = B["F3"]
            CP("dve", F3[:r, s, :, 0], bk[:r, 0:8], [bk], [F3])
            CP("dve", E["s8b"][:r, :], F3[:r, s, :, 0], [F3], [E["s8b"]])
            TT("dve", E["s8c"][:r, :], bk[:r, 0:8], E["s8b"][:r, :], ALU.subtract, [bk, E["s8b"]], [E["s8c"]])
            CP("dve", F3[:r, s, :, 1], E["s8c"][:r, :], [E["s8c"]], [F3])
            CP("dve", E["s8b"][:r, :], F3[:r, s, :, 1], [F3], [E["s8b"]])
            TT("dve", E["s8c"][:r, :], E["s8c"][:r, :], E["s8b"][:r, :], ALU.subtract, [E["s8c"], E["s8b"]], [E["s8c"]])
            CP("dve", F3[:r, s, :, 2], E["s8c"][:r, :], [E["s8c"]], [F3])

    def store_kv(key_base, N):
        nsub = cdiv(N, 128)
        r = min(128, N)
        ktb = key_base // 128
        DMA("sp", KT_scr[:, :, key_base:key_base + N].rearrange("q p t -> p q t"), B["KTt"][:, :, :N], [B["KTt"]], [KT_scr], slow=True)
        for s in range(nsub):
            rr = min(128, N - s * 128)
            DMA("sp", V_scr[:, :rr, ktb + s, :].rearrange("h p d -> p h d"), B["Vt"][:rr, s, :, :], [B["Vt"]], [V_scr], slow=True)

    def ingest_past(PASTN):
        for c in range(PASTN // 512):
            DMA("sp", E["lf"][:, :, :], I["cache_fox_logf"][c * 512:(c + 1) * 512, :].rearrange("(s p) h -> p s h", p=128),
                [], [E["lf"]], slow=True)
            for s in range(4):
                t0 = c * 512 + s * 128
                DMA("sp", E["kn"][:, :], I["cache_fox_k"][t0:t0 + 128, :], [], [E["kn"]])
                to_heads(B["KTt"], E["kn"], s, 128)
                DMA("sp", E["vf"][:, :], I["cache_fox_v"][t0:t0 + 128, :], [], [E["vf"]])
                CP("pool", B["Vt"][:, s, :, 0:64], E["vf"][:, :].rearrange("p (h d) -> p h d", d=64), [E["vf"]], [B["Vt"]])
                cumF(E["lf"][:, s, :], E["lf"], 128, c * 4 + s, s, False)
            store_kv(c * 512, 512)

    def even_mixer(N, S):
        key_base, t0 = S["key_base"], S["t0"]
        nsub = cdiv(N, 128)
        W = I["e_w_in"]
        convert(("even",))
        rmsnorm(N, 1)
        xp, xc, hs = E["xp"], E["xc"], E["hs"]
        CP("pool", xp[:, :, 0:3], xhist[:, :, :], [xhist], [xp])
        for c in range(4):
            wb = wib[rot("wib", NWI)]
            src = mixw[("e_w_in", c)]
            DMA(WQ, wb[:, :, 0, :], src[:, :].rearrange("p (k n) -> p k n", n=128), [src], [wb])
            bk = banks[rot("bk4", 4)]
            for kc in range(8):
                MM(bk[:, :N], wb[:, kc, 0, :], h[:, kc, :N], kc == 0, kc == 7, [wb, h], [bk])
            CP("act", xp[:, c, 3:3 + N], bk[:, :N], [bk], [xp])
        CP("pool", xhist[:, :, :], xp[:, :, N:N + 3], [xp], [xhist])
        sets = [(xc, E["t0"], E["t1"], E["t2"], E["t3"], B["xcb"]),
                (E["qn"], E["kn"], E["vf"], E["fo"][:, 0, :], E["fo"][:, 1, :], B["foT"][:, 0, :])]
        setbufs = [(xc, E["t0"], E["t1"], E["t2"], E["t3"], B["xcb"]),
                   (E["qn"], E["kn"], E["vf"], E["fo"], E["fo"], B["foT"])]
        for c in range(4):
            xc_, r_, i_, a_, q_, xb_ = sets[c % 2]
            bxc, br_, bi_, ba_, bq_, bxb = setbufs[c % 2]
            TS("dve", xc_[:, :N], xp[:, c, 0:N], convw[:, c, 0:1], convb[:, c:c + 1], ALU.mult, ALU.add, [xp, convw, convb], [bxc])
            for j in range(1, 4):
                STT(xc_[:, :N], xp[:, c, j:j + N], convw[:, c, j:j + 1], xc_[:, :N], ALU.mult, ALU.add, [xp, convw, bxc], [bxc])
            CP("act", xb_[:, :N], xc_[:, :N], [bxc], [bxb])
            b1, b2 = banks[rot("bk4", 4)], banks[rot("bk4", 4)]
            MM(b1[:, :N], bda[:, c, :], xb_[:, :N], True, True, [bda, bxb], [b1])
            MM(b2[:, :N], bdx[:, c, :], xb_[:, :N], True, True, [bdx, bxb], [b2])
            ACT(r_[:, :N], b1[:, :N], AF.Sigmoid, [b1, lba], [br_], bias=lba[:, c:c + 1])
            ACT(i_[:, :N], b2[:, :N], AF.Sigmoid, [b2, lbx], [bi_], bias=lbx[:, c:c + 1])
            ACT(a_[:, :N], r_[:, :N], AF.Exp, [br_, c1], [ba_], scale=c1[:, c:c + 1])
            if bq_ is E["fo"]:
                ACT(q_[:, :N], r_[:, :N], AF.Exp, [br_, c2], [], scale=c2[:, c:c + 1]) if False else \
                    P.op("act", (lambda e, o_=q_[:, :N], in__=r_[:, :N], sc=c2[:, c:c + 1]: e.activation(out=o_, in_=in__, func=AF.Exp, scale=sc)),
                         [br_, c2], [bq_])
            else:
                ACT(q_[:, :N], r_[:, :N], AF.Exp, [br_, c2], [bq_], scale=c2[:, c:c + 1])
            ACT(q_[:, :N], q_[:, :N], AF.Sqrt, [bq_], [bq_], bias=1.0, scale=-1.0)
            TT("dve", i_[:, :N], i_[:, :N], xc_[:, :N], ALU.mult, [bi_, bxc], [bi_])
            TT("dve", i_[:, :N], i_[:, :N], q_[:, :N], ALU.mult, [bi_, bq_], [bi_])
            P.op("dve", (lambda e, c=c, a__=a_[:, :N], i__=i_[:, :N]: e.tensor_tensor_scan(
                out=hs[:, c, :N], data0=a__, data1=i__, initial=hcar[:, c:c + 1], op0=ALU.mult, op1=ALU.add)),
                 [ba_, bi_, hcar], [hs])
            CP("dve", hcar[:, c:c + 1], hs[:, c, N - 1:N], [hs], [hcar])
        if stage < 1:
            return
        for c in range(4):
            wb = wib[rot("wib", NWI)]
            src = mixw[("e_w_in", 4 + c)]
            DMA(WQ, wb[:, :, 0, :], src[:, :].rearrange("p (k n) -> p k n", n=128), [src], [wb])
            bk = banks[rot("bk4", 4)]
            for kc in range(8):
                MM(bk[:, :N], wb[:, kc, 0, :], h[:, kc, :N], kc == 0, kc == 7, [wb, h], [bk])
            ga, gb = (E["t0"], E["t4"]) if c % 2 == 0 else (E["t1"], E["t2"])
            ACT(ga[:, :N], bk[:, :N], AF.Square, [bk], [ga])
            TS("dve", ga[:, :N], ga[:, :N], 0.044715, 1.0, ALU.mult, ALU.add, [ga], [ga])
            TT("dve", ga[:, :N], ga[:, :N], bk[:, :N], ALU.mult, [ga, bk], [ga])
            ACT(ga[:, :N], ga[:, :N], AF.Sigmoid, [ga], [ga], scale=1.5957691216057308)
            TT("dve", gb[:, :N], bk[:, :N], hs[:, c, :N], ALU.mult, [bk, hs], [gb])
            TT("dve", B["rnn"][:, c, :N], gb[:, :N], ga[:, :N], ALU.mult, [gb, ga], [B["rnn"]])
        if stage < 2:
            return
        for gi, c0 in enumerate((1024, 1536, 2048, 2560)):
            wt = wtm[rot("wtm", 2)]
            src = mixw[("e_w_in_t", c0)]
            DMA(WQ, wt[:, :, :], src[:, :].rearrange("p (k n) -> p k n", n=512), [src], [wt])
            for s in range(nsub):
                r = min(128, N - s * 128)
                bk = banks[4 + rot("bk4b", 4)]
                for kc in range(8):
                    MM(bk[:r, :], h[:, kc, s * 128:s * 128 + r], wt[:, kc, :], kc == 0, kc == 7, [h, wt], [bk])
                if gi > sub:
                    continue
                if gi == 0:
                    norm_heads(E["qn"], bk, gq, r)
                    to_heads(B["QT"], E["qn"], s, r)
                elif gi == 1:
                    norm_heads(E["kn"], bk, gk, r)
                    DMA("sp", S["fox_k"][t0 + s * 128:t0 + s * 128 + r, :], E["kn"][:r, :], [E["kn"]], [])
                    to_heads(B["KTt"], E["kn"], s, r, 1)
                elif gi == 2:
                    if vmode in (0, 1):
                        CP("act", E["vf"][:r, :], bk[:r, :], [bk], [E["vf"]])
                        DMA("sp", S["fox_v"][t0 + s * 128:t0 + s * 128 + r, :], E["vf"][:r, :], [E["vf"]], [])
                    if vmode in (0, 2):
                        CP("dve", B["Vt"][:r, s, :, 0:64], bk[:r, :].rearrange("p (h d) -> p h d", d=64), [bk], [B["Vt"]])
                else:
                    ACT(E["ogs"][:r, s, :], bk[:r, :], AF.Sigmoid, [bk], [E["ogs"]])
        if stage < 3:
            return
        for s in range(nsub):
            r = min(128, N - s * 128)
            bk = banks[rot("bk4", 4)]
            for kc in range(8):
                MM(bk[:r, 0:8], h[:, kc, s * 128:s * 128 + r], wfl[:, kc, :], kc == 0, kc == 7, [h, wfl], [bk])
            TT("dve", E["s8b"][:r, :], bk[:r, 0:8], fbias[:r, :], ALU.add, [bk, fbias], [E["s8b"]])
            ACT(E["s8b"][:r, :], E["s8b"][:r, :], AF.Exp, [E["s8b"]], [E["s8b"]], scale=-1.0)
            ACT(E["s8b"][:r, :], E["s8b"][:r, :], AF.Ln, [E["s8b"]], [E["s8b"]], bias=1.0)
            TS("dve", E["lf"][:r, s, :], E["s8b"][:r, :], -1.0, None, ALU.mult, None, [E["s8b"]], [E["lf"]])
            DMA("sp", S["fox_lf"][t0 + s * 128:t0 + s * 128 + r, :], E["lf"][:r, s, :], [E["lf"]], [])
            cumF(E["lf"][:r, s, :], E["lf"], r, key_base // 128 + s, s, True)
        for hh in range(8):
            bk = banks[rot("bk4", 4)]
            for s in range(nsub):
                r = min(128, N - s * 128)
                MM(bk[0:3, s * 128:s * 128 + r], B["F3"][:r, s, hh, :], identb[:r, :r], True, True, [B["F3"], identb], [bk])
            CP("act" if hh % 2 == 0 else "dve", B["FQ"][0:3, hh, :N], bk[0:3, :N], [bk], [], ) if False else \
                P.op("act" if hh % 2 == 0 else "dve",
                     (lambda e, o_=B["FQ"][0:3, hh, :N], i_=bk[0:3, :N], en=("act" if hh % 2 == 0 else "dve"):
                      (e.copy(out=o_, in_=i_) if en == "act" else e.tensor_copy(out=o_, in_=i_))), [bk], [], pwrites=[B["FQ"]])
        fqv = B["FQ"][0:3, :, :N].rearrange("p (h two) n -> p h two n", two=2)
        qv = B["QT"][:, :, :N].rearrange("p (h two) n -> p h two n", two=2)
        DMA("sp", qv[64:67, :, 0, :], fqv[:, :, 0, :], [B["FQ"]], [], PW=[B["QT"]])
        DMA("sp", qv[0:3, :, 1, :], fqv[:, :, 1, :], [B["FQ"]], [], PW=[B["QT"]])
        if stage < 4:
            return
        store_kv(key_base, N)
        n_keys = key_base + N
        nch = cdiv(n_keys, 512)
        for p in range(4):
            Ob = [banks[4 + rot("bkO", 4)], banks[4 + rot("bkO", 4)]]
            tiles = []
            loads = {}
            for c in range(nch):
                vk = min(512, n_keys - c * 512)
                diag = (c == nch - 1)
                for hh in range(2):
                    hd = 2 * p + hh
                    kb = ktl[(c % 2) * 2 + hh]
                    vb = vl[(c % 2) * 2 + hh]
                    loads[(c, hh)] = (kb, vb, vk, hd)
                    for kk in range(cdiv(vk, 128)):
                        rk = min(128, vk - kk * 128)
                        tiles.append((c, hh, hd, kk, rk, diag, kb, vb))

            def issue_loads(c):
                for hh in range(2):
                    kb, vb, vk, hd = loads[(c, hh)]
                    nkt = cdiv(vk, 128)
                    rr = min(128, vk)
                    DMA("sp", kb[:, :vk], KT_scr[hd, :, c * 512:c * 512 + vk], [KT_scr], [kb])
                    DMA("sp", vb[:rr, :nkt, :], V_scr[hd, :rr, c * 4:c * 4 + nkt, :], [V_scr], [vb])

            nt_ = len(tiles)
            first_seen = [True, True]
            last_ti = [max(i for i, t_ in enumerate(tiles) if t_[1] == hh_) for hh_ in range(2)]
            loaded = set()

            def need(c):
                if c < nch and c not in loaded:
                    loaded.add(c)
                    issue_loads(c)

            def emit_S(ti):
                c, hh, hd, kk, rk, diag, kb, vb = tiles[ti]
                need(c)
                sb_ = banks[rot("bkS", 4)]
                MM(sb_[:rk, :N], kb[:, kk * 128:kk * 128 + rk], B["QT"][:, hd, :N], True, not diag, [kb, B["QT"]], [sb_])
                if diag:
                    MM(sb_[:rk, :N], identb[:rk, :rk], maskb[:rk, kk, :N], False, True, [identb, maskb], [sb_])
                return sb_

            need(0)
            pend = emit_S(0) if nt_ else None
            for ti in range(nt_):
                c, hh, hd, kk, rk, diag, kb, vb = tiles[ti]
                if hh == 0 and kk == 0:
                    need(c + 1)
                sb_ = pend
                if ti + 1 < nt_:
                    pend = emit_S(ti + 1)
                pt = PT[rot("PT", 3)]
                ACT(pt[:rk, :N], sb_[:rk, :N], AF.Exp, [sb_, negF], [pt], bias=negF[:rk, c * 4 + kk, hd:hd + 1])
                qlist = [qs for qs in range(nsub) if not (diag and qs < kk)]
                for qs in qlist:
                    rq = min(128, N - qs * 128)
                    last = (ti == last_ti[hh]) and (qs == qlist[-1])
                    MM(Ob[hh][:rq, qs * 65:(qs + 1) * 65], pt[:rk, qs * 128:qs * 128 + rq], vb[:rk, kk, 0:65],
                       first_seen[hh], last, [pt, vb], [Ob[hh]])
                    first_seen[hh] = False
            for hh in range(2):
                hd = 2 * p + hh
                for qs in range(nsub):
                    rq = min(128, N - qs * 128)
                    RECIP(E["rec"][:rq, qs:qs + 1], Ob[hh][:rq, qs * 65 + 64:qs * 65 + 65], [Ob[hh]], [E["rec"]])
                    STT(E["fo"][:rq, qs, hd * 64:(hd + 1) * 64], Ob[hh][:rq, qs * 65:qs * 65 + 64], E["rec"][:rq, qs:qs + 1],
                        E["ogs"][:rq, qs, hd * 64:(hd + 1) * 64], ALU.mult, ALU.mult, [Ob[hh], E["rec"], E["ogs"]], [E["fo"]])
        if stage < 5:
            return
        for s in range(nsub):
            r = min(128, N - s * 128)
            for c in range(4):
                bk = banks[rot("bk4", 4)]
                TR(bk[:, :r], E["fo"][:r, s, c * 128:(c + 1) * 128], ident[:r, :r], [E["fo"], ident], [bk])
                CP("act" if c % 2 == 0 else "dve", B["foT"][:, c, s * 128:s * 128 + r], bk[:, :r], [bk], [B["foT"]])
        if stage < 6:
            return
        Wo = I["e_w_out"]
        for half in range(2):
            wt = wtm[rot("wtm", 2)]
            src = mixw[("e_w_out_t", half * 512)]
            DMA(WQ, wt[:, :, :], src[:, :].rearrange("p (k n) -> p k n", n=512), [src], [wt])
            for m in range(4):
                bk = banks[rot("bk4", 4)]
                for kc in range(8):
                    rhs = B["rnn"][:, kc, :N] if kc < 4 else B["foT"][:, kc - 4, :N]
                    MM(bk[:, :N], wt[:, kc, m * 128:(m + 1) * 128], rhs, kc == 0, kc == 7, [wt, B["rnn"], B["foT"]], [bk])
                cc = half * 4 + m
                TT("dve", x[:, cc, :N], x[:, cc, :N], bk[:, :N], ALU.add, [x, bk], [x])

    def even_finish(S):
        for j in range(3):
            DMA("sp", S["lru_conv"][j, :].rearrange("(c p) -> p c", p=128), xhist[:, :, j], [xhist], [], slow=True)
        DMA("sp", S["lru_h"][:].rearrange("(c p) -> p c", p=128), hcar[:, :], [hcar], [], slow=True)

    def even_init(S, sample):
        if sample:
            for j in range(3):
                DMA("sp", xhist[:, :, j], I["state_lru_conv"][j, :].rearrange("(c p) -> p c", p=128), [], [xhist], slow=True)
            DMA("sp", hcar[:, :], I["state_lru_h"][:].rearrange("(c p) -> p c", p=128), [], [hcar], slow=True)
        else:
            MEMSET("dve", xhist[:, :, :], 0.0, [xhist])
            MEMSET("dve", hcar[:, :], 0.0, [hcar])
        MEMSET("dve", Rsum[:, :], 0.0, [Rsum])
        MEMSET("pool", B["Vt"][:, :, :, :], 1.0, [B["Vt"]])

    Fo = od_f
    Bo = od_bf
    Wd = I["o_w_in"]
    TMP = [Fo["T%d" % i] for i in range(16)]

    def SCAN(o_, d0, d1, R, W):
        P.op("dve", lambda e: e.tensor_tensor_scan(out=o_, data0=d0, data1=d1, initial=0.0, op0=ALU.mult, op1=ALU.add), R, W)

    def odd_mixer(N, S):
        C = min(64, N)
        nch = N // C
        W2 = 2 * C
        MK = m64 if C == 64 else m16
        su, sl, ui, nsu = MK[:W2, 0, :], MK[:W2, 1, :], MK[:W2, 2, :], MK[:W2, 3, :]
        T = TMP
        convert(("odd",))
        rmsnorm(N, 4)
        for nm in ("bdA", "bdB", "bdK", "bdR"):
            MEMSET("pool", Bo[nm][:, :], 0.0, [Bo[nm]])
        for nm in ("fV", "fB", "fK", "fA"):
            MEMSET("pool", Fo[nm][:, :], 0.0, [Fo[nm]])

        def proj_fm(col0, bank):
            wb = wib[rot("wib", NWI)]
            DMA("pool", wb[:, :, 0, :], Wd[:, col0:col0 + 128].rearrange("(k p) n -> p k n", p=128), [], [wb])
            for kc in range(8):
                MM(bank[:, :N], wb[:, kc, 0, :], h[:, kc, :N], kc == 0, kc == 7, [wb, h], [bank])

        def v3(buf, pr=slice(0, 128)):
            return buf[pr, 0:N].rearrange("p (j t) -> p j t", t=C)

        def lastcol(buf, pr=slice(0, 128)):
            return v3(buf, pr)[:, :, C - 1:C].to_broadcast([pr.stop - pr.start, nch, C])

        def nb():
            return banks[rot("bkR", 8)]

        for c in range(4):
            bq, bf_, bi = nb(), nb(), nb()
            proj_fm(c * 128, bq)
            proj_fm(512 + c * 128, bf_)
            proj_fm(1024 + c * 128, bi)
            bg = nb()
            proj_fm(1536 + c * 128, bg)
            ACT(T[0][:, :N], bf_[:, :N], AF.Sigmoid, [bf_], [T[0]])
            TS("dve", T[0][:, :N], T[0][:, :N], oml[:, c:c + 1], lb[:, c:c + 1], ALU.mult, ALU.add, [T[0], oml, lb], [T[0]])
            TS("dve", T[1][:, :N], T[0][:, :N], -1.0, 1.0, ALU.mult, ALU.add, [T[0]], [T[1]])
            ACT(T[2][:, :N], T[0][:, :N], AF.Ln, [T[0]], [T[2]])
            for j in range(nch):
                SCAN(T[3][:, j * C:(j + 1) * C], ones_row[:, :C], T[2][:, j * C:(j + 1) * C], [ones_row, T[2]], [T[3]])
            ACT(T[4][:, :N], T[3][:, :N], AF.Exp, [T[3]], [T[4]])
            ACT(T[5][:, :N], T[3][:, :N], AF.Exp, [T[3]], [T[5]], scale=-1.0)
            TT("dve", Bo["b0"][:, :N], bq[:, :N], T[4][:, :N], ALU.mult, [bq, T[4]], [Bo["b0"]])
            TT("dve", T[1][:, :N], T[1][:, :N], T[5][:, :N], ALU.mult, [T[1], T[5]], [T[1]])
            CP("act", Bo["b1"][:, :N], T[1][:, :N], [T[1]], [Bo["b1"]])
            TT("dve", v3(T[6]), v3(T[1]), lastcol(T[4]), ALU.mult, [T[1], T[4]], [T[6]])
            CP("act", T[7][:, :N], bi[:, :N], [bi], [T[7]])
            ACT(T[10][:, :N], bg[:, :N], AF.Silu, [bg], [T[10]])
            HB = [Bo["q%d" % i] for i in range(32)]
            for j in range(nch):
                cs = slice(j * C, (j + 1) * C)
                vtm, kdtm = HB[j], HB[8 + j]
                t0_, t1_ = nb(), nb()
                TR(t0_[:C, :128], T[7][:, cs], ident[:, :], [T[7], ident], [t0_])
                CP("act", vtm[:C, :], t0_[:C, :128], [t0_], [vtm])
                TR(t1_[:C, :128], T[6][:, cs], ident[:, :], [T[6], ident], [t1_])
                CP("dve", kdtm[:C, :], t1_[:C, :128], [t1_], [kdtm])
            for j in range(nch):
                cs = slice(j * C, (j + 1) * C)
                Am = HB[16 + j]
                t2_ = nb()
                MM(t2_[:C, :C], Bo["b1"][:, cs], Bo["b0"][:, cs], True, True, [Bo["b1"], Bo["b0"]], [t2_])
                TT("dve", Am[:C, :C], t2_[:C, :C], ui[:C, :C], ALU.mult, [t2_, MK], [Am])
            Sb = [None] * (nch + 1)
            for j in range(nch):
                vtm, kdtm = HB[j], HB[8 + j]
                t4_ = nb()
                MM(t4_[:, :128], kdtm[:C, :], vtm[:C, :], True, True, [kdtm, vtm], [t4_])
                if j == 0:
                    CP("pool", HB[24][:, :], S_hgb[:, c, :], [S_hgb], [HB[24]])
                STT(S_hg[:, c, :], S_hg[:, c, :], T[4][:, (j + 1) * C - 1:(j + 1) * C], t4_[:, :128], ALU.mult, ALU.add,
                    [S_hg, T[4], t4_], [S_hg])
                if j < nch - 1:
                    CP("act", HB[24 + j + 1][:, :], S_hg[:, c, :], [S_hg], [HB[24 + j + 1]])
                else:
                    CP("act", S_hgb[:, c, :], S_hg[:, c, :], [S_hg], [S_hgb])
            for j in range(nch):
                cs = slice(j * C, (j + 1) * C)
                vtm, Am, Sbj = HB[j], HB[16 + j], HB[24 + j]
                t3_ = nb()
                MM(t3_[:, :C], vtm[:C, :], Am[:C, :C], True, False, [vtm, Am], [t3_])
                MM(t3_[:, :C], Sbj[:, :], Bo["b0"][:, cs], False, True, [Sbj, Bo["b0"]], [t3_])
                CP("act" if j % 2 == 0 else "dve", T[8][:, cs], t3_[:, :C], [t3_], [T[8]])
            ACT(T[9][:, :N], T[8][:, :N], AF.Square, [T[8]], [T[9]])
            t5_ = nb()
            MM(t5_[:, :N], ones1[:, :], T[9][:, :N], True, True, [ones1, T[9]], [t5_])
            ACT(T[9][:, :N], t5_[:, :N], AF.Ln, [t5_], [T[9]], scale=1.0 / 128, bias=EPS)
            ACT(T[9][:, :N], T[9][:, :N], AF.Exp, [T[9]], [T[9]], scale=-0.5)
            STT(T[8][:, :N], T[8][:, :N], hgn[:, c:c + 1], T[9][:, :N], ALU.mult, ALU.mult, [T[8], hgn, T[9]], [T[8]])
            TT("dve", Bo["hgT"][:, c, :N], T[8][:, :N], T[10][:, :N], ALU.mult, [T[8], T[10]], [Bo["hgT"]])

        def zmix(ci, dst):
            bk = nb()
            proj_fm(2048 + ci * 128, bk)
            zx = T[15]
            CP("act", zx[:, 1:N + 1], bk[:, :N], [bk], [zx])
            CP("pool", zx[:, 0:1], zprev[:, ci:ci + 1], [zprev], [zx])
            CP("pool", zprev[:, ci:ci + 1], zx[:, N:N + 1], [zx], [zprev])
            TS("dve", dst[:, :N], zx[:, 0:N], mu[:, ci:ci + 1], None, ALU.mult, None, [zx, mu], [dst])
            STT(dst[:, :N], zx[:, 1:N + 1], omu[:, ci:ci + 1], dst[:, :N], ALU.mult, ALU.add, [zx, omu, dst], [dst])

        zmix(12, T[0])
        ACT(Bo["b2"][0:64, :N], T[0][0:64, :N], AF.Tanh, [T[0]], [Bo["b2"]])
        CP("pool", Bo["b2"][64:128, :N], T[0][64:128, :N], [T[0]], [Bo["b2"]])
        zmix(13, T[0])
        ACT(Bo["b3"][:, :N], T[0][:, :N], AF.Sigmoid, [T[0]], [Bo["b3"]])
        nsteps = {64: 5, 16: 3}[C]
        def bdv(buf, hh):
            return buf[hh * 64:(hh + 1) * 64, 0:nch * W2].rearrange("p (j b t) -> p j b t", b=2, t=C)[:, :, hh, :]

        def prep_early(c):
            cc = slice(c * 128, (c + 1) * 128)
            specs = ((c, T[0], T[15]), (4 + c, T[1], T[13]), (8 + c, T[2], T[12]))
            pbanks = []
            for ci, dst, zx in specs:
                bk_ = nb()
                proj_fm(2048 + ci * 128, bk_)
                pbanks.append(bk_)
            tw, ta = nb(), nb()
            MM(tw[:, :N], w2t[0:64, cc], Bo["b2"][0:64, :N], True, True, [w2t, Bo["b2"]], [tw])
            MM(ta[:, :N], a2t[64:128, cc], Bo["b2"][64:128, :N], True, True, [a2t, Bo["b2"]], [ta])
            for (ci, dst, zx), bk_ in zip(specs, pbanks):
                CP("act", zx[:, 1:N + 1], bk_[:, :N], [bk_], [zx])
                CP("pool", zx[:, 0:1], zprev[:, ci:ci + 1], [zprev], [], ) if False else \
                    P.op("pool", (lambda e, o_=zx[:, 0:1], i_=zprev[:, ci:ci + 1]: e.tensor_copy(out=o_, in_=i_)), [zprev], [], pwrites=[zx])
                CP("pool", zprev[:, ci:ci + 1], zx[:, N:N + 1], [zx], [zprev])
            ACT(T[3][:, :N], tw[:, :N], AF.Exp, [tw, nw0], [T[3]], scale=-1.0, bias=nw0[:, c:c + 1])
            ACT(T[3][:, :N], T[3][:, :N], AF.Ln, [T[3]], [T[3]], bias=1.0)
            ACT(T[3][:, :N], T[3][:, :N], AF.Exp, [T[3]], [T[3]], scale=-1.0, bias=-0.5)
            ACT(T[4][:, :N], ta[:, :N], AF.Sigmoid, [ta, a0], [T[4]], bias=a0[:, c:c + 1])
            for ci, dst, zx in specs:
                TS("dve", dst[:, :N], zx[:, 0:N], mu[:, ci:ci + 1], None, ALU.mult, None, [zx, mu], [dst])
                STT(dst[:, :N], zx[:, 1:N + 1], omu[:, ci:ci + 1], dst[:, :N], ALU.mult, ALU.add, [zx, omu, dst], [dst])
            yield
            TS("dve", T[6][:, :N], T[1][:, :N], kkw[:, c:c + 1], None, ALU.mult, None, [T[1], kkw], [T[6]])
            TT("dve", T[10][:, :N], T[6][:, :N], T[6][:, :N], ALU.mult, [T[6]], [T[10]])
            t_ = nb()
            MM(t_[:, :N], blk[:, :], T[10][:, :N], True, True, [blk, T[10]], [t_])
            TS("dve", T[10][:, :N], t_[:, :N], 1e-24, None, ALU.max, None, [t_], [T[10]])
            yield
            ACT(T[10][:, :N], T[10][:, :N], AF.Ln, [T[10]], [T[10]])
            ACT(T[10][:, :N], T[10][:, :N], AF.Exp, [T[10]], [T[10]], scale=-0.5)
            TT("dve", T[6][:, :N], T[6][:, :N], T[10][:, :N], ALU.mult, [T[6], T[10]], [T[6]])
            yield
            TS("dve", T[10][:, :N], T[4][:, :N], kaw[:, c:c + 1], omka[:, c:c + 1], ALU.mult, ALU.add, [T[4], kaw, omka], [T[10]])
            TT("dve", T[1][:, :N], T[1][:, :N], T[10][:, :N], ALU.mult, [T[1], T[10]], [T[1]])
            TT("dve", T[8][:, :N], T[6][:, :N], T[4][:, :N], ALU.mult, [T[6], T[4]], [T[8]])
            yield
            for j in range(nch):
                SCAN(T[10][:, j * C:(j + 1) * C], ones_row[:, :C], T[3][:, j * C:(j + 1) * C], [ones_row, T[3]], [T[10]])
            yield
            ACT(T[12][:, :N], T[10][:, :N], AF.Exp, [T[10]], [T[12]])
            TT("dve", T[13][:, :N], T[3][:, :N], T[10][:, :N], ALU.subtract, [T[3], T[10]], [T[13]])
            ACT(T[13][:, :N], T[13][:, :N], AF.Exp, [T[13]], [T[13]])
            yield

        def prep_late(c):
            cc = slice(c * 128, (c + 1) * 128)
            t_ = nb()
            MM(t_[:, :N], g2t[:, cc], Bo["b3"][:, :N], True, True, [g2t, Bo["b3"]], [t_])
            CP("act", T[5][:, :N], t_[:, :N], [t_], [T[5]])
            STT(T[9][:, :N], T[0][:, :N], rkw[:, c:c + 1], T[1][:, :N], ALU.mult, ALU.mult, [T[0], rkw, T[1]], [T[9]])
            t_ = nb()
            MM(t_[:, :N], blk[:, :], T[9][:, :N], True, True, [blk, T[9]], [t_])
            TT("dve", T[9][:, :N], t_[:, :N], T[2][:, :N], ALU.mult, [t_, T[2]], [T[9]])
            ACT(T[11][:, :N], T[10][:, :N], AF.Exp, [T[10]], [T[11]], scale=-1.0)
            TT("dve", T[14][:, :N], T[1][:, :N], T[12][:, :N], ALU.mult, [T[1], T[12]], [T[14]])
            TT("dve", T[4][:, :N], T[8][:, :N], T[12][:, :N], ALU.mult, [T[8], T[12]], [T[4]])
            for hh in range(2):
                pr = slice(hh * 64, (hh + 1) * 64)
                eng = "dve" if hh == 0 else "pool"
                TT(eng, bdv(Bo["bdA"], hh), v3(T[13], pr), v3(T[6], pr), ALU.mult, [T[13], T[6]], [Bo["bdA"]])
                ceng = "dve" if hh == 0 else "act"
                CP(ceng, bdv(Bo["bdB"], hh), v3(T[4], pr), [T[4]], [Bo["bdB"]])
                CP(ceng, bdv(Bo["bdK"], hh), v3(T[14], pr), [T[14]], [Bo["bdK"]])
                TT(eng, bdv(Bo["bdR"], hh), v3(T[0], pr), v3(T[11], pr), ALU.mult, [T[0], T[11]], [Bo["bdR"]])
                CP(ceng, bdv(Fo["fV"], hh), v3(T[2], pr), [T[2]], [Fo["fV"]])
                TT(eng, bdv(Fo["fA"], hh), v3(T[13], pr), v3(T[6], pr), ALU.mult, [T[13], T[6]], [Fo["fA"]])
                TT(eng, bdv(Fo["fK"], hh), v3(T[14], pr), lastcol(T[11], pr), ALU.mult, [T[14], T[11]], [Fo["fK"]])
                if hh == 0:
                    STT(bdv(Fo["fB"], hh), v3(T[4], pr), -1.0, lastcol(T[11], pr), ALU.mult, ALU.mult, [T[4], T[11]], [Fo["fB"]])
                else:
                    TT("pool", bdv(Fo["fB"], hh), v3(T[4], pr), lastcol(T[11], pr), ALU.mult, [T[4], T[11]], [Fo["fB"]])
                    TS("pool", bdv(Fo["fB"], hh), bdv(Fo["fB"], hh), -1.0, None, ALU.mult, None, [Fo["fB"]], [Fo["fB"]])

        for c in range(4):
            for _ in prep_early(c):
                pass
            prep_late(c)
            nxt = None
            yT = T[7]

            def phaseA(j, M):
                ws = slice(j * W2, (j + 1) * W2)
                A_, B_, K_, R_ = Bo["bdA"][:, ws], Bo["bdB"][:, ws], Bo["bdK"][:, ws], Bo["bdR"][:, ws]
                (N0, Nt0, Na, Nta, P0, Pa, LakT, nMrbT, MrkT, Vbd, nBd, Kd, Atm, Wtm, LV, U0, Rhat, PhiT) = M
                p = nb()
                MM(p[:W2, :W2], B_, A_, True, True, [Bo["bdB"], Bo["bdA"]], [p])
                CP("act", N0[:W2, :W2], p[:W2, :W2], [p], [N0])
                TT("pool", N0[:W2, :W2], N0[:W2, :W2], nsu, ALU.mult, [N0, MK], [N0])
                yield
                p = nb()
                MM(p[:W2, :W2], A_, B_, True, True, [Bo["bdA"], Bo["bdB"]], [p])
                STT(Nt0[:W2, :W2], p[:W2, :W2], -1.0, sl, ALU.mult, ALU.mult, [p, MK], [Nt0])
                TT("pool", P0[:W2, :W2], N0[:W2, :W2], ident[:W2, :W2], ALU.add, [N0, ident], [P0])
                yield
                p = nb()
                MM(p[:W2, :W2], K_, A_, True, True, [Bo["bdK"], Bo["bdA"]], [p])
                CP("act", LakT[:W2, :W2], p[:W2, :W2], [p], [LakT])
                TT("pool", LakT[:W2, :W2], LakT[:W2, :W2], su, ALU.mult, [LakT, MK], [LakT])
                yield
                p = nb()
                MM(p[:W2, :W2], B_, R_, True, True, [Bo["bdB"], Bo["bdR"]], [p])
                STT(nMrbT[:W2, :W2], p[:W2, :W2], -1.0, ui, ALU.mult, ALU.mult, [p, MK], [nMrbT])
                yield
                p = nb()
                MM(p[:W2, :W2], K_, R_, True, True, [Bo["bdK"], Bo["bdR"]], [p])
                CP("act", MrkT[:W2, :W2], p[:W2, :W2], [p], [MrkT])
                TT("pool", MrkT[:W2, :W2], MrkT[:W2, :W2], ui, ALU.mult, [MrkT, MK], [MrkT])
                yield
                for (src, dstm, eng) in ((Fo["fV"], Vbd, "act"), (Fo["fB"], nBd, "dve"), (Fo["fK"], Kd, "act"), (Fo["fA"], Atm, "dve")):
                    p = nb()
                    TR(p[:W2, :128], src[:, ws], ident[:, :], [src, ident], [p])
                    CP(eng, dstm[:W2, :], p[:W2, :128], [p], [dstm])
                    yield
                p = nb()
                MM(p[:W2, :128], LakT[:W2, :W2], Vbd[:W2, :], True, True, [LakT, Vbd], [p])
                CP("act", LV[:W2, :], p[:W2, :128], [p], [LV])
                yield
                Nc, Ntc, Pc = N0, Nt0, P0
                oth = {id(N0): Na, id(Na): N0, id(Nt0): Nta, id(Nta): Nt0, id(P0): Pa, id(Pa): P0}
                for i in range(nsteps):
                    nN, nNt, nP = oth[id(Nc)], oth[id(Ntc)], oth[id(Pc)]
                    q1 = nb()
                    MM(q1[:W2, :W2], Nc[:W2, :W2], Ntc[:W2, :W2], True, True, [Nc, Ntc], [q1])
                    CP("act" if i % 2 == 1 else "dve", nNt[:W2, :W2], q1[:W2, :W2], [q1], [nNt])
                    if i < nsteps - 1:
                        q0 = nb()
                        MM(q0[:W2, :W2], Ntc[:W2, :W2], Nc[:W2, :W2], True, True, [Ntc, Nc], [q0])
                        CP("dve", nN[:W2, :W2], q0[:W2, :W2], [q0], [nN])
                    yield
                    q2 = nb()
                    MM(q2[:W2, :W2], nNt[:W2, :W2], Pc[:W2, :W2], True, False, [nNt, Pc], [q2])
                    MM(q2[:W2, :W2], identb[:W2, :W2], Pc[:W2, :W2], False, True, [identb, Pc], [q2])
                    CP("act" if i % 2 == 0 else "dve", nP[:W2, :W2], q2[:W2, :W2], [q2], [nP])
                    yield
                    Nc, Ntc, Pc = nN, nNt, nP
                Tt = Pc
                p = nb()
                MM(p[:W2, :128], Tt[:W2, :W2], Atm[:W2, :], True, True, [Tt, Atm], [p])
                CP("act", Wtm[:W2, :], p[:W2, :128], [p], [Wtm])
                p = nb()
                MM(p[:W2, :128], Tt[:W2, :W2], LV[:W2, :], True, True, [Tt, LV], [p])
                CP("dve", U0[:W2, :], p[:W2, :128], [p], [U0])
                yield
                p = nb()
                MM(p[:, :W2], Wtm[:W2, :], nMrbT[:W2, :W2], True, False, [Wtm, nMrbT], [p])
                MM(p[:, :W2], identb[:, :], R_, False, True, [identb, Bo["bdR"]], [p])
                CP("dve", Rhat[:, :W2], p[:, :W2], [p], [Rhat])
                p = nb()
                MM(p[:, :128], Wtm[:W2, :], nBd[:W2, :], True, True, [Wtm, nBd], [p])
                CP("act", PhiT[:, :], p[:, :128], [p], [PhiT])
                yield

            def phaseB(j, M):
                (N0, Nt0, Na, Nta, P0, Pa, LakT, nMrbT, MrkT, Vbd, nBd, Kd, Atm, Wtm, LV, U0, Rhat, PhiT) = M
                y0 = nb()
                MM(y0[:, :W2], U0[:W2, :], nMrbT[:W2, :W2], True, False, [U0, nMrbT], [y0])
                MM(y0[:, :W2], Vbd[:W2, :], MrkT[:W2, :W2], False, False, [Vbd, MrkT], [y0])
                MM(y0[:, :W2], STb[:, c, :], Rhat[:, :W2], False, True, [STb, Rhat], [y0])
                s0 = nb()
                MM(s0[:, :128], Kd[:W2, :], Vbd[:W2, :], True, False, [Kd, Vbd], [s0])
                MM(s0[:, :128], nBd[:W2, :], U0[:W2, :], False, False, [nBd, U0], [s0])
                MM(s0[:, :128], PhiT[:, :], STb[:, c, :], False, True, [PhiT, STb], [s0])
                STT(ST[:, c, :], ST[:, c, :], T[11][:, (j + 1) * C - 1:(j + 1) * C], s0[:, :128], ALU.mult, ALU.add,
                    [ST, T[11], s0], [ST])
                CP("act", STb[:, c, :], ST[:, c, :], [ST], [STb])
                CP("act", yT[0:64, j * C:(j + 1) * C], y0[0:64, 0:C], [y0], [yT])
                CP("act", yT[64:128, j * C:(j + 1) * C], y0[64:128, C:W2], [y0], [yT])

            GS = 4
            for g0 in range(0, nch, GS):
                js = list(range(g0, min(nch, g0 + GS)))
                Ms = {j: [Bo["q%d" % (18 * (j - g0) + i)] for i in range(18)] for j in js}
                gens = [phaseA(j, Ms[j]) for j in js]
                while gens:
                    for g_ in list(gens):
                        try:
                            next(g_)
                        except StopIteration:
                            gens.remove(g_)
                    if nxt is not None:
                        try:
                            next(nxt)
                        except StopIteration:
                            nxt = None
                for j in js:
                    phaseB(j, Ms[j])
            if nxt is not None:
                for _ in nxt:
                    pass
            t_ = nb()
            MM(t_[:, :N], blk[:, :], yT[:, :N], True, True, [blk, yT], [t_])
            STT(yT[:, :N], t_[:, :N], -1.0 / 64, yT[:, :N], ALU.mult, ALU.add, [t_, yT], [yT])
            TT("dve", T[14][:, :N], yT[:, :N], yT[:, :N], ALU.mult, [yT], [T[14]])
            t_ = nb()
            MM(t_[:, :N], blk[:, :], T[14][:, :N], True, True, [blk, T[14]], [t_])
            ACT(T[14][:, :N], t_[:, :N], AF.Ln, [t_], [T[14]], scale=1.0 / 64, bias=64e-5)
            ACT(T[14][:, :N], T[14][:, :N], AF.Exp, [T[14]], [T[14]], scale=-0.5)
            TT("dve", yT[:, :N], yT[:, :N], T[14][:, :N], ALU.mult, [yT, T[14]], [yT])
            TS("dve", yT[:, :N], yT[:, :N], lng[:, c:c + 1], lnb[:, c:c + 1], ALU.mult, ALU.add, [yT, lng, lnb], [yT])
            TT("dve", yT[:, :N], yT[:, :N], T[9][:, :N], ALU.add, [yT, T[9]], [yT])
            TT("dve", Bo["rwT"][:, c, :N], yT[:, :N], T[5][:, :N], ALU.mult, [yT, T[5]], [Bo["rwT"]])
        Wo = I["o_w_out"]
        for half in range(2):
            wt = wtm[rot("wtm", 2)]
            DMA("pool", wt[:, :, :], Wo[:, half * 512:(half + 1) * 512].rearrange("(k p) n -> p k n", p=128), [], [wt])
            for m in range(4):
                bk = banks[rot("bk4", 4)]
                for kc in range(8):
                    rhs = Bo["hgT"][:, kc, :N] if kc < 4 else Bo["rwT"][:, kc - 4, :N]
                    MM(bk[:, :N], wt[:, kc, m * 128:(m + 1) * 128], rhs, kc == 0, kc == 7, [wt, Bo["hgT"], Bo["rwT"]], [bk])
                cc_ = half * 4 + m
                TT("dve", x[:, cc_, :N], x[:, cc_, :N], bk[:, :N], ALU.add, [x, bk], [x])

    def odd_init(S, sample):
        if sample:
            DMA("sp", S_hg[:, :, :], I["state_hgrn_S"][:, :, :].rearrange("h k v -> k h v"), [], [S_hg])
            MEMSET("dve", ST[:, :, :], 0.0, [ST])
            for hd in range(8):
                pp, hh = hd // 2, hd % 2
                DMA("sp", ST[hh * 64:(hh + 1) * 64, pp, hh * 64:(hh + 1) * 64], I["state_rwkv_S"][hd].rearrange("v k -> k v"),
                    [], [ST], slow=True)
            DMA("sp", zprev[:, :], I["state_rwkv_shift"][:].rearrange("(c p) -> p c", p=128), [], [zprev], slow=True)
        else:
            MEMSET("dve", S_hg[:, :, :], 0.0, [S_hg])
            MEMSET("dve", ST[:, :, :], 0.0, [ST])
            MEMSET("dve", zprev[:, :], 0.0, [zprev])
        CP("pool", S_hgb[:, :, :], S_hg[:, :, :], [S_hg], [S_hgb])
        CP("pool", STb[:, :, :], ST[:, :, :], [ST], [STb])

    def odd_finish(S):
        DMA("sp", S["hgrn_S"][:, :, :].rearrange("h k v -> k h v"), S_hg[:, :, :], [S_hg], [])
        for hd in range(8):
            pp, hh = hd // 2, hd % 2
            DMA("sp", S["rwkv_S"][hd].rearrange("v k -> k v"), ST[hh * 64:(hh + 1) * 64, pp, hh * 64:(hh + 1) * 64],
                [ST], [], slow=True)
        DMA("sp", S["rwkv_shift"][:].rearrange("(c p) -> p c", p=128), zprev[:, :], [zprev], [], slow=True)

    def tile(S, src, dst, t0, N):
        S["t0"] = t0
        load_x(src, t0, N)
        for layer in range(nlayers):
            P.barrier(grp_all, dummy)
            ffn(N, layer, 0)
            P.barrier(grp_all, dummy)
            if layer == 0:
                MEMSET("pool", B["Vt"][:, :, :, 64:66], 1.0, [B["Vt"]])
                init_heads()
                even_mixer(N, S)
            else:
                odd_mixer(N, S)
            P.barrier(grp_all, dummy)
            ffn(N, layer, 1)
        store_x(dst, t0, N)

    S = {"fox_k": O["fox_k_p"], "fox_v": O["fox_v_p"], "fox_lf": O["fox_logf_p"], "lru_conv": O["lru_conv_p"],
         "lru_h": O["lru_h_p"], "key_base": 0, "hgrn_S": O["hgrn_S_p"], "rwkv_S": O["rwkv_S_p"],
         "rwkv_shift": O["rwkv_shift_p"]}
    even_init(S, False)
    odd_init(S, False)
    for t in range(SEQ // NT):
        S["key_base"] = t * NT
        tile(S, I["x_prompt"], O["y_prompt"], t * NT, NT)
    even_finish(S)
    odd_finish(S)
    if do_sample:
        S = {"fox_k": O["fox_k_s"], "fox_v": O["fox_v_s"], "fox_lf": O["fox_logf_s"], "lru_conv": O["lru_conv_s"],
             "lru_h": O["lru_h_s"], "key_base": PAST, "hgrn_S": O["hgrn_S_s"], "rwkv_S": O["rwkv_S_s"],
             "rwkv_shift": O["rwkv_shift_s"]}
        P.barrier(grp_all, dummy)
        even_init(S, True)
        odd_init(S, True)
        init_heads()
        ingest_past(PAST)
        tile(S, I["x_sample"], O["y_sample"], 0, DSEQ)
        even_finish(S)
        odd_finish(S)

    P.finish()
    stack.close()
    return nc


def consts():
    k = np.arange(128)
    q = np.arange(512)
    mask = np.zeros((128, 4, 512), np.float32)
    for kk in range(4):
        mask[:, kk, :] = np.where((kk * 128 + k)[:, None] <= q[None, :], 0.0, -30000.0)
    def bdmask(C, fn):
        m = np.zeros((2 * C, 2 * C), np.float32)
        i = np.arange(C)
        blkm = fn(i[:, None], i[None, :]).astype(np.float32)
        m[:C, :C] = blkm
        m[C:, C:] = blkm
        return m
    def m3(C):
        return np.stack([bdmask(C, lambda j, t: t > j), bdmask(C, lambda t, j: t > j), bdmask(C, lambda j, t: t >= j),
                         -bdmask(C, lambda j, t: t > j)], axis=1)
    blk = np.zeros((128, 128), np.float32)
    blk[:64, :64] = 1.0
    blk[64:, 64:] = 1.0
    return {"c_m64": m3(64), "c_m16": m3(16), "c_blk": blk,
            "c_ones": np.full((128, 128), 1.0 / 1024, np.float32),
            "c_ident": np.eye(128, dtype=np.float32),
            "c_utri": np.triu(np.ones((128, 128), np.float32)),
            "c_mask": mask}


OUT_NAMES = ["y_prompt", "y_sample", "lru_conv_p", "lru_conv_s", "lru_h_p", "lru_h_s", "fox_k_p", "fox_k_s", "fox_v_p",
             "fox_v_s", "fox_logf_p", "fox_logf_s", "hgrn_S_p", "hgrn_S_s", "rwkv_shift_p", "rwkv_shift_s", "rwkv_S_p",
             "rwkv_S_s"]


def percore_inputs(inp, b):
    f = lambda a: np.ascontiguousarray(np.asarray(a), dtype=np.float32)
    m = {"x_prompt": f(inp["x_prompt"][b]), "x_sample": f(inp["x_sample"][b]),
         "state_lru_conv": f(inp["state_lru_conv"][0, b]), "state_lru_h": f(inp["state_lru_h"][0, b]),
         "cache_fox_k": f(inp["cache_fox_k"][0, b]).reshape(-1, G), "cache_fox_v": f(inp["cache_fox_v"][0, b]).reshape(-1, G),
         "cache_fox_logf": f(inp["cache_fox_logf"][0, b]), "state_hgrn_S": f(inp["state_hgrn_S"][0, b]),
         "state_rwkv_shift": f(inp["state_rwkv_shift"][0, b]), "state_rwkv_S": f(inp["state_rwkv_S"][0, b]),
         "norm_g": f(inp["norm_g"]), "ffn_w_in": f(inp["ffn_w_in"]), "ffn_w_out": f(inp["ffn_w_out"]),
         "hg_lb_logits": f(inp["hg_lb_logits"]), "rw_rk": f(inp["rw_rk"][0]).reshape(-1)}
    for k in ("e_w_in", "e_w_out", "lru_conv_w", "lru_conv_b", "lru_wa", "lru_ba", "lru_wx", "lru_bx", "lru_lambda",
              "fox_q_gain", "fox_k_gain", "fox_f_bias", "o_w_in", "o_w_out", "hg_norm_g", "rw_mu", "rw_w0", "rw_w2",
              "rw_a0", "rw_a2", "rw_g2", "rw_kk", "rw_ka", "rw_ln_g", "rw_ln_b"):
        m[k] = f(inp[k][0])
    m.update(consts())
    return m


_NC_CACHE = {}


def kernel(**inputs):
    SEQ = inputs["x_prompt"].shape[1]
    NB = inputs["x_prompt"].shape[0]
    if SEQ not in _NC_CACHE:
        _NC_CACHE[SEQ] = build(SEQ)
    nc = _NC_CACHE[SEQ]
    in_maps = [percore_inputs(inputs, b) for b in range(NB)]
    res = run_bass_kernel_spmd(nc, in_maps, core_ids=list(range(NB))).results
    outs = []
    for nm in OUT_NAMES:
        a = np.stack([np.asarray(r[nm], dtype=np.float32) for r in res], axis=0)
        if nm.startswith("y_"):
            outs.append(a)
        elif nm.startswith("fox_k") or nm.startswith("fox_v"):
            outs.append(a.reshape(1, NB, a.shape[1], 8, 64))
        else:
            outs.append(a.reshape((1, NB) + a.shape[1:]))
    return tuple(outs)
```

```python
import contextlib
import numpy as np
import concourse.bass as bass
import concourse.mybir as mybir
from concourse.bass_utils import run_bass_kernel_spmd

F32 = mybir.dt.float32
BF16 = mybir.dt.bfloat16
AF = mybir.ActivationFunctionType
ALU = mybir.AluOpType
AX = mybir.AxisListType

D = 1024
DFF = 2816
NJ = DFF // 128
G = 512
ECOLS = 3080
RWC = 1792
OCOLS = 4 * G + RWC
EPS = 1e-6


class Buf:
    def __init__(self, t, name):
        self.t = t
        self.name = name
        self.lw = None
        self.pw = []
        self.rd = {}
        self.rdd = []
        self.psum = False

    def __getitem__(self, k):
        return self.t[k]

    def ap(self):
        return self.t


class Rec:
    __slots__ = ("eng", "fn", "dma", "deps", "ref", "val", "sem")

    def __init__(self, eng, fn, dma):
        self.eng = eng
        self.fn = fn
        self.dma = dma
        self.deps = set()
        self.ref = False
        self.val = None
        self.sem = None


ENGS = ["pe", "act", "dve", "pool", "sp"]
EPOCH = 16000
NDSEM = 8


class Prog:
    def __init__(self, nc, stack):
        self.nc = nc
        self.stack = stack
        self.ins = {e: [] for e in ENGS}
        self.nbuf = 0

    def sb(self, name, shape, dt):
        t = self.stack.enter_context(self.nc.sbuf_tensor(name, list(shape), dt))
        return Buf(t, name)

    def ps(self, name, shape, dt=F32):
        t = self.stack.enter_context(self.nc.psum_tensor(name, list(shape), dt))
        b = Buf(t, name)
        b.psum = True
        return b

    def dram(self, name, shape, dt, kind):
        t = self.nc.dram_tensor(name, list(shape), dt, kind=kind)
        return Buf(t.ap(), name)

    def view(self, arena, name, off, shape):
        n = 1
        for d in shape[1:]:
            n *= d
        ap = arena.t[0:shape[0], off:off + n]
        if len(shape) == 3:
            ap = ap.rearrange("p (a b) -> p a b", b=shape[2])
        elif len(shape) == 4:
            ap = ap.rearrange("p (a b c) -> p a b c", b=shape[2], c=shape[3])
        return Buf(ap, name)

    def barrier(self, bufs, dummy):
        self.op("dve", lambda e: e.memset(dummy[0:1, 0:1], 0.0), [], list(bufs) + [dummy])

    def op(self, eng, fn, reads=(), writes=(), dma=False, pwrites=()):
        rec = Rec(eng, fn, dma)
        for b in reads:
            if b.lw is not None:
                rec.deps.add(b.lw)
            for tk in b.pw:
                rec.deps.add(tk)
            if b.psum:
                for e2, i2 in b.rd.items():
                    if e2 != eng:
                        rec.deps.add((e2, i2))
        for b in writes:
            if b.lw is not None:
                rec.deps.add(b.lw)
            for tk in b.pw:
                rec.deps.add(tk)
            for e2, i2 in b.rd.items():
                rec.deps.add((e2, i2))
            for tk in b.rdd:
                rec.deps.add(tk)
        for b in pwrites:
            if b.lw is not None:
                rec.deps.add(b.lw)
            for e2, i2 in b.rd.items():
                rec.deps.add((e2, i2))
            for tk in b.rdd:
                rec.deps.add(tk)
        idx = len(self.ins[eng])
        self.ins[eng].append(rec)
        tok = (eng, idx)
        for b in writes:
            b.lw = tok
            b.pw = []
            b.rd = {}
            b.rdd = []
        for b in pwrites:
            b.pw.append(tok)
        for b in reads:
            if dma:
                b.rdd.append(tok)
            else:
                if b.rd.get(eng, -1) < idx:
                    b.rd[eng] = idx
        return tok

    def finish(self):
        nc = self.nc
        ins = self.ins
        for e in ENGS:
            for i, r in enumerate(ins[e]):
                nd = set()
                for (e2, i2) in r.deps:
                    if e2 == e and i2 == i:
                        continue
                    r2 = ins[e2][i2]
                    if e2 == e and e == "pe":
                        continue
                    nd.add((e2, i2))
                    r2.ref = True
                r.deps = nd
        esems = {e: [] for e in ENGS}
        dsems = {}
        for e in ENGS:
            cnt = 0
            nd = 0
            for r in ins[e]:
                if r.dma:
                    k = nd % NDSEM
                    r.sem = ("d", e, k)
                    r.val = 16 * (nd // NDSEM + 1)
                    nd += 1
                elif r.ref:
                    ep = cnt // EPOCH
                    r.sem = ("e", e, ep)
                    r.val = cnt % EPOCH + 1
                    cnt += 1
            nep = (cnt + EPOCH - 1) // EPOCH
            for ep in range(max(nep, 1)):
                esems[e].append(self.stack.enter_context(nc.semaphore(f"s_{e}_{ep}")))
            if nd:
                dsems[e] = [self.stack.enter_context(nc.semaphore(f"d_{e}_{k}")) for k in range(NDSEM)]

        def semh(key):
            if key[0] == "e":
                return esems[key[1]][key[2]]
            return dsems[key[1]][key[2]]

        final_waits = []
        for e in ("sp", "pool"):
            if e in dsems:
                last = {}
                for r in ins[e]:
                    if r.dma:
                        last[r.sem] = r.val
                final_waits += list(last.items())

        block = self.stack.enter_context(nc.Block())

        def run(e, eng):
            waited = {}
            for r in ins[e]:
                need = {}
                for (e2, i2) in r.deps:
                    r2 = ins[e2][i2]
                    if need.get(r2.sem, 0) < r2.val:
                        need[r2.sem] = r2.val
                if r.dma and r.val > 16:
                    if need.get(r.sem, 0) < r.val - 16:
                        need[r.sem] = r.val - 16
                for sk, v in need.items():
                    if waited.get(sk, 0) < v:
                        eng.wait_ge(semh(sk), v)
                        waited[sk] = v
                i = r.fn(eng)
                if r.dma:
                    i.then_inc(semh(r.sem), 16)
                elif r.ref:
                    i.then_inc(semh(r.sem), 1)
            if e == "sp":
                for sk, v in final_waits:
                    if waited.get(sk, 0) < v:
                        eng.wait_ge(semh(sk), v)

        @block.tensor
        def _(eng):
            run("pe", eng)

        @block.scalar
        def _(eng):
            run("act", eng)

        @block.vector
        def _(eng):
            run("dve", eng)

        @block.gpsimd
        def _(eng):
            run("pool", eng)

        @block.sync
        def _(eng):
            run("sp", eng)


def cdiv(a, b):
    return (a + b - 1) // b


def build(SEQ, DSEQ=16, PAST=2048, NT=512, do_sample=True, nlayers=2, stage=99, sub=99, vmode=0):
    nc = bass.Bass("TRN2", target_bir_lowering=False)
    stack = contextlib.ExitStack()
    P = Prog(nc, stack)
    TK = max(SEQ, PAST + 128)
    KTMAX = TK // 128

    def din(name, shape):
        return P.dram(name, shape, F32, "ExternalInput")

    def dout(name, shape):
        return P.dram(name, shape, F32, "ExternalOutput")

    I = {}
    for nm, sh in [("x_prompt", [SEQ, D]), ("x_sample", [DSEQ, D]), ("state_lru_conv", [3, G]), ("state_lru_h", [G]),
                   ("cache_fox_k", [PAST, G]), ("cache_fox_v", [PAST, G]), ("cache_fox_logf", [PAST, 8]),
                   ("state_hgrn_S", [4, 128, 128]), ("state_rwkv_shift", [RWC]), ("state_rwkv_S", [8, 64, 64]),
                   ("norm_g", [2, 3, D]), ("ffn_w_in", [2, 2, D, 2 * DFF]), ("ffn_w_out", [2, 2, DFF, D]),
                   ("e_w_in", [D, ECOLS]), ("e_w_out", [D, D]), ("lru_conv_w", [4, G]), ("lru_conv_b", [G]),
                   ("lru_wa", [8, 64, 64]), ("lru_ba", [G]), ("lru_wx", [8, 64, 64]), ("lru_bx", [G]),
                   ("lru_lambda", [G]), ("fox_q_gain", [64]), ("fox_k_gain", [64]), ("fox_f_bias", [8]),
                   ("o_w_in", [D, OCOLS]), ("o_w_out", [D, D]), ("hg_lb_logits", [2, G]), ("hg_norm_g", [G]),
                   ("rw_mu", [RWC]), ("rw_w0", [G]), ("rw_w2", [64, G]), ("rw_a0", [G]), ("rw_a2", [64, G]),
                   ("rw_g2", [128, G]), ("rw_kk", [G]), ("rw_ka", [G]), ("rw_rk", [G]), ("rw_ln_g", [G]),
                   ("rw_ln_b", [G]),
                   ("c_ones", [128, 128]), ("c_ident", [128, 128]), ("c_utri", [128, 128]), ("c_mask", [128, 4, 512]), ("c_m64", [128, 4, 128]), ("c_m16", [32, 4, 32]),
                   ("c_blk", [128, 128])]:
        I[nm] = din(nm, sh)
    O = {}
    for nm, sh in [("y_prompt", [SEQ, D]), ("y_sample", [DSEQ, D]), ("lru_conv_p", [3, G]), ("lru_conv_s", [3, G]),
                   ("lru_h_p", [G]), ("lru_h_s", [G]), ("fox_k_p", [SEQ, G]), ("fox_k_s", [DSEQ, G]),
                   ("fox_v_p", [SEQ, G]), ("fox_v_s", [DSEQ, G]), ("fox_logf_p", [SEQ, 8]), ("fox_logf_s", [DSEQ, 8]),
                   ("hgrn_S_p", [4, 128, 128]), ("hgrn_S_s", [4, 128, 128]), ("rwkv_shift_p", [RWC]),
                   ("rwkv_shift_s", [RWC]), ("rwkv_S_p", [8, 64, 64]), ("rwkv_S_s", [8, 64, 64])]:
        O[nm] = dout(nm, sh)
    KT_scr = P.dram("kt_scr", [8, 128, TK], BF16, "Internal")
    VW = 72
    V_scr = P.dram("v_scr", [8, 128, KTMAX, VW], BF16, "Internal")

    def MM(o, l, r, st, sp, R, W):
        P.op("pe", lambda e: e.matmul(o, lhsT=l, rhs=r, start=st, stop=sp), R, W)

    def TR(o, i, idn, R, W):
        P.op("pe", lambda e: e.transpose(o, i, idn), R, W)

    def ACT(o, i, f, R, W, bias=None, scale=None):
        kw = {}
        if bias is not None:
            kw["bias"] = bias
        if scale is not None:
            kw["scale"] = scale
        P.op("act", lambda e: e.activation(out=o, in_=i, func=f, **kw), R, W)

    def TT(eng, o, a, b, op, R, W):
        P.op(eng, lambda e: e.tensor_tensor(out=o, in0=a, in1=b, op=op), R, W)

    def TS(eng, o, a, s1, s2, op0, op1, R, W):
        if s2 is None:
            P.op(eng, lambda e: e.tensor_scalar(out=o, in0=a, scalar1=s1, scalar2=None, op0=op0), R, W)
        else:
            P.op(eng, lambda e: e.tensor_scalar(out=o, in0=a, scalar1=s1, scalar2=s2, op0=op0, op1=op1), R, W)

    def STT(o, a, sc, b, op0, op1, R, W):
        P.op("dve", lambda e: e.scalar_tensor_tensor(out=o, in0=a, scalar=sc, in1=b, op0=op0, op1=op1), R, W)

    def CP(eng, o, i, R, W):
        if eng == "act":
            P.op("act", lambda e: e.copy(out=o, in_=i), R, W)
        else:
            P.op(eng, lambda e: e.tensor_copy(out=o, in_=i), R, W)

    def DMA(q, o, i, R, W, slow=False, PW=()):
        if slow:
            P.op(q, lambda e: e.dma_start(out=o, in_=i, allow_slow_non_contiguous=True), R, W, dma=True, pwrites=PW)
        else:
            P.op(q, lambda e: e.dma_start(out=o, in_=i), R, W, dma=True, pwrites=PW)

    def RECIP(o, i, R, W):
        P.op("dve", lambda e: e.reciprocal(out=o, in_=i), R, W)

    def MEMSET(eng, o, v, W):
        P.op(eng, lambda e: e.memset(o, v), [], W)

    cnt = {}

    def rot(key, n):
        v = cnt.get(key, 0)
        cnt[key] = v + 1
        return v % n

    ones_f = P.sb("ones_f", [128, 128], F32)
    ones_fb = P.sb("ones_fb", [128, 128], BF16)
    ones1 = P.sb("ones1", [128, 128], F32)
    ident = P.sb("ident", [128, 128], F32)
    identb = P.sb("identb", [128, 128], BF16)
    utri = P.sb("utri", [128, 128], F32)
    maskb = P.sb("maskb", [128, 4, 512], BF16)
    ones3 = P.sb("ones3", [3, 128], BF16)
    normg = P.sb("normg", [128, 6, 8], F32)
    dummy = P.sb("dummy_bar", [128, 8], F32)
    DMA("sp", ones_f[:], I["c_ones"][:], [], [ones_f])
    DMA("pool", ones_fb[:], I["c_ones"][:], [], [ones_fb])
    DMA("sp", ident[:], I["c_ident"][:], [], [ident])
    DMA("sp", utri[:], I["c_utri"][:], [], [utri])
    DMA("pool", identb[:], I["c_ident"][:], [], [identb])
    DMA("pool", maskb[:], I["c_mask"][:], [], [maskb])
    MEMSET("dve", ones3[:], 1.0, [ones3])
    MEMSET("dve", ones1[:], 1.0, [ones1])
    DMA("sp", normg[:], I["norm_g"][:].rearrange("l w (c p) -> p (l w) c", p=128), [], [normg], slow=True)

    def colvec(name, src, n):
        t = P.sb(name, [128, n], F32)
        DMA("sp", t[:], src.rearrange("(c p) -> p c", p=128), [], [t], slow=True)
        return t

    convb = colvec("convb", I["lru_conv_b"][:], 4)
    lba = colvec("lba", I["lru_ba"][:], 4)
    lbx = colvec("lbx", I["lru_bx"][:], 4)
    lam = colvec("lam", I["lru_lambda"][:], 4)
    convw = P.sb("convw", [128, 4, 4], F32)
    for j in range(4):
        DMA("sp", convw[:, :, j], I["lru_conv_w"][j, :].rearrange("(c p) -> p c", p=128), [], [convw], slow=True)
    c1 = P.sb("c1", [128, 4], F32)
    c2 = P.sb("c2", [128, 4], F32)
    ACT(c1[:], lam[:], AF.Exp, [lam], [c1], scale=-1.0)
    ACT(c1[:], c1[:], AF.Ln, [c1], [c1], bias=1.0)
    TS("dve", c2[:], c1[:], -16.0, None, ALU.mult, None, [c1], [c2])
    TS("dve", c1[:], c1[:], -8.0, None, ALU.mult, None, [c1], [c1])
    bda = P.sb("bda", [128, 4, 128], BF16)
    bdx = P.sb("bdx", [128, 4, 128], BF16)
    for bd, src in ((bda, I["lru_wa"]), (bdx, I["lru_wx"])):
        MEMSET("pool", bd[:], 0.0, [bd])
        for c in range(4):
            DMA("pool", bd[0:64, c, 0:64], src[2 * c], [], [bd])
            DMA("pool", bd[64:128, c, 64:128], src[2 * c + 1], [], [bd])
    gq = P.sb("gq", [128, 64], F32)
    gk = P.sb("gk", [128, 64], F32)
    fbias = P.sb("fbias", [128, 8], F32)
    DMA("sp", gq[:], I["fox_q_gain"][:].partition_broadcast(128), [], [gq])
    DMA("sp", gk[:], I["fox_k_gain"][:].partition_broadcast(128), [], [gk])
    DMA("sp", fbias[:], I["fox_f_bias"][:].partition_broadcast(128), [], [fbias])
    TS("dve", gq[:], gq[:], 0.125, None, ALU.mult, None, [gq], [gq])
    wfl = P.sb("wfl", [128, 8, 8], BF16)
    DMA("pool", wfl[:], I["e_w_in"][:, 3072:3080].rearrange("(k p) n -> p k n", p=128), [], [wfl], slow=True)


    m64 = P.sb("m64", [128, 4, 128], F32)
    m16 = P.sb("m16", [32, 4, 32], F32)
    blk = P.sb("blk", [128, 128], F32)
    DMA("sp", m64[:], I["c_m64"][:], [], [m64])
    DMA("sp", m16[:], I["c_m16"][:], [], [m16])
    DMA("sp", blk[:], I["c_blk"][:], [], [blk])
    ones_row = P.sb("ones_row", [128, 64], F32)
    MEMSET("dve", ones_row[:], 1.0, [ones_row])
    lb0 = colvec("lb0", I["hg_lb_logits"][0, :], 4)
    lb = colvec("lb", I["hg_lb_logits"][1, :], 4)
    oml = P.sb("oml", [128, 4], F32)
    TT("dve", lb[:], lb[:], lb0[:], ALU.subtract, [lb, lb0], [lb])
    ACT(lb[:], lb[:], AF.Sigmoid, [lb], [lb])
    TS("dve", oml[:], lb[:], -1.0, 1.0, ALU.mult, ALU.add, [lb], [oml])
    hgn = colvec("hgn", I["hg_norm_g"][:], 4)
    mu = colvec("mu", I["rw_mu"][:], 14)
    omu = P.sb("omu", [128, 14], F32)
    TS("dve", omu[:], mu[:], -1.0, 1.0, ALU.mult, ALU.add, [mu], [omu])
    nw0 = colvec("nw0", I["rw_w0"][:], 4)
    TS("dve", nw0[:], nw0[:], -1.0, None, ALU.mult, None, [nw0], [nw0])
    a0 = colvec("a0", I["rw_a0"][:], 4)
    kkw = colvec("kkw", I["rw_kk"][:], 4)
    kaw = colvec("kaw", I["rw_ka"][:], 4)
    omka = P.sb("omka", [128, 4], F32)
    TS("dve", omka[:], kaw[:], -1.0, 1.0, ALU.mult, ALU.add, [kaw], [omka])
    rkw = colvec("rkw", I["rw_rk"][:], 4)
    lng = colvec("lng", I["rw_ln_g"][:], 4)
    lnb = colvec("lnb", I["rw_ln_b"][:], 4)
    w2t = P.sb("w2t", [64, G], BF16)
    a2t = P.sb("a2t", [128, G], BF16)
    g2t = P.sb("g2t", [128, G], BF16)
    DMA("pool", w2t[:, :], I["rw_w2"][:, :], [], [w2t])
    DMA("pool", a2t[64:128, :], I["rw_a2"][:, :], [], [a2t])
    DMA("pool", g2t[:, :], I["rw_g2"][:, :], [], [g2t])
    S_hg = P.sb("S_hg", [128, 4, 128], F32)
    S_hgb = P.sb("S_hgb", [128, 4, 128], BF16)
    ST = P.sb("ST_rw", [128, 4, 128], F32)
    STb = P.sb("ST_rwb", [128, 4, 128], BF16)
    zprev = P.sb("zprev", [128, 14], F32)


    WQ = "sp"
    ffi_t = nc.dram_tensor("ffn_in_b", [2, 2, NJ, 128, 2048], BF16, kind="Internal").ap()
    ffo_t = nc.dram_tensor("ffn_out_b", [2, 2, 2, NJ // 2, 128, 1024], BF16, kind="Internal").ap()
    ffi_b, ffo_b = {}, {}
    mixw = {}
    conv_done = set()
    mix_t = {}
    for nm, ncol in (("e_w_in", 3072), ("o_w_in", OCOLS)):
        mix_t[nm] = nc.dram_tensor(nm + "_b", [ncol // 128, 128, 1024], BF16, kind="Internal").ap()
    for nm, c0s in (("e_w_in", (1024, 1536, 2048, 2560)), ("e_w_out", (0, 512)), ("o_w_out", (0, 512))):
        mix_t[nm + "_t"] = nc.dram_tensor(nm + "_tb", [len(c0s), 128, 4096], BF16, kind="Internal").ap()

    def convert(key):
        if key in conv_done:
            return
        conv_done.add(key)
        if key[0] == "ffn":
            _, l, w = key
            for j in range(NJ):
                bb = Buf(ffi_t[l, w, j], "ffi")
                ffi_b[(l, w, j)] = bb
                for gu in range(2):
                    c0 = gu * DFF + j * 128
                    DMA("pool", bb[:, :].rearrange("p (c g n) -> p c g n", g=2, n=128)[:, :, gu, :],
                        I["ffn_w_in"][l, w, :, c0:c0 + 128].rearrange("(c p) n -> p c n", p=128), [], [], PW=[bb])
            for half in range(2):
                for j2 in range(NJ // 2):
                    bb = Buf(ffo_t[l, w, half, j2], "ffo")
                    ffo_b[(l, w, half, j2)] = bb
                    DMA("pool", bb[:, :].rearrange("p (a n) -> p a n", a=2),
                        I["ffn_w_out"][l, w, j2 * 256:(j2 + 1) * 256, half * 512:(half + 1) * 512].rearrange("(a p) n -> p a n", p=128),
                        [], [], PW=[bb])
        else:
            for nm, ncol in ((("e_w_in", 3072),) if key[0] == "even" else (("o_w_in", OCOLS),)):
                for c in range(ncol // 128):
                    bb = Buf(mix_t[nm][c], nm + "_b")
                    mixw[(nm, c)] = bb
                    DMA("pool", bb[:, :].rearrange("p (k n) -> p k n", n=128),
                        I[nm][:, c * 128:(c + 1) * 128].rearrange("(k p) n -> p k n", p=128), [], [], PW=[bb])
            groups = (("e_w_in", (1024, 1536, 2048, 2560)), ("e_w_out", (0, 512))) if key[0] == "even" else (("o_w_out", (0, 512)),)
            for nm, c0s in groups:
                for gi_, c0 in enumerate(c0s):
                    bb = Buf(mix_t[nm + "_t"][gi_], nm + "_tb")
                    mixw[(nm + "_t", c0)] = bb
                    DMA("pool", bb[:, :].rearrange("p (k n) -> p k n", n=512),
                        I[nm][:, c0:c0 + 512].rearrange("(k p) n -> p k n", p=128), [], [], PW=[bb])

    x = P.sb("x", [128, 8, NT], F32)
    h = P.sb("h", [128, 8, NT], BF16)
    sqt = [P.sb(f"sqt{i}", [128, NT], BF16) for i in range(4)]
    rstd = P.sb("rstd", [128, NT], F32)
    sg = [P.sb(f"sg{i}", [128, NT], F32) for i in range(2)]
    NWB = 3
    wib = [P.sb(f"wib{i}", [128, 8, 2, 128], BF16) for i in range(NWB)]
    wob = [P.sb(f"wob{i}", [128, 2, 512], BF16) for i in range(NWB)]
    wtm = [P.sb(f"wtm{i}", [128, 8, 512], BF16) for i in range(2)]
    xtms = [P.sb(f"xtm{i}", [128, D], F32) for i in range(2)]
    banks = [P.ps(f"bank{i}", [128, 512], F32) for i in range(8)]
    hcar = P.sb("hcar", [128, 4], F32)
    xhist = P.sb("xhist", [128, 4, 3], F32)
    Rsum = P.sb("Rsum", [128, 8], F32)
    negF = P.sb("negF", [128, KTMAX, 8], F32)
    ktl = [P.sb(f"ktl{i}", [128, 512], BF16) for i in range(4)]
    vl = [P.sb(f"vl{i}", [128, 4, VW], BF16) for i in range(4)]
    PT = [P.sb(f"PT{i}", [128, 512], BF16) for i in range(3)]
    ABF = P.sb("arena_bf", [128, 20480], BF16)
    AF32 = P.sb("arena_f32", [128, 13056], F32)
    hid = P.view(ABF, "hid", 0, [128, NJ, NT])
    o = 0
    ev_bf = {}
    for nm, sh in [("xcb", [128, NT]), ("rnn", [128, 4, NT]), ("QT", [128, 8, NT]), ("KTt", [128, 8, NT]),
                   ("Vt", [128, 4, 8, VW]), ("FQ", [3, 8, NT]), ("foT", [128, 4, NT]), ("F3", [128, 4, 8, 3])]:
        n = int(np.prod(sh[1:]))
        ev_bf[nm] = P.view(ABF, "ev_" + nm, o, sh)
        o += n
    assert o <= 20480, o
    o = 0
    ev_f = {}
    for nm, sh in [("xp", [128, 4, NT + 3]), ("xc", [128, NT]), ("hs", [128, 4, NT]), ("t0", [128, NT]), ("t1", [128, NT]),
                   ("t2", [128, NT]), ("t3", [128, NT]), ("t4", [128, NT]), ("qn", [128, NT]), ("kn", [128, NT]),
                   ("vf", [128, NT]), ("ogs", [128, 4, NT]), ("fo", [128, 4, NT]), ("lf", [128, 4, 8]),
                   ("s8a", [128, 8]), ("s8b", [128, 8]), ("s8c", [128, 8]), ("rec", [128, 8])]:
        n = int(np.prod(sh[1:]))
        ev_f[nm] = P.view(AF32, "evf_" + nm, o, sh)
        o += n
    assert o <= 13056, o

    o = 0
    od_bf = {}
    for nm, sh in [("bdA", [128, 1024]), ("bdB", [128, 1024]), ("bdK", [128, 1024]), ("bdR", [128, 1024]),
                   ("hgT", [128, 4, NT]), ("rwT", [128, 4, NT]), ("b0", [128, NT]), ("b1", [128, NT]), ("b2", [128, NT]),
                   ("b3", [128, NT])] + [("q%d" % i, [128, 128]) for i in range(72)]:
        n = int(np.prod(sh[1:]))
        od_bf[nm] = P.view(ABF, "od_" + nm, o, sh)
        o += n
    assert o <= 20480, o
    o = 0
    od_f = {}
    for nm, sh in ([("T%d" % i, [128, NT + 8]) for i in range(16)] +
                   [("fV", [128, 1024]), ("fB", [128, 1024]), ("fK", [128, 1024]), ("fA", [128, 1024])]):
        n = int(np.prod(sh[1:]))
        od_f[nm] = P.view(AF32, "odf_" + nm, o, sh)
        o += n
    assert o <= 13056, o
    grp_odd = list(od_bf.values()) + list(od_f.values())
    grp_ffn = [hid]
    grp_even = list(ev_bf.values()) + list(ev_f.values())
    grp_all = grp_even + grp_ffn + grp_odd

    def rmsnorm(N, gi):
        b = banks[7]
        for c in range(8):
            s = sqt[rot("sqt", 4)]
            if c % 2 == 0:
                ACT(s[:, :N], x[:, c, :N], AF.Square, [x], [s])
            else:
                TT("dve", s[:, :N], x[:, c, :N], x[:, c, :N], ALU.mult, [x], [s])
            MM(b[:, :N], ones_fb[:], s[:, :N], c == 0, c == 7, [ones_fb, s], [b])
        ACT(rstd[:, :N], b[:, :N], AF.Ln, [b], [rstd], bias=EPS)
        ACT(rstd[:, :N], rstd[:, :N], AF.Exp, [rstd], [rstd], scale=-0.5)
        for c in range(8):
            STT(h[:, c, :N], x[:, c, :N], normg[:, gi, c:c + 1], rstd[:, :N], ALU.mult, ALU.mult, [x, normg, rstd], [h])

    def ffn(N, layer, which):
        convert(("ffn", layer, which))
        rmsnorm(N, layer * 3 + (0 if which == 0 else 2))
        win = I["ffn_w_in"]
        wout = I["ffn_w_out"]
        for j in range(NJ):
            wb = wib[rot("wib", NWB)]
            src = ffi_b[(layer, which, j)]
            DMA(WQ, wb[:, :, :, :], src[:, :].rearrange("p (c g n) -> p c g n", g=2, n=128), [src], [wb])
            bg, bu = banks[(2 * j) % 4], banks[(2 * j + 1) % 4]
            for gu, bk in ((0, bg), (1, bu)):
                for c in range(8):
                    MM(bk[:, :N], wb[:, c, gu, :], h[:, c, :N], c == 0, c == 7, [wb, h], [bk])
            s = sg[rot("sg", 2)]
            ACT(s[:, :N], bg[:, :N], AF.Silu, [bg], [s])
            TT("dve", hid[:, j, :N], s[:, :N], bu[:, :N], ALU.mult, [s, bu], [hid])
        for half in range(2):
            acc = [banks[4 + m] for m in range(4)]
            for j in range(NJ):
                if j % 2 == 0:
                    wb = wob[rot("wob", NWB)]
                    src = ffo_b[(layer, which, half, j // 2)]
                    DMA(WQ, wb[:, :, :], src[:, :].rearrange("p (a n) -> p a n", a=2), [src], [wb])
                for m in range(4):
                    MM(acc[m][:, :N], wb[:, j % 2, m * 128:(m + 1) * 128], hid[:, j, :N], j == 0, j == NJ - 1, [wb, hid], [acc[m]])
            for m in range(4):
                c = half * 4 + m
                STT(x[:, c, :N], acc[m][:, :N], 0.5, x[:, c, :N], ALU.mult, ALU.add, [acc[m], x], [x])

    def load_x(src, t0, N):
        for s in range(cdiv(N, 128)):
            r = min(128, N - s * 128)
            xtm = xtms[rot("xtm", 2)]
            DMA("sp", xtm[:r, :], src[t0 + s * 128:t0 + s * 128 + r, :], [], [xtm])
            for c in range(8):
                bk = banks[rot("bk4", 4)]
                TR(bk[:, :r], xtm[:r, c * 128:(c + 1) * 128], ident[:r, :r], [xtm, ident], [bk])
                CP("dve" if c % 2 == 0 else "act", x[:, c, s * 128:s * 128 + r], bk[:, :r], [bk], [x])

    def store_x(dst, t0, N):
        for s in range(cdiv(N, 128)):
            r = min(128, N - s * 128)
            xtm = xtms[rot("xtm", 2)]
            for c in range(8):
                bk = banks[rot("bk4", 4)]
                TR(bk[:r, :128], x[:, c, s * 128:s * 128 + r], ident[:, :], [x, ident], [bk])
                CP("dve" if c % 2 == 0 else "act", xtm[:r, c * 128:(c + 1) * 128], bk[:r, :128], [bk], [xtm])
            DMA("sp", dst[t0 + s * 128:t0 + s * 128 + r, :], xtm[:r, :], [xtm], [])

    E = ev_f
    B = ev_bf

    def norm_heads(dst, src_ps, gain, r):
        ACT(E["t0"][:r, :], src_ps[:r, :], AF.Square, [src_ps], [E["t0"]])
        P.op("dve", lambda e: e.tensor_reduce(out=E["s8a"][:r, :], in_=E["t0"][:r, :].rearrange("p (h d) -> p h d", d=64),
                                              axis=AX.X, op=ALU.add), [E["t0"]], [E["s8a"]])
        TS("dve", E["s8a"][:r, :], E["s8a"][:r, :], 1.0 / 64, EPS, ALU.mult, ALU.add, [E["s8a"]], [E["s8a"]])
        ACT(E["s8a"][:r, :], E["s8a"][:r, :], AF.Sqrt, [E["s8a"]], [E["s8a"]])
        P.op("dve", lambda e: e.reciprocal(out=E["s8a"][:r, :], in_=E["s8a"][:r, :]), [E["s8a"]], [E["s8a"]])
        d3 = dst[:r, :].rearrange("p (h d) -> p h d", d=64)
        TT("dve", d3, src_ps[:r, :].rearrange("p (h d) -> p h d", d=64),
           E["s8a"][:r, :].unsqueeze(2).to_broadcast([r, 8, 64]), ALU.mult, [src_ps, E["s8a"]], [dst])
        TT("dve", d3, d3, gain[:r, :].unsqueeze(1).to_broadcast([r, 8, 64]), ALU.mult, [dst, gain], [dst])

    def to_featmajor(dstT, src, s, r, eng_alt=0):
        for c in range(4):
            bk = banks[rot("bk4", 4)]
            TR(bk[:, :r], src[:r, c * 128:(c + 1) * 128], ident[:r, :r], [src, ident], [bk])
            CP("act" if (c + eng_alt) % 2 == 0 else "dve", dstT[:, c, s * 128:s * 128 + r], bk[:, :r], [bk], [dstT])

    def to_heads(dstT, src, s, r, eng_alt=0):
        for c in range(4):
            bk = banks[rot("bk4", 4)]
            TR(bk[:, :r], src[:r, c * 128:(c + 1) * 128], ident[:r, :r], [src, ident], [bk])
            e0, e1 = ("act", "dve") if (c + eng_alt) % 2 == 0 else ("dve", "act")
            P.op(e0, (lambda e, o_=dstT[0:64, 2 * c, s * 128:s * 128 + r], i_=bk[0:64, :r], en=e0:
                      (e.copy(out=o_, in_=i_) if en == "act" else e.tensor_copy(out=o_, in_=i_))), [bk], [], pwrites=[dstT])
            P.op(e1, (lambda e, o_=dstT[64:128, 2 * c + 1, s * 128:s * 128 + r], i_=bk[64:128, :r], en=e1:
                      (e.copy(out=o_, in_=i_) if en == "act" else e.tensor_copy(out=o_, in_=i_))), [bk], [], pwrites=[dstT])

    def init_heads():
        MEMSET("pool", B["QT"][:, :, :], 0.0, [B["QT"]])
        MEMSET("pool", B["KTt"][:, :, :], 0.0, [B["KTt"]])
        kv = B["KTt"][:, :, :].rearrange("p (h two) n -> p h two n", two=2)
        P.op("pool", lambda e: e.memset(kv[64:67, :, 0, :], 1.0), [], [], pwrites=[B["KTt"]])
        P.op("pool", lambda e: e.memset(kv[0:3, :, 1, :], 1.0), [], [], pwrites=[B["KTt"]])

    def cumF(lf_ap, lfbuf, r, kt, s, want_fq):
        bk = banks[rot("bk4", 4)]
        MM(bk[:r, 0:8], utri[:r, :r], lf_ap, True, False, [utri, lfbuf], [bk])
        MM(bk[:r, 0:8], ones1[:, :r], Rsum[:, :], False, True, [ones1, Rsum], [bk])
        TS("dve", negF[:r, kt, :], bk[:r, 0:8], -1.0, None, ALU.mult, None, [bk], [negF])
        TT("pool", Rsum[:r, :], Rsum[:r, :], lf_ap, ALU.add, [Rsum, lfbuf], [Rsum])
        if want_fq:
            F3 = B["F3"]
            CP("dve", F3[:r, s, :, 0], bk[:r, 0:8], [bk], [F3])
            CP("dve", E["s8b"][:r, :], F3[:r, s, :, 0], [F3], [E["s8b"]])
            TT("dve", E["s8c"][:r, :], bk[:r, 0:8], E["s8b"][:r, :], ALU.subtract, [bk, E["s8b"]], [E["s8c"]])
            CP("dve", F3[:r, s, :, 1], E["s8c"][:r, :], [E["s8c"]], [F3])
            CP("dve", E["s8b"][:r, :], F3[:r, s, :, 1], [F3], [E["s8b"]])
            TT("dve", E["s8c"][:r, :], E["s8c"][:r, :], E["s8b"][:r, :], ALU.subtract, [E["s8c"], E["s8b"]], [E["s8c"]])
            CP("dve", F3[:r, s, :, 2], E["s8c"][:r, :], [E["s8c"]], [F3])

    def store_kv(key_base, N):
        nsub = cdiv(N, 128)
        r = min(128, N)
        ktb = key_base // 128
        DMA("sp", KT_scr[:, :, key_base:key_base + N].rearrange("q p t -> p q t"), B["KTt"][:, :, :N], [B["KTt"]], [KT_scr], slow=True)
        for s in range(nsub):
            rr = min(128, N - s * 128)
            DMA("sp", V_scr[:, :rr, ktb + s, :].rearrange("h p d -> p h d"), B["Vt"][:rr, s, :, :], [B["Vt"]], [V_scr], slow=True)

    def ingest_past(PASTN):
        for c in range(PASTN // 512):
            DMA("sp", E["lf"][:, :, :], I["cache_fox_logf"][c * 512:(c + 1) * 512, :].rearrange("(s p) h -> p s h", p=128),
                [], [E["lf"]], slow=True)
            for s in range(4):
                t0 = c * 512 + s * 128
                DMA("sp", E["kn"][:, :], I["cache_fox_k"][t0:t0 + 128, :], [], [E["kn"]])
                to_heads(B["KTt"], E["kn"], s, 128)
                DMA("sp", E["vf"][:, :], I["cache_fox_v"][t0:t0 + 128, :], [], [E["vf"]])
                CP("pool", B["Vt"][:, s, :, 0:64], E["vf"][:, :].rearrange("p (h d) -> p h d", d=64), [E["vf"]], [B["Vt"]])
                cumF(E["lf"][:, s, :], E["lf"], 128, c * 4 + s, s, False)
            store_kv(c * 512, 512)

    def even_mixer(N, S):
        key_base, t0 = S["key_base"], S["t0"]
        nsub = cdiv(N, 128)
        W = I["e_w_in"]
        convert(("even",))
        rmsnorm(N, 1)
        xp, xc, hs = E["xp"], E["xc"], E["hs"]
        CP("pool", xp[:, :, 0:3], xhist[:, :, :], [xhist], [xp])
        for c in range(4):
            wb = wib[rot("wib", NWB)]
            src = mixw[("e_w_in", c)]
            DMA(WQ, wb[:, :, 0, :], src[:, :].rearrange("p (k n) -> p k n", n=128), [src], [wb])
            bk = banks[rot("bk4", 4)]
            for kc in range(8):
                MM(bk[:, :N], wb[:, kc, 0, :], h[:, kc, :N], kc == 0, kc == 7, [wb, h], [bk])
            CP("act", xp[:, c, 3:3 + N], bk[:, :N], [bk], [xp])
        CP("pool", xhist[:, :, :], xp[:, :, N:N + 3], [xp], [xhist])
        sets = [(xc, E["t0"], E["t1"], E["t2"], E["t3"], B["xcb"]),
                (E["qn"], E["kn"], E["vf"], E["fo"][:, 0, :], E["fo"][:, 1, :], B["foT"][:, 0, :])]
        setbufs = [(xc, E["t0"], E["t1"], E["t2"], E["t3"], B["xcb"]),
                   (E["qn"], E["kn"], E["vf"], E["fo"], E["fo"], B["foT"])]
        for c in range(4):
            xc_, r_, i_, a_, q_, xb_ = sets[c % 2]
            bxc, br_, bi_, ba_, bq_, bxb = setbufs[c % 2]
            TS("dve", xc_[:, :N], xp[:, c, 0:N], convw[:, c, 0:1], convb[:, c:c + 1], ALU.mult, ALU.add, [xp, convw, convb], [bxc])
            for j in range(1, 4):
                STT(xc_[:, :N], xp[:, c, j:j + N], convw[:, c, j:j + 1], xc_[:, :N], ALU.mult, ALU.add, [xp, convw, bxc], [bxc])
            CP("act", xb_[:, :N], xc_[:, :N], [bxc], [bxb])
            b1, b2 = banks[rot("bk4", 4)], banks[rot("bk4", 4)]
            MM(b1[:, :N], bda[:, c, :], xb_[:, :N], True, True, [bda, bxb], [b1])
            MM(b2[:, :N], bdx[:, c, :], xb_[:, :N], True, True, [bdx, bxb], [b2])
            ACT(r_[:, :N], b1[:, :N], AF.Sigmoid, [b1, lba], [br_], bias=lba[:, c:c + 1])
            ACT(i_[:, :N], b2[:, :N], AF.Sigmoid, [b2, lbx], [bi_], bias=lbx[:, c:c + 1])
            ACT(a_[:, :N], r_[:, :N], AF.Exp, [br_, c1], [ba_], scale=c1[:, c:c + 1])
            if bq_ is E["fo"]:
                ACT(q_[:, :N], r_[:, :N], AF.Exp, [br_, c2], [], scale=c2[:, c:c + 1]) if False else \
                    P.op("act", (lambda e, o_=q_[:, :N], in__=r_[:, :N], sc=c2[:, c:c + 1]: e.activation(out=o_, in_=in__, func=AF.Exp, scale=sc)),
                         [br_, c2], [bq_])
            else:
                ACT(q_[:, :N], r_[:, :N], AF.Exp, [br_, c2], [bq_], scale=c2[:, c:c + 1])
            ACT(q_[:, :N], q_[:, :N], AF.Sqrt, [bq_], [bq_], bias=1.0, scale=-1.0)
            TT("dve", i_[:, :N], i_[:, :N], xc_[:, :N], ALU.mult, [bi_, bxc], [bi_])
            TT("dve", i_[:, :N], i_[:, :N], q_[:, :N], ALU.mult, [bi_, bq_], [bi_])
            P.op("dve", (lambda e, c=c, a__=a_[:, :N], i__=i_[:, :N]: e.tensor_tensor_scan(
                out=hs[:, c, :N], data0=a__, data1=i__, initial=hcar[:, c:c + 1], op0=ALU.mult, op1=ALU.add)),
                 [ba_, bi_, hcar], [hs])
            CP("dve", hcar[:, c:c + 1], hs[:, c, N - 1:N], [hs], [hcar])
        if stage < 1:
            return
        for c in range(4):
            wb = wib[rot("wib", NWB)]
            src = mixw[("e_w_in", 4 + c)]
            DMA(WQ, wb[:, :, 0, :], src[:, :].rearrange("p (k n) -> p k n", n=128), [src], [wb])
            bk = banks[rot("bk4", 4)]
            for kc in range(8):
                MM(bk[:, :N], wb[:, kc, 0, :], h[:, kc, :N], kc == 0, kc == 7, [wb, h], [bk])
            ga, gb = (E["t0"], E["t4"]) if c % 2 == 0 else (E["t1"], E["t2"])
            ACT(ga[:, :N], bk[:, :N], AF.Square, [bk], [ga])
            TS("dve", ga[:, :N], ga[:, :N], 0.044715, 1.0, ALU.mult, ALU.add, [ga], [ga])
            TT("dve", ga[:, :N], ga[:, :N], bk[:, :N], ALU.mult, [ga, bk], [ga])
            ACT(ga[:, :N], ga[:, :N], AF.Sigmoid, [ga], [ga], scale=1.5957691216057308)
            TT("dve", gb[:, :N], bk[:, :N], hs[:, c, :N], ALU.mult, [bk, hs], [gb])
            TT("dve", B["rnn"][:, c, :N], gb[:, :N], ga[:, :N], ALU.mult, [gb, ga], [B["rnn"]])
        if stage < 2:
            return
        for gi, c0 in enumerate((1024, 1536, 2048, 2560)):
            wt = wtm[rot("wtm", 2)]
            src = mixw[("e_w_in_t", c0)]
            DMA(WQ, wt[:, :, :], src[:, :].rearrange("p (k n) -> p k n", n=512), [src], [wt])
            for s in range(nsub):
                r = min(128, N - s * 128)
                bk = banks[4 + rot("bk4b", 4)]
                for kc in range(8):
                    MM(bk[:r, :], h[:, kc, s * 128:s * 128 + r], wt[:, kc, :], kc == 0, kc == 7, [h, wt], [bk])
                if gi > sub:
                    continue
                if gi == 0:
                    norm_heads(E["qn"], bk, gq, r)
                    to_heads(B["QT"], E["qn"], s, r)
                elif gi == 1:
                    norm_heads(E["kn"], bk, gk, r)
                    DMA("sp", S["fox_k"][t0 + s * 128:t0 + s * 128 + r, :], E["kn"][:r, :], [E["kn"]], [])
                    to_heads(B["KTt"], E["kn"], s, r, 1)
                elif gi == 2:
                    if vmode in (0, 1):
                        CP("act", E["vf"][:r, :], bk[:r, :], [bk], [E["vf"]])
                        DMA("sp", S["fox_v"][t0 + s * 128:t0 + s * 128 + r, :], E["vf"][:r, :], [E["vf"]], [])
                    if vmode in (0, 2):
                        CP("dve", B["Vt"][:r, s, :, 0:64], bk[:r, :].rearrange("p (h d) -> p h d", d=64), [bk], [B["Vt"]])
                else:
                    ACT(E["ogs"][:r, s, :], bk[:r, :], AF.Sigmoid, [bk], [E["ogs"]])
        if stage < 3:
            return
        for s in range(nsub):
            r = min(128, N - s * 128)
            bk = banks[rot("bk4", 4)]
            for kc in range(8):
                MM(bk[:r, 0:8], h[:, kc, s * 128:s * 128 + r], wfl[:, kc, :], kc == 0, kc == 7, [h, wfl], [bk])
            TT("dve", E["s8b"][:r, :], bk[:r, 0:8], fbias[:r, :], ALU.add, [bk, fbias], [E["s8b"]])
            ACT(E["s8b"][:r, :], E["s8b"][:r, :], AF.Exp, [E["s8b"]], [E["s8b"]], scale=-1.0)
            ACT(E["s8b"][:r, :], E["s8b"][:r, :], AF.Ln, [E["s8b"]], [E["s8b"]], bias=1.0)
            TS("dve", E["lf"][:r, s, :], E["s8b"][:r, :], -1.0, None, ALU.mult, None, [E["s8b"]], [E["lf"]])
            DMA("sp", S["fox_lf"][t0 + s * 128:t0 + s * 128 + r, :], E["lf"][:r, s, :], [E["lf"]], [])
            cumF(E["lf"][:r, s, :], E["lf"], r, key_base // 128 + s, s, True)
        for hh in range(8):
            bk = banks[rot("bk4", 4)]
            for s in range(nsub):
                r = min(128, N - s * 128)
                MM(bk[0:3, s * 128:s * 128 + r], B["F3"][:r, s, hh, :], identb[:r, :r], True, True, [B["F3"], identb], [bk])
            CP("act" if hh % 2 == 0 else "dve", B["FQ"][0:3, hh, :N], bk[0:3, :N], [bk], [], ) if False else \
                P.op("act" if hh % 2 == 0 else "dve",
                     (lambda e, o_=B["FQ"][0:3, hh, :N], i_=bk[0:3, :N], en=("act" if hh % 2 == 0 else "dve"):
                      (e.copy(out=o_, in_=i_) if en == "act" else e.tensor_copy(out=o_, in_=i_))), [bk], [], pwrites=[B["FQ"]])
        fqv = B["FQ"][0:3, :, :N].rearrange("p (h two) n -> p h two n", two=2)
        qv = B["QT"][:, :, :N].rearrange("p (h two) n -> p h two n", two=2)
        DMA("sp", qv[64:67, :, 0, :], fqv[:, :, 0, :], [B["FQ"]], [], PW=[B["QT"]])
        DMA("sp", qv[0:3, :, 1, :], fqv[:, :, 1, :], [B["FQ"]], [], PW=[B["QT"]])
        if stage < 4:
            return
        store_kv(key_base, N)
        n_keys = key_base + N
        nch = cdiv(n_keys, 512)
        for p in range(4):
            Ob = [banks[4 + rot("bkO", 4)], banks[4 + rot("bkO", 4)]]
            tiles = []
            loads = {}
            for c in range(nch):
                vk = min(512, n_keys - c * 512)
                diag = (c == nch - 1)
                for hh in range(2):
                    hd = 2 * p + hh
                    kb = ktl[(c % 2) * 2 + hh]
                    vb = vl[(c % 2) * 2 + hh]
                    loads[(c, hh)] = (kb, vb, vk, hd)
                    for kk in range(cdiv(vk, 128)):
                        rk = min(128, vk - kk * 128)
                        tiles.append((c, hh, hd, kk, rk, diag, kb, vb))

            def issue_loads(c):
                for hh in range(2):
                    kb, vb, vk, hd = loads[(c, hh)]
                    nkt = cdiv(vk, 128)
                    rr = min(128, vk)
                    DMA("sp", kb[:, :vk], KT_scr[hd, :, c * 512:c * 512 + vk], [KT_scr], [kb])
                    DMA("sp", vb[:rr, :nkt, :], V_scr[hd, :rr, c * 4:c * 4 + nkt, :], [V_scr], [vb])

            nt_ = len(tiles)
            first_seen = [True, True]
            last_ti = [max(i for i, t_ in enumerate(tiles) if t_[1] == hh_) for hh_ in range(2)]
            loaded = set()

            def need(c):
                if c < nch and c not in loaded:
                    loaded.add(c)
                    issue_loads(c)

            def emit_S(ti):
                c, hh, hd, kk, rk, diag, kb, vb = tiles[ti]
                need(c)
                sb_ = banks[rot("bkS", 4)]
                MM(sb_[:rk, :N], kb[:, kk * 128:kk * 128 + rk], B["QT"][:, hd, :N], True, not diag, [kb, B["QT"]], [sb_])
                if diag:
                    MM(sb_[:rk, :N], identb[:rk, :rk], maskb[:rk, kk, :N], False, True, [identb, maskb], [sb_])
                return sb_

            need(0)
            pend = emit_S(0) if nt_ else None
            for ti in range(nt_):
                c, hh, hd, kk, rk, diag, kb, vb = tiles[ti]
                if hh == 0 and kk == 0:
                    need(c + 1)
                sb_ = pend
                if ti + 1 < nt_:
                    pend = emit_S(ti + 1)
                pt = PT[rot("PT", 3)]
                ACT(pt[:rk, :N], sb_[:rk, :N], AF.Exp, [sb_, negF], [pt], bias=negF[:rk, c * 4 + kk, hd:hd + 1])
                qlist = [qs for qs in range(nsub) if not (diag and qs < kk)]
                for qs in qlist:
                    rq = min(128, N - qs * 128)
                    last = (ti == last_ti[hh]) and (qs == qlist[-1])
                    MM(Ob[hh][:rq, qs * 65:(qs + 1) * 65], pt[:rk, qs * 128:qs * 128 + rq], vb[:rk, kk, 0:65],
                       first_seen[hh], last, [pt, vb], [Ob[hh]])
                    first_seen[hh] = False
            for hh in range(2):
                hd = 2 * p + hh
                for qs in range(nsub):
                    rq = min(128, N - qs * 128)
                    RECIP(E["rec"][:rq, qs:qs + 1], Ob[hh][:rq, qs * 65 + 64:qs * 65 + 65], [Ob[hh]], [E["rec"]])
                    STT(E["fo"][:rq, qs, hd * 64:(hd + 1) * 64], Ob[hh][:rq, qs * 65:qs * 65 + 64], E["rec"][:rq, qs:qs + 1],
                        E["ogs"][:rq, qs, hd * 64:(hd + 1) * 64], ALU.mult, ALU.mult, [Ob[hh], E["rec"], E["ogs"]], [E["fo"]])
        if stage < 5:
            return
        for s in range(nsub):
            r = min(128, N - s * 128)
            for c in range(4):
                bk = banks[rot("bk4", 4)]
                TR(bk[:, :r], E["fo"][:r, s, c * 128:(c + 1) * 128], ident[:r, :r], [E["fo"], ident], [bk])
                CP("act" if c % 2 == 0 else "dve", B["foT"][:, c, s * 128:s * 128 + r], bk[:, :r], [bk], [B["foT"]])
        if stage < 6:
            return
        Wo = I["e_w_out"]
        for half in range(2):
            wt = wtm[rot("wtm", 2)]
            src = mixw[("e_w_out_t", half * 512)]
            DMA(WQ, wt[:, :, :], src[:, :].rearrange("p (k n) -> p k n", n=512), [src], [wt])
            for m in range(4):
                bk = banks[rot("bk4", 4)]
                for kc in range(8):
                    rhs = B["rnn"][:, kc, :N] if kc < 4 else B["foT"][:, kc - 4, :N]
                    MM(bk[:, :N], wt[:, kc, m * 128:(m + 1) * 128], rhs, kc == 0, kc == 7, [wt, B["rnn"], B["foT"]], [bk])
                cc = half * 4 + m
                TT("dve", x[:, cc, :N], x[:, cc, :N], bk[:, :N], ALU.add, [x, bk], [x])

    def even_finish(S):
        for j in range(3):
            DMA("sp", S["lru_conv"][j, :].rearrange("(c p) -> p c", p=128), xhist[:, :, j], [xhist], [], slow=True)
        DMA("sp", S["lru_h"][:].rearrange("(c p) -> p c", p=128), hcar[:, :], [hcar], [], slow=True)

    def even_init(S, sample):
        if sample:
            for j in range(3):
                DMA("sp", xhist[:, :, j], I["state_lru_conv"][j, :].rearrange("(c p) -> p c", p=128), [], [xhist], slow=True)
            DMA("sp", hcar[:, :], I["state_lru_h"][:].rearrange("(c p) -> p c", p=128), [], [hcar], slow=True)
        else:
            MEMSET("dve", xhist[:, :, :], 0.0, [xhist])
            MEMSET("dve", hcar[:, :], 0.0, [hcar])
        MEMSET("dve", Rsum[:, :], 0.0, [Rsum])
        MEMSET("pool", B["Vt"][:, :, :, :], 1.0, [B["Vt"]])

    Fo = od_f
    Bo = od_bf
    Wd = I["o_w_in"]
    TMP = [Fo["T%d" % i] for i in range(16)]

    def SCAN(o_, d0, d1, R, W):
        P.op("dve", lambda e: e.tensor_tensor_scan(out=o_, data0=d0, data1=d1, initial=0.0, op0=ALU.mult, op1=ALU.add), R, W)

    def odd_mixer(N, S):
        C = min(64, N)
        nch = N // C
        W2 = 2 * C
        MK = m64 if C == 64 else m16
        su, sl, ui, nsu = MK[:W2, 0, :], MK[:W2, 1, :], MK[:W2, 2, :], MK[:W2, 3, :]
        T = TMP
        convert(("odd",))
        rmsnorm(N, 4)
        for nm in ("bdA", "bdB", "bdK", "bdR"):
            MEMSET("pool", Bo[nm][:, :], 0.0, [Bo[nm]])
        for nm in ("fV", "fB", "fK", "fA"):
            MEMSET("pool", Fo[nm][:, :], 0.0, [Fo[nm]])

        def proj_fm(col0, bank):
            wb = wib[rot("wib", NWB)]
            DMA("pool", wb[:, :, 0, :], Wd[:, col0:col0 + 128].rearrange("(k p) n -> p k n", p=128), [], [wb])
            for kc in range(8):
                MM(bank[:, :N], wb[:, kc, 0, :], h[:, kc, :N], kc == 0, kc == 7, [wb, h], [bank])

        def v3(buf, pr=slice(0, 128)):
            return buf[pr, 0:N].rearrange("p (j t) -> p j t", t=C)

        def lastcol(buf, pr=slice(0, 128)):
            return v3(buf, pr)[:, :, C - 1:C].to_broadcast([pr.stop - pr.start, nch, C])

        def nb():
            return banks[rot("bkR", 8)]

        for c in range(4):
            bq, bf_, bi = nb(), nb(), nb()
            proj_fm(c * 128, bq)
            proj_fm(512 + c * 128, bf_)
            proj_fm(1024 + c * 128, bi)
            bg = nb()
            proj_fm(1536 + c * 128, bg)
            ACT(T[0][:, :N], bf_[:, :N], AF.Sigmoid, [bf_], [T[0]])
            TS("dve", T[0][:, :N], T[0][:, :N], oml[:, c:c + 1], lb[:, c:c + 1], ALU.mult, ALU.add, [T[0], oml, lb], [T[0]])
            TS("dve", T[1][:, :N], T[0][:, :N], -1.0, 1.0, ALU.mult, ALU.add, [T[0]], [T[1]])
            ACT(T[2][:, :N], T[0][:, :N], AF.Ln, [T[0]], [T[2]])
            for j in range(nch):
                SCAN(T[3][:, j * C:(j + 1) * C], ones_row[:, :C], T[2][:, j * C:(j + 1) * C], [ones_row, T[2]], [T[3]])
            ACT(T[4][:, :N], T[3][:, :N], AF.Exp, [T[3]], [T[4]])
            ACT(T[5][:, :N], T[3][:, :N], AF.Exp, [T[3]], [T[5]], scale=-1.0)
            TT("dve", Bo["b0"][:, :N], bq[:, :N], T[4][:, :N], ALU.mult, [bq, T[4]], [Bo["b0"]])
            TT("dve", T[1][:, :N], T[1][:, :N], T[5][:, :N], ALU.mult, [T[1], T[5]], [T[1]])
            CP("act", Bo["b1"][:, :N], T[1][:, :N], [T[1]], [Bo["b1"]])
            TT("dve", v3(T[6]), v3(T[1]), lastcol(T[4]), ALU.mult, [T[1], T[4]], [T[6]])
            CP("act", T[7][:, :N], bi[:, :N], [bi], [T[7]])
            ACT(T[10][:, :N], bg[:, :N], AF.Silu, [bg], [T[10]])
            HB = [Bo["q%d" % i] for i in range(32)]
            for j in range(nch):
                cs = slice(j * C, (j + 1) * C)
                vtm, kdtm = HB[j], HB[8 + j]
                t0_, t1_ = nb(), nb()
                TR(t0_[:C, :128], T[7][:, cs], ident[:, :], [T[7], ident], [t0_])
                CP("act", vtm[:C, :], t0_[:C, :128], [t0_], [vtm])
                TR(t1_[:C, :128], T[6][:, cs], ident[:, :], [T[6], ident], [t1_])
                CP("dve", kdtm[:C, :], t1_[:C, :128], [t1_], [kdtm])
            for j in range(nch):
                cs = slice(j * C, (j + 1) * C)
                Am = HB[16 + j]
                t2_ = nb()
                MM(t2_[:C, :C], Bo["b1"][:, cs], Bo["b0"][:, cs], True, True, [Bo["b1"], Bo["b0"]], [t2_])
                TT("dve", Am[:C, :C], t2_[:C, :C], ui[:C, :C], ALU.mult, [t2_, MK], [Am])
            Sb = [None] * (nch + 1)
            for j in range(nch):
                vtm, kdtm = HB[j], HB[8 + j]
                t4_ = nb()
                MM(t4_[:, :128], kdtm[:C, :], vtm[:C, :], True, True, [kdtm, vtm], [t4_])
                if j == 0:
                    CP("act", HB[24][:, :], S_hgb[:, c, :], [S_hgb], [HB[24]])
                STT(S_hg[:, c, :], S_hg[:, c, :], T[4][:, (j + 1) * C - 1:(j + 1) * C], t4_[:, :128], ALU.mult, ALU.add,
                    [S_hg, T[4], t4_], [S_hg])
                if j < nch - 1:
                    CP("act", HB[24 + j + 1][:, :], S_hg[:, c, :], [S_hg], [HB[24 + j + 1]])
                else:
                    CP("act", S_hgb[:, c, :], S_hg[:, c, :], [S_hg], [S_hgb])
            for j in range(nch):
                cs = slice(j * C, (j + 1) * C)
                vtm, Am, Sbj = HB[j], HB[16 + j], HB[24 + j]
                t3_ = nb()
                MM(t3_[:, :C], vtm[:C, :], Am[:C, :C], True, False, [vtm, Am], [t3_])
                MM(t3_[:, :C], Sbj[:, :], Bo["b0"][:, cs], False, True, [Sbj, Bo["b0"]], [t3_])
                CP("act" if j % 2 == 0 else "dve", T[8][:, cs], t3_[:, :C], [t3_], [T[8]])
            ACT(T[9][:, :N], T[8][:, :N], AF.Square, [T[8]], [T[9]])
            t5_ = nb()
            MM(t5_[:, :N], ones1[:, :], T[9][:, :N], True, True, [ones1, T[9]], [t5_])
            ACT(T[9][:, :N], t5_[:, :N], AF.Ln, [t5_], [T[9]], scale=1.0 / 128, bias=EPS)
            ACT(T[9][:, :N], T[9][:, :N], AF.Exp, [T[9]], [T[9]], scale=-0.5)
            STT(T[8][:, :N], T[8][:, :N], hgn[:, c:c + 1], T[9][:, :N], ALU.mult, ALU.mult, [T[8], hgn, T[9]], [T[8]])
            TT("dve", Bo["hgT"][:, c, :N], T[8][:, :N], T[10][:, :N], ALU.mult, [T[8], T[10]], [Bo["hgT"]])

        def zmix(ci, dst):
            bk = nb()
            proj_fm(2048 + ci * 128, bk)
            zx = T[15]
            CP("act", zx[:, 1:N + 1], bk[:, :N], [bk], [zx])
            CP("pool", zx[:, 0:1], zprev[:, ci:ci + 1], [zprev], [zx])
            CP("pool", zprev[:, ci:ci + 1], zx[:, N:N + 1], [zx], [zprev])
            TS("dve", dst[:, :N], zx[:, 0:N], mu[:, ci:ci + 1], None, ALU.mult, None, [zx, mu], [dst])
            STT(dst[:, :N], zx[:, 1:N + 1], omu[:, ci:ci + 1], dst[:, :N], ALU.mult, ALU.add, [zx, omu, dst], [dst])

        zmix(12, T[0])
        ACT(Bo["b2"][0:64, :N], T[0][0:64, :N], AF.Tanh, [T[0]], [Bo["b2"]])
        CP("pool", Bo["b2"][64:128, :N], T[0][64:128, :N], [T[0]], [Bo["b2"]])
        zmix(13, T[0])
        ACT(Bo["b3"][:, :N], T[0][:, :N], AF.Sigmoid, [T[0]], [Bo["b3"]])
        nsteps = {64: 5, 16: 3}[C]
        def bdv(buf, hh):
            return buf[hh * 64:(hh + 1) * 64, 0:nch * W2].rearrange("p (j b t) -> p j b t", b=2, t=C)[:, :, hh, :]

        def prep_early(c):
            cc = slice(c * 128, (c + 1) * 128)
            specs = ((c, T[0], T[15]), (4 + c, T[1], T[13]), (8 + c, T[2], T[12]))
            pbanks = []
            for ci, dst, zx in specs:
                bk_ = nb()
                proj_fm(2048 + ci * 128, bk_)
                pbanks.append(bk_)
            tw, ta = nb(), nb()
            MM(tw[:, :N], w2t[0:64, cc], Bo["b2"][0:64, :N], True, True, [w2t, Bo["b2"]], [tw])
            MM(ta[:, :N], a2t[64:128, cc], Bo["b2"][64:128, :N], True, True, [a2t, Bo["b2"]], [ta])
            for (ci, dst, zx), bk_ in zip(specs, pbanks):
                CP("act", zx[:, 1:N + 1], bk_[:, :N], [bk_], [zx])
                CP("pool", zx[:, 0:1], zprev[:, ci:ci + 1], [zprev], [], ) if False else \
                    P.op("pool", (lambda e, o_=zx[:, 0:1], i_=zprev[:, ci:ci + 1]: e.tensor_copy(out=o_, in_=i_)), [zprev], [], pwrites=[zx])
                CP("pool", zprev[:, ci:ci + 1], zx[:, N:N + 1], [zx], [zprev])
            ACT(T[3][:, :N], tw[:, :N], AF.Exp, [tw, nw0], [T[3]], scale=-1.0, bias=nw0[:, c:c + 1])
            ACT(T[3][:, :N], T[3][:, :N], AF.Ln, [T[3]], [T[3]], bias=1.0)
            ACT(T[3][:, :N], T[3][:, :N], AF.Exp, [T[3]], [T[3]], scale=-1.0, bias=-0.5)
            ACT(T[4][:, :N], ta[:, :N], AF.Sigmoid, [ta, a0], [T[4]], bias=a0[:, c:c + 1])
            for ci, dst, zx in specs:
                TS("dve", dst[:, :N], zx[:, 0:N], mu[:, ci:ci + 1], None, ALU.mult, None, [zx, mu], [dst])
                STT(dst[:, :N], zx[:, 1:N + 1], omu[:, ci:ci + 1], dst[:, :N], ALU.mult, ALU.add, [zx, omu, dst], [dst])
            yield
            TS("dve", T[6][:, :N], T[1][:, :N], kkw[:, c:c + 1], None, ALU.mult, None, [T[1], kkw], [T[6]])
            TT("dve", T[10][:, :N], T[6][:, :N], T[6][:, :N], ALU.mult, [T[6]], [T[10]])
            t_ = nb()
            MM(t_[:, :N], blk[:, :], T[10][:, :N], True, True, [blk, T[10]], [t_])
            TS("dve", T[10][:, :N], t_[:, :N], 1e-24, None, ALU.max, None, [t_], [T[10]])
            yield
            ACT(T[10][:, :N], T[10][:, :N], AF.Ln, [T[10]], [T[10]])
            ACT(T[10][:, :N], T[10][:, :N], AF.Exp, [T[10]], [T[10]], scale=-0.5)
            TT("dve", T[6][:, :N], T[6][:, :N], T[10][:, :N], ALU.mult, [T[6], T[10]], [T[6]])
            yield
            TS("dve", T[10][:, :N], T[4][:, :N], kaw[:, c:c + 1], omka[:, c:c + 1], ALU.mult, ALU.add, [T[4], kaw, omka], [T[10]])
            TT("dve", T[1][:, :N], T[1][:, :N], T[10][:, :N], ALU.mult, [T[1], T[10]], [T[1]])
            TT("dve", T[8][:, :N], T[6][:, :N], T[4][:, :N], ALU.mult, [T[6], T[4]], [T[8]])
            yield
            for j in range(nch):
                SCAN(T[10][:, j * C:(j + 1) * C], ones_row[:, :C], T[3][:, j * C:(j + 1) * C], [ones_row, T[3]], [T[10]])
            yield
            ACT(T[12][:, :N], T[10][:, :N], AF.Exp, [T[10]], [T[12]])
            TT("dve", T[13][:, :N], T[3][:, :N], T[10][:, :N], ALU.subtract, [T[3], T[10]], [T[13]])
            ACT(T[13][:, :N], T[13][:, :N], AF.Exp, [T[13]], [T[13]])
            yield

        def prep_late(c):
            cc = slice(c * 128, (c + 1) * 128)
            t_ = nb()
            MM(t_[:, :N], g2t[:, cc], Bo["b3"][:, :N], True, True, [g2t, Bo["b3"]], [t_])
            CP("act", T[5][:, :N], t_[:, :N], [t_], [T[5]])
            STT(T[9][:, :N], T[0][:, :N], rkw[:, c:c + 1], T[1][:, :N], ALU.mult, ALU.mult, [T[0], rkw, T[1]], [T[9]])
            t_ = nb()
            MM(t_[:, :N], blk[:, :], T[9][:, :N], True, True, [blk, T[9]], [t_])
            TT("dve", T[9][:, :N], t_[:, :N], T[2][:, :N], ALU.mult, [t_, T[2]], [T[9]])
            ACT(T[11][:, :N], T[10][:, :N], AF.Exp, [T[10]], [T[11]], scale=-1.0)
            TT("dve", T[14][:, :N], T[1][:, :N], T[12][:, :N], ALU.mult, [T[1], T[12]], [T[14]])
            TT("dve", T[4][:, :N], T[8][:, :N], T[12][:, :N], ALU.mult, [T[8], T[12]], [T[4]])
            for hh in range(2):
                pr = slice(hh * 64, (hh + 1) * 64)
                eng = "dve" if hh == 0 else "pool"
                TT("dve", bdv(Bo["bdA"], hh), v3(T[13], pr), v3(T[6], pr), ALU.mult, [T[13], T[6]], [Bo["bdA"]])
                ceng = "dve" if hh == 0 else "act"
                CP(ceng, bdv(Bo["bdB"], hh), v3(T[4], pr), [T[4]], [Bo["bdB"]])
                CP(ceng, bdv(Bo["bdK"], hh), v3(T[14], pr), [T[14]], [Bo["bdK"]])
                TT("dve", bdv(Bo["bdR"], hh), v3(T[0], pr), v3(T[11], pr), ALU.mult, [T[0], T[11]], [Bo["bdR"]])
                CP(ceng, bdv(Fo["fV"], hh), v3(T[2], pr), [T[2]], [Fo["fV"]])
                TT(eng, bdv(Fo["fA"], hh), v3(T[13], pr), v3(T[6], pr), ALU.mult, [T[13], T[6]], [Fo["fA"]])
                TT(eng, bdv(Fo["fK"], hh), v3(T[14], pr), lastcol(T[11], pr), ALU.mult, [T[14], T[11]], [Fo["fK"]])
                if hh == 0:
                    STT(bdv(Fo["fB"], hh), v3(T[4], pr), -1.0, lastcol(T[11], pr), ALU.mult, ALU.mult, [T[4], T[11]], [Fo["fB"]])
                else:
                    TT("pool", bdv(Fo["fB"], hh), v3(T[4], pr), lastcol(T[11], pr), ALU.mult, [T[4], T[11]], [Fo["fB"]])
                    TS("pool", bdv(Fo["fB"], hh), bdv(Fo["fB"], hh), -1.0, None, ALU.mult, None, [Fo["fB"]], [Fo["fB"]])

        for c in range(4):
            for _ in prep_early(c):
                pass
            prep_late(c)
            nxt = None
            yT = T[7]

            def phaseA(j, M):
                ws = slice(j * W2, (j + 1) * W2)
                A_, B_, K_, R_ = Bo["bdA"][:, ws], Bo["bdB"][:, ws], Bo["bdK"][:, ws], Bo["bdR"][:, ws]
                (N0, Nt0, Na, Nta, P0, Pa, LakT, nMrbT, MrkT, Vbd, nBd, Kd, Atm, Wtm, LV, U0, Rhat, PhiT) = M
                p = nb()
                MM(p[:W2, :W2], B_, A_, True, True, [Bo["bdB"], Bo["bdA"]], [p])
                CP("act", N0[:W2, :W2], p[:W2, :W2], [p], [N0])
                TT("pool", N0[:W2, :W2], N0[:W2, :W2], nsu, ALU.mult, [N0, MK], [N0])
                yield
                p = nb()
                MM(p[:W2, :W2], A_, B_, True, True, [Bo["bdA"], Bo["bdB"]], [p])
                STT(Nt0[:W2, :W2], p[:W2, :W2], -1.0, sl, ALU.mult, ALU.mult, [p, MK], [Nt0])
                TT("pool", P0[:W2, :W2], N0[:W2, :W2], ident[:W2, :W2], ALU.add, [N0, ident], [P0])
                yield
                p = nb()
                MM(p[:W2, :W2], K_, A_, True, True, [Bo["bdK"], Bo["bdA"]], [p])
                CP("act", LakT[:W2, :W2], p[:W2, :W2], [p], [LakT])
                TT("pool", LakT[:W2, :W2], LakT[:W2, :W2], su, ALU.mult, [LakT, MK], [LakT])
                yield
                p = nb()
                MM(p[:W2, :W2], B_, R_, True, True, [Bo["bdB"], Bo["bdR"]], [p])
                STT(nMrbT[:W2, :W2], p[:W2, :W2], -1.0, ui, ALU.mult, ALU.mult, [p, MK], [nMrbT])
                yield
                p = nb()
                MM(p[:W2, :W2], K_, R_, True, True, [Bo["bdK"], Bo["bdR"]], [p])
                CP("act", MrkT[:W2, :W2], p[:W2, :W2], [p], [MrkT])
                TT("pool", MrkT[:W2, :W2], MrkT[:W2, :W2], ui, ALU.mult, [MrkT, MK], [MrkT])
                yield
                for (src, dstm, eng) in ((Fo["fV"], Vbd, "act"), (Fo["fB"], nBd, "dve"), (Fo["fK"], Kd, "act"), (Fo["fA"], Atm, "dve")):
                    p = nb()
                    TR(p[:W2, :128], src[:, ws], ident[:, :], [src, ident], [p])
                    CP(eng, dstm[:W2, :], p[:W2, :128], [p], [dstm])
                    yield
                p = nb()
                MM(p[:W2, :128], LakT[:W2, :W2], Vbd[:W2, :], True, True, [LakT, Vbd], [p])
                CP("act", LV[:W2, :], p[:W2, :128], [p], [LV])
                yield
                Nc, Ntc, Pc = N0, Nt0, P0
                oth = {id(N0): Na, id(Na): N0, id(Nt0): Nta, id(Nta): Nt0, id(P0): Pa, id(Pa): P0}
                for i in range(nsteps):
                    nN, nNt, nP = oth[id(Nc)], oth[id(Ntc)], oth[id(Pc)]
                    q1 = nb()
                    MM(q1[:W2, :W2], Nc[:W2, :W2], Ntc[:W2, :W2], True, True, [Nc, Ntc], [q1])
                    CP("act" if i % 2 == 1 else "dve", nNt[:W2, :W2], q1[:W2, :W2], [q1], [nNt])
                    if i < nsteps - 1:
                        q0 = nb()
                        MM(q0[:W2, :W2], Ntc[:W2, :W2], Nc[:W2, :W2], True, True, [Ntc, Nc], [q0])
                        CP("dve", nN[:W2, :W2], q0[:W2, :W2], [q0], [nN])
                    yield
                    q2 = nb()
                    MM(q2[:W2, :W2], nNt[:W2, :W2], Pc[:W2, :W2], True, False, [nNt, Pc], [q2])
                    MM(q2[:W2, :W2], identb[:W2, :W2], Pc[:W2, :W2], False, True, [identb, Pc], [q2])
                    CP("act" if i % 2 == 0 else "dve", nP[:W2, :W2], q2[:W2, :W2], [q2], [nP])
                    yield
                    Nc, Ntc, Pc = nN, nNt, nP
                Tt = Pc
                p = nb()
                MM(p[:W2, :128], Tt[:W2, :W2], Atm[:W2, :], True, True, [Tt, Atm], [p])
                CP("act", Wtm[:W2, :], p[:W2, :128], [p], [Wtm])
                p = nb()
                MM(p[:W2, :128], Tt[:W2, :W2], LV[:W2, :], True, True, [Tt, LV], [p])
                CP("dve", U0[:W2, :], p[:W2, :128], [p], [U0])
                yield
                p = nb()
                MM(p[:, :W2], Wtm[:W2, :], nMrbT[:W2, :W2], True, False, [Wtm, nMrbT], [p])
                MM(p[:, :W2], identb[:, :], R_, False, True, [identb, Bo["bdR"]], [p])
                CP("dve", Rhat[:, :W2], p[:, :W2], [p], [Rhat])
                p = nb()
                MM(p[:, :128], Wtm[:W2, :], nBd[:W2, :], True, True, [Wtm, nBd], [p])
                CP("act", PhiT[:, :], p[:, :128], [p], [PhiT])
                yield

            def phaseB(j, M):
                (N0, Nt0, Na, Nta, P0, Pa, LakT, nMrbT, MrkT, Vbd, nBd, Kd, Atm, Wtm, LV, U0, Rhat, PhiT) = M
                y0 = nb()
                MM(y0[:, :W2], U0[:W2, :], nMrbT[:W2, :W2], True, False, [U0, nMrbT], [y0])
                MM(y0[:, :W2], Vbd[:W2, :], MrkT[:W2, :W2], False, False, [Vbd, MrkT], [y0])
                MM(y0[:, :W2], STb[:, c, :], Rhat[:, :W2], False, True, [STb, Rhat], [y0])
                s0 = nb()
                MM(s0[:, :128], Kd[:W2, :], Vbd[:W2, :], True, False, [Kd, Vbd], [s0])
                MM(s0[:, :128], nBd[:W2, :], U0[:W2, :], False, False, [nBd, U0], [s0])
                MM(s0[:, :128], PhiT[:, :], STb[:, c, :], False, True, [PhiT, STb], [s0])
                STT(ST[:, c, :], ST[:, c, :], T[11][:, (j + 1) * C - 1:(j + 1) * C], s0[:, :128], ALU.mult, ALU.add,
                    [ST, T[11], s0], [ST])
                CP("act", STb[:, c, :], ST[:, c, :], [ST], [STb])
                CP("act", yT[0:64, j * C:(j + 1) * C], y0[0:64, 0:C], [y0], [yT])
                CP("act", yT[64:128, j * C:(j + 1) * C], y0[64:128, C:W2], [y0], [yT])

            GS = 4
            for g0 in range(0, nch, GS):
                js = list(range(g0, min(nch, g0 + GS)))
                Ms = {j: [Bo["q%d" % (18 * (j - g0) + i)] for i in range(18)] for j in js}
                gens = [phaseA(j, Ms[j]) for j in js]
                while gens:
                    for g_ in list(gens):
                        try:
                            next(g_)
                        except StopIteration:
                            gens.remove(g_)
                    if nxt is not None:
                        try:
                            next(nxt)
                        except StopIteration:
                            nxt = None
                for j in js:
                    phaseB(j, Ms[j])
            if nxt is not None:
                for _ in nxt:
                    pass
            t_ = nb()
            MM(t_[:, :N], blk[:, :], yT[:, :N], True, True, [blk, yT], [t_])
            STT(yT[:, :N], t_[:, :N], -1.0 / 64, yT[:, :N], ALU.mult, ALU.add, [t_, yT], [yT])
            TT("dve", T[14][:, :N], yT[:, :N], yT[:, :N], ALU.mult, [yT], [T[14]])
            t_ = nb()
            MM(t_[:, :N], blk[:, :], T[14][:, :N], True, True, [blk, T[14]], [t_])
            ACT(T[14][:, :N], t_[:, :N], AF.Ln, [t_], [T[14]], scale=1.0 / 64, bias=64e-5)
            ACT(T[14][:, :N], T[14][:, :N], AF.Exp, [T[14]], [T[14]], scale=-0.5)
            TT("dve", yT[:, :N], yT[:, :N], T[14][:, :N], ALU.mult, [yT, T[14]], [yT])
            TS("dve", yT[:, :N], yT[:, :N], lng[:, c:c + 1], lnb[:, c:c + 1], ALU.mult, ALU.add, [yT, lng, lnb], [yT])
            TT("dve", yT[:, :N], yT[:, :N], T[9][:, :N], ALU.add, [yT, T[9]], [yT])
            TT("dve", Bo["rwT"][:, c, :N], yT[:, :N], T[5][:, :N], ALU.mult, [yT, T[5]], [Bo["rwT"]])
        Wo = I["o_w_out"]
        for half in range(2):
            wt = wtm[rot("wtm", 2)]
            DMA("pool", wt[:, :, :], Wo[:, half * 512:(half + 1) * 512].rearrange("(k p) n -> p k n", p=128), [], [wt])
            for m in range(4):
                bk = banks[rot("bk4", 4)]
                for kc in range(8):
                    rhs = Bo["hgT"][:, kc, :N] if kc < 4 else Bo["rwT"][:, kc - 4, :N]
                    MM(bk[:, :N], wt[:, kc, m * 128:(m + 1) * 128], rhs, kc == 0, kc == 7, [wt, Bo["hgT"], Bo["rwT"]], [bk])
                cc_ = half * 4 + m
                TT("dve", x[:, cc_, :N], x[:, cc_, :N], bk[:, :N], ALU.add, [x, bk], [x])

    def odd_init(S, sample):
        if sample:
            DMA("sp", S_hg[:, :, :], I["state_hgrn_S"][:, :, :].rearrange("h k v -> k h v"), [], [S_hg])
            MEMSET("dve", ST[:, :, :], 0.0, [ST])
            for hd in range(8):
                pp, hh = hd // 2, hd % 2
                DMA("sp", ST[hh * 64:(hh + 1) * 64, pp, hh * 64:(hh + 1) * 64], I["state_rwkv_S"][hd].rearrange("v k -> k v"),
                    [], [ST], slow=True)
            DMA("sp", zprev[:, :], I["state_rwkv_shift"][:].rearrange("(c p) -> p c", p=128), [], [zprev], slow=True)
        else:
            MEMSET("dve", S_hg[:, :, :], 0.0, [S_hg])
            MEMSET("dve", ST[:, :, :], 0.0, [ST])
            MEMSET("dve", zprev[:, :], 0.0, [zprev])
        CP("pool", S_hgb[:, :, :], S_hg[:, :, :], [S_hg], [S_hgb])
        CP("pool", STb[:, :, :], ST[:, :, :], [ST], [STb])

    def odd_finish(S):
        DMA("sp", S["hgrn_S"][:, :, :].rearrange("h k v -> k h v"), S_hg[:, :, :], [S_hg], [])
        for hd in range(8):
            pp, hh = hd // 2, hd % 2
            DMA("sp", S["rwkv_S"][hd].rearrange("v k -> k v"), ST[hh * 64:(hh + 1) * 64, pp, hh * 64:(hh + 1) * 64],
                [ST], [], slow=True)
        DMA("sp", S["rwkv_shift"][:].rearrange("(c p) -> p c", p=128), zprev[:, :], [zprev], [], slow=True)

    def tile(S, src, dst, t0, N):
        S["t0"] = t0
        load_x(src, t0, N)
        for layer in range(nlayers):
            P.barrier(grp_all, dummy)
            ffn(N, layer, 0)
            P.barrier(grp_all, dummy)
            if layer == 0:
                MEMSET("pool", B["Vt"][:, :, :, 64:66], 1.0, [B["Vt"]])
                init_heads()
                even_mixer(N, S)
            else:
                odd_mixer(N, S)
            P.barrier(grp_all, dummy)
            ffn(N, layer, 1)
        store_x(dst, t0, N)

    S = {"fox_k": O["fox_k_p"], "fox_v": O["fox_v_p"], "fox_lf": O["fox_logf_p"], "lru_conv": O["lru_conv_p"],
         "lru_h": O["lru_h_p"], "key_base": 0, "hgrn_S": O["hgrn_S_p"], "rwkv_S": O["rwkv_S_p"],
         "rwkv_shift": O["rwkv_shift_p"]}
    even_init(S, False)
    odd_init(S, False)
    for t in range(SEQ // NT):
        S["key_base"] = t * NT
        tile(S, I["x_prompt"], O["y_prompt"], t * NT, NT)
    even_finish(S)
    odd_finish(S)
    if do_sample:
        S = {"fox_k": O["fox_k_s"], "fox_v": O["fox_v_s"], "fox_lf": O["fox_logf_s"], "lru_conv": O["lru_conv_s"],
             "lru_h": O["lru_h_s"], "key_base": PAST, "hgrn_S": O["hgrn_S_s"], "rwkv_S": O["rwkv_S_s"],
             "rwkv_shift": O["rwkv_shift_s"]}
        P.barrier(grp_all, dummy)
        even_init(S, True)
        odd_init(S, True)
        init_heads()
        ingest_past(PAST)
        tile(S, I["x_sample"], O["y_sample"], 0, DSEQ)
        even_finish(S)
        odd_finish(S)

    P.finish()
    stack.close()
    return nc


def consts():
    k = np.arange(128)
    q = np.arange(512)
    mask = np.zeros((128, 4, 512), np.float32)
    for kk in range(4):
        mask[:, kk, :] = np.where((kk * 128 + k)[:, None] <= q[None, :], 0.0, -30000.0)
    def bdmask(C, fn):
        m = np.zeros((2 * C, 2 * C), np.float32)
        i = np.arange(C)
        blkm = fn(i[:, None], i[None, :]).astype(np.float32)
        m[:C, :C] = blkm
        m[C:, C:] = blkm
        return m
    def m3(C):
        return np.stack([bdmask(C, lambda j, t: t > j), bdmask(C, lambda t, j: t > j), bdmask(C, lambda j, t: t >= j),
                         -bdmask(C, lambda j, t: t > j)], axis=1)
    blk = np.zeros((128, 128), np.float32)
    blk[:64, :64] = 1.0
    blk[64:, 64:] = 1.0
    return {"c_m64": m3(64), "c_m16": m3(16), "c_blk": blk,
            "c_ones": np.full((128, 128), 1.0 / 1024, np.float32),
            "c_ident": np.eye(128, dtype=np.float32),
            "c_utri": np.triu(np.ones((128, 128), np.float32)),
            "c_mask": mask}


OUT_NAMES = ["y_prompt", "y_sample", "lru_conv_p", "lru_conv_s", "lru_h_p", "lru_h_s", "fox_k_p", "fox_k_s", "fox_v_p",
             "fox_v_s", "fox_logf_p", "fox_logf_s", "hgrn_S_p", "hgrn_S_s", "rwkv_shift_p", "rwkv_shift_s", "rwkv_S_p",
             "rwkv_S_s"]


def percore_inputs(inp, b):
    f = lambda a: np.ascontiguousarray(np.asarray(a), dtype=np.float32)
    m = {"x_prompt": f(inp["x_prompt"][b]), "x_sample": f(inp["x_sample"][b]),
         "state_lru_conv": f(inp["state_lru_conv"][0, b]), "state_lru_h": f(inp["state_lru_h"][0, b]),
         "cache_fox_k": f(inp["cache_fox_k"][0, b]).reshape(-1, G), "cache_fox_v": f(inp["cache_fox_v"][0, b]).reshape(-1, G),
         "cache_fox_logf": f(inp["cache_fox_logf"][0, b]), "state_hgrn_S": f(inp["state_hgrn_S"][0, b]),
         "state_rwkv_shift": f(inp["state_rwkv_shift"][0, b]), "state_rwkv_S": f(inp["state_rwkv_S"][0, b]),
         "norm_g": f(inp["norm_g"]), "ffn_w_in": f(inp["ffn_w_in"]), "ffn_w_out": f(inp["ffn_w_out"]),
         "hg_lb_logits": f(inp["hg_lb_logits"]), "rw_rk": f(inp["rw_rk"][0]).reshape(-1)}
    for k in ("e_w_in", "e_w_out", "lru_conv_w", "lru_conv_b", "lru_wa", "lru_ba", "lru_wx", "lru_bx", "lru_lambda",
              "fox_q_gain", "fox_k_gain", "fox_f_bias", "o_w_in", "o_w_out", "hg_norm_g", "rw_mu", "rw_w0", "rw_w2",
              "rw_a0", "rw_a2", "rw_g2", "rw_kk", "rw_ka", "rw_ln_g", "rw_ln_b"):
        m[k] = f(inp[k][0])
    m.update(consts())
    return m


_NC_CACHE = {}


def kernel(**inputs):
    SEQ = inputs["x_prompt"].shape[1]
    NB = inputs["x_prompt"].shape[0]
    if SEQ not in _NC_CACHE:
        _NC_CACHE[SEQ] = build(SEQ)
    nc = _NC_CACHE[SEQ]
    in_maps = [percore_inputs(inputs, b) for b in range(NB)]
    res = run_bass_kernel_spmd(nc, in_maps, core_ids=list(range(NB))).results
    outs = []
    for nm in OUT_NAMES:
        a = np.stack([np.asarray(r[nm], dtype=np.float32) for r in res], axis=0)
        if nm.startswith("y_"):
            outs.append(a)
        elif nm.startswith("fox_k") or nm.startswith("fox_v"):
            outs.append(a.reshape(1, NB, a.shape[1], 8, 64))
        else:
            outs.append(a.reshape((1, NB) + a.shape[1:]))
    return tuple(outs)
```
